# Optimizing a Trainium2 kernel written in Bass

```python
import jax, jax.numpy as jnp
from jax import lax
import numpy as np

D_MODEL = 1024
BATCH = 4
SEQ = 4096
DEPTH = 1
DEC_BATCH = 16
DEC_SEQ = 32
PAST_LEN = 4096

CHUNK = 64
N_HEADS = 8
HEAD_DIM = 64
D_ATTN = N_HEADS * HEAD_DIM
D_CONV = D_MODEL // 2
CONV_WIDTH = 31
D_FF = ((8 * D_MODEL // 3 + 255) // 256) * 256
Q_BLOCK = 128
RMS_EPS = 1e-6
LN_EPS = 1e-5
NEG = -1e30
D_IN = 2 * D_CONV + 3 * D_ATTN + N_HEADS + 2 * D_MODEL

kernel_name = "hybrid_conformer_fox_stream_step"


def rmsnorm(x, g):
    xf = x.astype(jnp.float32)
    y = xf * lax.rsqrt(jnp.mean(xf * xf, axis=-1, keepdims=True) + RMS_EPS)
    return (y * g.astype(jnp.float32)).astype(x.dtype)


def layernorm(x, g, b):
    xf = x.astype(jnp.float32)
    mu = jnp.mean(xf, axis=-1, keepdims=True)
    var = jnp.mean(jnp.square(xf - mu), axis=-1, keepdims=True)
    y = (xf - mu) * lax.rsqrt(var + LN_EPS)
    return (y * g.astype(jnp.float32) + b.astype(jnp.float32)).astype(x.dtype)


def depthwise_causal(xpad, w_dw, b_dw):
    out = lax.conv_general_dilated(
        xpad, w_dw[:, None, :].astype(xpad.dtype), window_strides=(1,), padding='VALID',
        dimension_numbers=('NWC', 'WIO', 'NWC'), feature_group_count=D_CONV)
    return out + b_dw.astype(out.dtype)


def fox_block(q, k, v, cq, ck, q_pos, k_pos):
    s = jnp.einsum('bqhd,bkhd->bhqk', q.astype(jnp.float32), k.astype(jnp.float32)) * (HEAD_DIM ** -0.5)
    decay = jnp.transpose(cq, (0, 2, 1))[:, :, :, None] - jnp.transpose(ck, (0, 2, 1))[:, :, None, :]
    mask = k_pos[None, :] <= q_pos[:, None]
    s = jnp.where(mask[None, None], s + decay, NEG)
    p = jax.nn.softmax(s, axis=-1)
    return jnp.einsum('bhqk,bkhd->bqhd', p, v.astype(jnp.float32)).astype(v.dtype)


def fox_prompt(q, k, v, logf):
    B, S = q.shape[0], q.shape[1]
    c = jnp.cumsum(logf, axis=1)
    pos = jnp.arange(S)
    nb = S // Q_BLOCK
    qb = jnp.transpose(q.reshape(B, nb, Q_BLOCK, N_HEADS, HEAD_DIM), (1, 0, 2, 3, 4))
    cb = jnp.transpose(c.reshape(B, nb, Q_BLOCK, N_HEADS), (1, 0, 2, 3))
    pb = pos.reshape(nb, Q_BLOCK)
    out = lax.map(lambda a: fox_block(a[0], k, v, a[1], c, a[2], pos), (qb, cb, pb))
    return jnp.transpose(out, (1, 0, 2, 3, 4)).reshape(B, S, N_HEADS, HEAD_DIM)


def fox_sample(q, k, v, logf, cache_k, cache_v, cache_logf):
    P, T = cache_k.shape[1], q.shape[1]
    kk = jnp.concatenate([cache_k.astype(k.dtype), k], axis=1)
    vv = jnp.concatenate([cache_v.astype(v.dtype), v], axis=1)
    c = jnp.cumsum(jnp.concatenate([cache_logf.astype(jnp.float32), logf], axis=1), axis=1)
    k_pos = jnp.arange(P + T)
    q_pos = P + jnp.arange(T)
    return fox_block(q, kk, vv, c[:, P:], c, q_pos, k_pos)


def layer(x, conv_hist, attend, norm_mix_g, w_in, b_f, w_dw, b_dw, ln_g, ln_b,
          w_conv_pw, w_attn_o, w_out, norm_ffn_g, w_gate, w_up, w_down):
    B, T, _ = x.shape
    u = rmsnorm(x, norm_mix_g) @ w_in
    o = 2 * D_CONV
    glu_in = u[..., :o]
    q = u[..., o:o + D_ATTN].reshape(B, T, N_HEADS, HEAD_DIM); o += D_ATTN
    k = u[..., o:o + D_ATTN].reshape(B, T, N_HEADS, HEAD_DIM); o += D_ATTN
    v = u[..., o:o + D_ATTN].reshape(B, T, N_HEADS, HEAD_DIM); o += D_ATTN
    f_logit = u[..., o:o + N_HEADS]; o += N_HEADS
    g_conv = u[..., o:o + D_MODEL]; o += D_MODEL
    g_attn = u[..., o:o + D_MODEL]
    logf = jax.nn.log_sigmoid(f_logit.astype(jnp.float32) + b_f.astype(jnp.float32))

    a, gl = jnp.split(glu_in, 2, axis=-1)
    cu = a * jax.nn.sigmoid(gl)
    cpad = jnp.concatenate([conv_hist.astype(cu.dtype), cu], axis=1)
    new_hist = cpad[:, -(CONV_WIDTH - 1):]
    cy = jax.nn.silu(layernorm(depthwise_causal(cpad, w_dw, b_dw), ln_g, ln_b)) @ w_conv_pw

    ao = attend(q, k, v, logf).reshape(B, T, D_ATTN) @ w_attn_o

    mixed = jax.nn.sigmoid(g_conv) * cy + jax.nn.sigmoid(g_attn) * ao
    h = x + mixed @ w_out

    z = rmsnorm(h, norm_ffn_g)
    h = h + (jax.nn.silu(z @ w_gate) * (z @ w_up)) @ w_down
    return h, k, v, logf.astype(x.dtype), new_hist


def setup_inputs(seed: int = 0) -> dict:
    key = jax.random.key(seed)
    ks = jax.random.split(key, 24)
    L = DEPTH

    def nrm(k, shape, scale):
        return jax.random.normal(k, shape, jnp.float32) * scale

    return {
        "x_prompt": nrm(ks[0], (BATCH, SEQ, D_MODEL), 1.0),
        "x_sample": nrm(ks[1], (DEC_BATCH, DEC_SEQ, D_MODEL), 1.0),
        "cache_k": nrm(ks[2], (L, DEC_BATCH, PAST_LEN, N_HEADS, HEAD_DIM), 1.0),
        "cache_v": nrm(ks[3], (L, DEC_BATCH, PAST_LEN, N_HEADS, HEAD_DIM), 1.0),
        "cache_logf": jax.nn.log_sigmoid(nrm(ks[4], (L, DEC_BATCH, PAST_LEN, N_HEADS), 1.0) + 3.0),
        "state_conv": nrm(ks[5], (L, DEC_BATCH, CONV_WIDTH - 1, D_CONV), 1.0),
        "norm_mix_g": 1.0 + nrm(ks[6], (L, D_MODEL), 0.02),
        "w_in": nrm(ks[7], (L, D_MODEL, D_IN), D_MODEL ** -0.5),
        "b_f": jax.random.uniform(ks[8], (L, N_HEADS), jnp.float32, 1.0, 5.0),
        "w_dw": nrm(ks[9], (L, CONV_WIDTH, D_CONV), CONV_WIDTH ** -0.5),
        "b_dw": nrm(ks[10], (L, D_CONV), 0.02),
        "ln_g": 1.0 + nrm(ks[11], (L, D_CONV), 0.02),
        "ln_b": nrm(ks[12], (L, D_CONV), 0.02),
        "w_conv_pw": nrm(ks[13], (L, D_CONV, D_MODEL), D_CONV ** -0.5),
        "w_attn_o": nrm(ks[14], (L, D_ATTN, D_MODEL), D_ATTN ** -0.5),
        "w_out": nrm(ks[15], (L, D_MODEL, D_MODEL), D_MODEL ** -0.5),
        "norm_ffn_g": 1.0 + nrm(ks[16], (L, D_MODEL), 0.02),
        "w_gate": nrm(ks[17], (L, D_MODEL, D_FF), D_MODEL ** -0.5),
        "w_up": nrm(ks[18], (L, D_MODEL, D_FF), D_MODEL ** -0.5),
        "w_down": nrm(ks[19], (L, D_FF, D_MODEL), D_FF ** -0.5),
        "final_norm_g": 1.0 + nrm(ks[20], (D_MODEL,), 0.02),
    }


def reference(x_prompt, x_sample, cache_k, cache_v, cache_logf, state_conv,
              norm_mix_g, w_in, b_f, w_dw, b_dw, ln_g, ln_b, w_conv_pw, w_attn_o, w_out,
              norm_ffn_g, w_gate, w_up, w_down, final_norm_g):
    hp, hs = x_prompt, x_sample
    kp_l, vp_l, fp_l, cp_l = [], [], [], []
    ks_l, vs_l, fs_l, cs_l = [], [], [], []
    for l in range(DEPTH):
        wts = (norm_mix_g[l], w_in[l], b_f[l], w_dw[l], b_dw[l], ln_g[l], ln_b[l],
               w_conv_pw[l], w_attn_o[l], w_out[l], norm_ffn_g[l], w_gate[l], w_up[l], w_down[l])
        zero_hist = jnp.zeros((hp.shape[0], CONV_WIDTH - 1, D_CONV), hp.dtype)
        hp, kp, vp, fp, cp = layer(hp, zero_hist, fox_prompt, *wts)
        ck, cv, cf = cache_k[l], cache_v[l], cache_logf[l]
        attend_s = lambda q, k, v, lf: fox_sample(q, k, v, lf, ck, cv, cf)
        hs, ksn, vsn, fsn, csn = layer(hs, state_conv[l], attend_s, *wts)
        kp_l.append(kp); vp_l.append(vp); fp_l.append(fp); cp_l.append(cp)
        ks_l.append(ksn); vs_l.append(vsn); fs_l.append(fsn); cs_l.append(csn)
    y_prompt = rmsnorm(hp, final_norm_g)
    y_sample = rmsnorm(hs, final_norm_g)
    return (y_prompt, y_sample,
            jnp.stack(kp_l), jnp.stack(vp_l), jnp.stack(fp_l), jnp.stack(cp_l),
            jnp.stack(ks_l), jnp.stack(vs_l), jnp.stack(fs_l), jnp.stack(cs_l))
```

```python
import numpy as np
from contextlib import ExitStack
import concourse.bass as bass
import concourse.mybir as mybir
from concourse.bass_utils import run_bass_kernel_spmd

F32 = mybir.dt.float32
BF16 = mybir.dt.bfloat16
AF = mybir.ActivationFunctionType
ALU = mybir.AluOpType

D = 1024; DIN = 4616; DFF = 2816; NH = 8; HD = 64; CW = 31
NLC = 8
NKC = 8 + 2 * 9
NDS = 64
NEGBIG = -30000.0


class Buf:
    __slots__ = ("w", "r")

    def __init__(self):
        self.w = {}
        self.r = {}


class Prog:
    def __init__(self, nc, es):
        self.nc = nc
        self.eng = {"pe": nc.tensor, "act": nc.scalar, "dve": nc.vector, "pool": nc.gpsimd, "sp": nc.sync}
        self.sem = {k: es.enter_context(nc.semaphore("s_" + k)) for k in self.eng}
        self.dsem = [es.enter_context(nc.semaphore("d%d" % i)) for i in range(NDS)]
        self.qrange = {"sp": (0, NDS - 16), "pool": (NDS - 16, NDS)}
        self.reset()

    def reset(self):
        self.cnt = {k: 0 for k in self.eng}
        self.seen = {k: {} for k in self.eng}
        self.dval = [0] * NDS
        self.dlast = [None] * NDS
        self.dnext = {"sp": 0, "pool": NDS - 16}
        self.dry = False
        self.nwait = 0

    def _wait(self, e, key, val):
        if key == ("e", "pe") and e == "pe":
            return
        s = self.seen[e]
        if s.get(key, 0) >= val:
            return
        s[key] = val
        sem = self.sem[key[1]] if key[0] == "e" else self.dsem[key[1]]
        self.eng[e].wait_ge(sem, val)
        self.nwait += 1

    def _deps(self, e, r, w):
        for b in r:
            for k, v in b.w.items():
                self._wait(e, k, v)
        for b in w:
            for k, v in b.w.items():
                self._wait(e, k, v)
            for k, v in b.r.items():
                self._wait(e, k, v)

    def _upd(self, key, val, r, w):
        for b in r:
            if b.r.get(key, 0) < val:
                b.r[key] = val
        for b in w:
            b.w = {key: val}
            b.r = {}

    def op(self, e, fn, r=(), w=()):
        if self.dry:
            return
        self._deps(e, r, w)
        ins = fn(self.eng[e])
        self.cnt[e] += 1
        ins.then_inc(self.sem[e], 1)
        self._upd(("e", e), self.cnt[e], r, w)

    def mm(self, fns, r=(), w=()):
        if self.dry:
            return
        self._deps("pe", r, w)
        ins = None
        for fn in fns:
            ins = fn(self.eng["pe"])
        self.cnt["pe"] += 1
        ins.then_inc(self.sem["pe"], 1)
        self._upd(("e", "pe"), self.cnt["pe"], r, w)

    def dma(self, q, out, in_, r=(), w=()):
        if self.dry:
            return
        for b in r:
            for k, v in b.w.items():
                self._wait(q, k, v)
        for b in w:
            for k, v in b.w.items():
                if k[0] != "d":
                    self._wait(q, k, v)
            for k, v in b.r.items():
                self._wait(q, k, v)
        i = self.dnext[q]
        lo, hi = self.qrange[q]
        self.dnext[q] = lo + (i + 1 - lo) % (hi - lo)
        if self.dlast[i] is not None:
            self._wait(q, ("d", i), self.dlast[i])
        ins = self.eng[q].dma_start(out=out, in_=in_)
        self.dval[i] += 16
        ins.then_inc(self.dsem[i], 16)
        self.dlast[i] = self.dval[i]
        key = ("d", i)
        for b in r:
            if b.r.get(key, 0) < self.dval[i]:
                b.r[key] = self.dval[i]
        for b in w:
            keep = {k: v for k, v in b.w.items() if k[0] == "d"}
            keep[key] = self.dval[i]
            b.w = keep
            b.r = {}

    def finish(self):
        for i in range(NDS):
            if self.dlast[i] is not None:
                self._wait("sp", ("d", i), self.dlast[i])


class Stream:
    def __init__(self, P, slots, depth):
        self.P = P
        self.slots = slots
        self.depth = depth
        self.rec = []
        self.issued = 0
        self.req = 0

    def start_real(self):
        self.issued = 0
        self.req = 0

    def get(self, loader, depth=None):
        depth = self.depth if depth is None else depth
        P = self.P
        n = len(self.slots)
        if P.dry:
            self.rec.append(loader)
            i = len(self.rec) - 1
            return self.slots[i % n]
        i = self.req
        self.req += 1
        lim = min(len(self.rec), i + 1 + depth)
        while self.issued < lim:
            j = self.issued
            t, b = self.slots[j % n]
            self.rec[j](t, b)
            self.issued += 1
        return self.slots[i % n]


def build_nc():
    nc = bass.Bass("TRN2", target_bir_lowering=False)

    def din(name, shape, dt=F32):
        return nc.dram_tensor(name, list(shape), dt, kind="ExternalInput").ap()

    def dout(name, shape, dt=F32):
        return nc.dram_tensor(name, list(shape), dt, kind="ExternalOutput").ap()

    xloc = din("xloc", [NLC, 512, D])
    xs = din("xs", [64, D])
    ck = din("ck", [2, 4096, 512]); cv = din("cv", [2, 4096, 512]); clf = din("clf", [2, 4096, NH])
    sconv = din("sconv", [2, 30, 512])
    kmask_d = din("kmask", [128, NLC])
    w_in = din("w_in", [D, DIN]); w_pw = din("w_pw", [512, D]); w_ao = din("w_ao", [512, D]); w_out = din("w_out", [D, D])
    w_gate = din("w_gate", [D, DFF]); w_up = din("w_up", [D, DFF]); w_down = din("w_down", [DFF, D])
    g1c_d = din("g1c", [128, 8]); g2c_d = din("g2c", [128, 8]); gfin_d = din("gfin", [128, D]); bf_d = din("bfb", [128, NH])
    wdw_d = din("wdw", [128, 4, CW]); bdw_d = din("bdw", [128, 4]); lng_d = din("lng", [128, 4]); lnb_d = din("lnb", [128, 4])
    ident_d = din("ident", [128, 128]); tri_d = din("tri", [128, 128])

    y_own = dout("y_own", [4, 512, D]); k_own = dout("k_own", [4, 512, 512]); v_own = dout("v_own", [4, 512, 512])
    lf_own = dout("lf_own", [4, 512, NH]); conv_own = dout("conv_own", [30, 512])
    ys_o = dout("ys", [64, D]); ks_o = dout("ks", [64, 512]); vs_o = dout("vs", [64, 512]); lfs_o = dout("lfs", [64, NH])
    convs_o = dout("convs", [2, 30, 512])

    KT = nc.dram_tensor("KT", [NKC, 70, NH, 512], BF16, kind="Internal").ap()
    VA = nc.dram_tensor("VA", [NKC, 128, NH, 4, 65], BF16, kind="Internal").ap()

    with ExitStack() as es:
        def sb(name, shape, dt=F32):
            return es.enter_context(nc.sbuf_tensor("sb_" + name, list(shape), dt))

        P = Prog(nc, es)
        ident_f = sb("ident_f", [128, 128]); ident_b = sb("ident_b", [128, 128], BF16)
        tri_f = sb("tri_f", [128, 128]); tri_b = sb("tri_b", [128, 128], BF16)
        ones_f = sb("ones_f", [128, 128])
        negones = sb("negones", [3, 512], BF16)
        g1c = sb("g1c", [128, 8]); g2c = sb("g2c", [128, 8]); gfin = sb("gfin", [128, D]); bfb = sb("bfb", [128, NH])
        wdw = sb("wdw", [128, 4, CW]); bdw = sb("bdw", [128, 4]); lng = sb("lng", [128, 4]); lnb = sb("lnb", [128, 4])
        lngh = sb("lngh", [128, 4]); lnbh = sb("lnbh", [128, 4])
        kmask = sb("kmask", [128, NLC]); kmb = sb("kmb", [128, NLC])
        B_const = Buf()

        xt = [sb("xt%d" % i, [128, 4, D]) for i in range(2)]
        B_xt = [Buf(), Buf()]
        xnb_ring = [(sb("xnb%d" % i, [128, D], BF16), Buf()) for i in range(2)]
        xnb_state = [0]
        junk = sb("junk", [128, D], BF16); B_junk = Buf()
        stat = sb("stat", [128, 32]); B_stat = Buf()
        stat2 = sb("stat2", [128, 32]); B_stat2 = Buf()
        st_sel = [(stat, B_stat), (stat2, B_stat2)]
        xnT = sb("xnT", [128, 8, 544], BF16); B_xnT = Buf()

        wslots = [(sb("wsl%d" % i, [128, 8, 512], BF16), Buf()) for i in range(6)]
        WS = Stream(P, wslots, 2)
        actT = sb("actT", [128, 22, 512], BF16); B_actT = Buf()
        wk_t = actT[:, 0:8, :]; wv_t = actT[:, 8:16, :]; wf_t = sb("wf_t", [128, 8, NH], BF16)
        B_wkv = B_actT

        aoT = sb("aoT", [64, NH, 512], BF16); B_aoT = Buf()
        ktst = aoT; B_ktst = B_aoT
        vaug = sb("vaug", [128, NH, 4, 65], BF16); B_vaug = Buf()
        kvout = sb("kvout", [128, 512]); B_kvout = Buf()
        sT = sb("sT", [128, 4, 512], BF16); B_sT = Buf()
        ckb = sT; B_ckb = B_sT

        zp = sb("zp", [128, 32, NH]); B_zp = Buf()
        lfp = sb("lfp", [128, 32, NH]); B_lfp = Buf()
        zs = sb("zs", [128, NH]); B_zs = Buf()
        lfs = sb("lfs", [128, NH]); B_lfs = Buf()
        lfc = sb("lfc", [128, 33, NH]); B_lfc = Buf()
        ctmp = sb("ctmp", [128, 33, NH]); B_ctmp = Buf()
        coff = sb("coff", [128, 33, NH]); B_coff = Buf()
        ctot = sb("ctot", [128, 33, NH]); B_ctot = Buf()
        cres = sb("cres", [128, 33, NH]); B_cres = Buf()
        spl = sb("spl", [128, 33, 3, NH], BF16); B_spl = Buf()
        rowst = sb("rowst", [24, 512], BF16); B_rowst = Buf()

        cuT = sb("cuT", [128, 4, 544]); B_cuT = Buf()
        cacc = sb("cacc", [128, 4, 512]); B_cacc = Buf()
        xhalo = cacc[:, :, :].rearrange("p a b -> p (a b)")[0:32, 0:D]; B_xhalo = B_cacc
        thc = sb("thc", [128, 512]); B_thc = Buf()
        tg = thc; B_tg = B_thc
        csq = thc; B_csq = B_thc
        m1 = sb("m1", [128, 512]); B_m1 = Buf()
        lnm = m1; B_lnm = B_m1
        lnr = sb("lnr", [128, 512]); B_lnr = Buf()
        mixT = sb("mixT", [128, 8, 512], BF16); B_mixT = Buf()
        qT = mixT[0:70, :, :]; B_qT = B_mixT; B_qTaug = B_mixT
        vsl = [(sb("vsl%d" % i, [128, 4, 65], BF16), Buf()) for i in range(6)]
        VS = Stream(P, vsl, 3)
        ktsl = [(sb("ktsl%d" % i, [70, 512], BF16), Buf()) for i in range(6)]
        KS = Stream(P, ktsl, 3)
        NPT = 5
        pT = [sb("pT%d" % i, [128, 512], BF16) for i in range(NPT)]; B_pT = [Buf() for _ in range(NPT)]
        oc = [sb("oc%d" % i, [65, 512]) for i in range(2)]; B_oc = [Buf(), Buf()]
        rinv = sb("rinv", [64, 512]); B_rinv = Buf()
        zrow = sb("zrow", [1, 512], BF16)
        zT = xnT[:, :, 0:512]; B_zT = B_xnT
        yout = cuT[:, :, :].rearrange("p a b -> p (a b)")[:, 0:D]; B_yout = B_cuT
        chst = sb("chst", [32, 512]); B_chst = Buf()
        schist = chst; B_schist = B_chst

        pst = [es.enter_context(nc.psum_tensor("pst%d" % i, [128, 1024], BF16)) for i in range(2)]
        B_pst = [Buf(), Buf()]
        psf = [es.enter_context(nc.psum_tensor("psf%d" % i, [128, 512], F32)) for i in range(6)]
        B_psf = [Buf() for _ in range(6)]
        psstate = {"free": [True] * 6, "nxt": 0, "tn": 0}

        def ps_get():
            for _ in range(6):
                i = psstate["nxt"]
                psstate["nxt"] = (i + 1) % 6
                if psstate["free"][i]:
                    psstate["free"][i] = False
                    return i
            raise RuntimeError("psum exhausted")

        def ps_free(i):
            psstate["free"][i] = True

        def pst_get():
            i = psstate["tn"]
            psstate["tn"] = 1 - i
            return i

        wl = {"n": 0, "flags": [], "scr": {}}

        def wload_generic(key, shape, src, part):
            if P.dry:
                first = key not in wl["scr"]
                if first:
                    wl["scr"][key] = (nc.dram_tensor("wsc%d" % len(wl["scr"]), list(shape), BF16, kind="Internal").ap(), Buf())
                wl["flags"].append(first)
            else:
                first = wl["flags"][wl["n"]]
                wl["n"] += 1
            scr, Bscr = wl["scr"][key]

            def view(t):
                return t[0:part, 0:shape[1], 0:shape[2]]
            if first:
                def loader(t, b):
                    P.dma("pool", view(t), src, r=(), w=(b,))
            else:
                def loader(t, b):
                    P.dma("sp", view(t), scr, r=(Bscr,), w=(b,))
            t, b = WS.get(loader)
            if first:
                P.dma("sp", scr, view(t), r=(b,), w=(Bscr,))
            return t, b

        def wload(W, k0, nk, c0, ncols, wn=None):
            src = W[k0 * 128:(k0 + nk) * 128, c0:c0 + ncols].rearrange("(k p) n -> p k n", p=128)
            return wload_generic((W.tensor.name, k0, nk, c0, ncols), [128, nk, ncols], src, 128)

        def rstd_from_ss(ncol, nfeat, eps, st=None, B_st=None):
            st = stat if st is None else st
            B_st = B_stat if B_st is None else B_st
            P.op("dve", lambda e: e.tensor_scalar(out=st[:, 8:8 + ncol], in0=st[:, 0:ncol], scalar1=1.0 / nfeat, scalar2=eps,
                                                 op0=ALU.mult, op1=ALU.add), r=(B_st,), w=(B_st,))
            P.op("act", lambda e: e.activation(out=st[:, 24:24 + ncol], in_=st[:, 8:8 + ncol], func=AF.Sqrt), r=(B_st,), w=(B_st,))
            P.op("dve", lambda e: e.reciprocal(out=st[:, 16:16 + ncol], in_=st[:, 24:24 + ncol]), r=(B_st,), w=(B_st,))

        def norm_stats(blocks, extra_r=(), st=None, B_st=None):
            st = stat if st is None else st
            B_st = B_stat if B_st is None else B_st
            nb = len(blocks)
            P.op("dve", lambda e: e.memset(st[:, 0:8], 0.0), w=(B_st,))
            for bi, (src, npp, c0) in enumerate(blocks):
                P.op("act", lambda e, src=src, npp=npp, bi=bi: e.activation(out=junk[0:npp, :], in_=src, func=AF.Square,
                                                                            accum_out=st[0:npp, bi:bi + 1]),
                     r=extra_r, w=(B_junk, B_st))
            rstd_from_ss(nb, D, 1e-6, st, B_st)

        def norm_apply(blocks, gcol, dstT, B_dstT, extra_r=(), st=None, B_st=None):
            st = stat if st is None else st
            B_st = B_stat if B_st is None else B_st
            for bi, (src, npp, c0) in enumerate(blocks):
                xnb, B_xnb = xnb_ring[xnb_state[0]]; xnb_state[0] = 1 - xnb_state[0]
                P.op("act", lambda e, src=src, npp=npp, bi=bi, xnb=xnb: e.activation(out=xnb[0:npp, :], in_=src, func=AF.Copy, scale=st[0:npp, 16 + bi:17 + bi]),
                     r=extra_r + (B_st,), w=(B_xnb,))
                ti = pst_get()
                pv = pst[ti][:, :].rearrange("p (k t) -> p k t", k=8)
                P.mm([lambda e, kc=kc, npp=npp, pv=pv, xnb=xnb: e.transpose(out=pv[:, kc, 0:npp], in_=xnb[0:npp, kc * 128:(kc + 1) * 128],
                                                                   identity=ident_b[0:npp, 0:npp]) for kc in range(8)],
                     r=(B_xnb, B_const), w=(B_pst[ti],))
                P.op("dve", lambda e, pv=pv, npp=npp, c0=c0: e.tensor_tensor(out=dstT[:, :, c0:c0 + npp], in0=pv[:, :, 0:npp],
                                                                            in1=gcol[:, 0:8].unsqueeze(2).to_broadcast([128, 8, npp]), op=ALU.mult),
                     r=(B_pst[ti], B_const), w=(B_dstT,))

        def norm_T(blocks, gcol, dstT, B_dstT, extra_r=(), st=None, B_st=None):
            norm_stats(blocks, extra_r, st, B_st)
            norm_apply(blocks, gcol, dstT, B_dstT, extra_r, st, B_st)

        def mm_fm(wt, B_wt, nk, wc0, M, actTile, B_act, ac0, N, bank):
            P.mm([lambda e, kc=kc: e.matmul(psf[bank][0:M, 0:N], lhsT=wt[:, kc, wc0:wc0 + M], rhs=actTile[:, kc, ac0:ac0 + N],
                                             start=(kc == 0), stop=(kc == nk - 1)) for kc in range(nk)],
                 r=(B_wt, B_act), w=(B_psf[bank],))

        def mm_tm(actTile, B_act, nk, ac0, M, wt, B_wt, wc0, N, bank, first=True, last=True, kp=128):
            P.mm([lambda e, kc=kc: e.matmul(psf[bank][0:M, 0:N], lhsT=actTile[0:kp, kc, ac0:ac0 + M], rhs=wt[0:kp, kc, wc0:wc0 + N],
                                             start=(first and kc == 0), stop=(last and kc == nk - 1)) for kc in range(nk)],
                 r=(B_wt, B_act), w=(B_psf[bank],))

        def program():
            for bi in range(4):
                P.dma("sp", xt[0][:, bi, :], xloc[0, bi * 128:(bi + 1) * 128, :], w=(B_xt[0],))
            P.dma("pool", wk_t, w_in[:, 1536:2048].rearrange("(k p) n -> p k n", p=128), w=(B_wkv,))
            for t, d in ((ident_f, ident_d), (tri_f, tri_d), (g1c, g1c_d), (g2c, g2c_d), (gfin, gfin_d), (bfb, bf_d),
                         (bdw, bdw_d), (lng, lng_d), (lnb, lnb_d), (kmask, kmask_d)):
                P.dma("sp", t[:], d, w=(B_const,))
            P.dma("sp", wdw[:], wdw_d, w=(B_const,))
            P.op("dve", lambda e: e.tensor_copy(out=ident_b[:], in_=ident_f[:]), r=(B_const,), w=(B_const,))
            P.op("dve", lambda e: e.tensor_copy(out=tri_b[:], in_=tri_f[:]), r=(B_const,), w=(B_const,))
            P.op("dve", lambda e: e.memset(ones_f[:], 1.0), w=(B_const,))
            P.op("dve", lambda e: e.memset(negones[:], -1.0), w=(B_const,))
            P.op("dve", lambda e: e.tensor_scalar(out=lngh[:], in0=lng[:], scalar1=0.5, scalar2=None, op0=ALU.mult), r=(B_const,), w=(B_const,))
            P.op("dve", lambda e: e.tensor_scalar(out=lnbh[:], in0=lnb[:], scalar1=0.5, scalar2=None, op0=ALU.mult), r=(B_const,), w=(B_const,))
            P.op("dve", lambda e: e.tensor_scalar(out=kmb[:], in0=kmask[:], scalar1=-1.0, scalar2=-NEGBIG, op0=ALU.add, op1=ALU.mult),
                 r=(B_const,), w=(B_const,))
            P.op("dve", lambda e: e.memset(vaug[:], 1.0), w=(B_vaug,))
            P.op("dve", lambda e: e.memset(zrow[:], 0.0), w=(B_const,))
            P.op("dve", lambda e: e.memset(zs[:], 0.0), w=(B_zs,))
            B_KT = [Buf() for _ in range(NKC)]
            B_KTa = [Buf() for _ in range(NKC)]
            B_KTn = [Buf() for _ in range(NKC)]
            B_VA = [Buf() for _ in range(NKC)]
            P.dma("pool", wv_t, w_in[:, 2048:2560].rearrange("(k p) n -> p k n", p=128), w=(B_wkv,))
            P.dma("pool", wf_t[:], w_in[:, 2560:2568].rearrange("(k p) n -> p k n", p=128), w=(B_wkv,))

            def phase1_load(ti, src_blocks):
                xtile = xt[ti % 2]; Bx = B_xt[ti % 2]
                for bi, (src, npp) in enumerate(src_blocks):
                    P.dma("sp", xtile[0:npp, bi, :], src, w=(Bx,))

            xnT2 = cuT[:, :, :].rearrange("p a b -> p (a b)").bitcast(BF16).rearrange("p (k t) -> p k t", k=8)
            xn_sel = [(xnT, B_xnT), (xnT2, B_cuT)]

            def phase1_stats(ti, src_blocks, kc_idx, own_i, is_sample):
                xtile = xt[ti % 2]; Bx = B_xt[ti % 2]
                st, B_st = st_sel[ti % 2]
                norm_stats([(xtile[0:npp, bi, :], npp, bi * 128) for bi, (src, npp) in enumerate(src_blocks)], (Bx,), st, B_st)

            def phase1_norm(ti, src_blocks, kc_idx, own_i, is_sample):
                xtile = xt[ti % 2]; Bx = B_xt[ti % 2]
                xnT, B_xnT = xn_sel[ti % 2]
                st, B_st = st_sel[ti % 2]
                norm_apply([(xtile[0:npp, bi, :], npp, bi * 128) for bi, (src, npp) in enumerate(src_blocks)], g1c, xnT, B_xnT, (Bx,), st, B_st)

            def phase1_tile(ti, src_blocks, kc_idx, own_i, is_sample, mid):
                xtile = xt[ti % 2]; Bx = B_xt[ti % 2]
                xnT, B_xnT = xn_sel[ti % 2]
                nb = len(src_blocks)
                NT = sum(b[1] for b in src_blocks)
                for g in range(4):
                    bk = ps_get()
                    mm_fm(wk_t, B_wkv, 8, g * 128, 128, xnT, B_xnT, 0, NT, bk)
                    P.op("act", lambda e, g=g, bk=bk: e.copy(out=ktst[0:64, 2 * g, 0:NT], in_=psf[bk][0:64, 0:NT]), r=(B_psf[bk],), w=(B_ktst,))
                    P.op("dve", lambda e, g=g, bk=bk: e.tensor_copy(out=ktst[0:64, 2 * g + 1, 0:NT], in_=psf[bk][64:128, 0:NT]), r=(B_psf[bk],), w=(B_ktst,))
                    ps_free(bk)
                if not is_sample:
                    P.dma("sp", KT[kc_idx, 0:64, :, :], ktst[:, :, :], r=(B_ktst,), w=(B_KT[kc_idx],))
                else:
                    for s in range(2):
                        kcn = 8 + 9 * s + 8
                        P.dma("sp", KT[kcn, 0:64, :, 0:32], ktst[:, :, 32 * s:32 * s + 32], r=(B_ktst,), w=(B_KT[kcn],))
                mid()
                for bi, (src, npp) in enumerate(src_blocks):
                    if own_i is not None or is_sample:
                        bk = ps_get()
                        mm_tm(xnT, B_xnT, 8, bi * 128, npp, wk_t, B_wkv, 0, 512, bk)
                        P.op("act", lambda e, bk=bk, npp=npp, bi=bi: e.copy(out=kvout[0:npp, :], in_=psf[bk][0:npp, :]), r=(B_psf[bk],), w=(B_kvout,))
                        ps_free(bk)
                        if own_i is not None:
                            P.dma("sp", k_own[own_i, bi * 128:(bi + 1) * 128, :], kvout[:, :], r=(B_kvout,), w=())
                        else:
                            P.dma("sp", ks_o[:, :], kvout[0:64, :], r=(B_kvout,), w=())
                for bi, (src, npp) in enumerate(src_blocks):
                    bk = ps_get()
                    mm_tm(xnT, B_xnT, 8, bi * 128, npp, wv_t, B_wkv, 0, 512, bk)
                    P.op("dve", lambda e, bk=bk, npp=npp, bi=bi: e.tensor_copy(out=vaug[0:npp, :, bi, 0:64],
                                                                               in_=psf[bk][0:npp, :].rearrange("p (h d) -> p h d", h=NH)),
                         r=(B_psf[bk],), w=(B_vaug,))
                    if own_i is not None or is_sample:
                        P.op("act", lambda e, bk=bk, npp=npp, bi=bi: e.copy(out=kvout[0:npp, :], in_=psf[bk][0:npp, :]), r=(B_psf[bk],), w=(B_kvout,))
                        if own_i is not None:
                            P.dma("sp", v_own[own_i, bi * 128:(bi + 1) * 128, :], kvout[:, :], r=(B_kvout,), w=())
                        else:
                            P.dma("sp", vs_o[:, :], kvout[0:64, :], r=(B_kvout,), w=())
                    ps_free(bk)
                if not is_sample:
                    P.dma("sp", VA[kc_idx].rearrange("p h b d -> p (h b d)"), vaug[:, :, :, :].rearrange("p h b d -> p (h b d)"), r=(B_vaug,), w=(B_VA[kc_idx],))
                else:
                    for s in range(2):
                        kcn = 8 + 9 * s + 8
                        P.dma("sp", VA[kcn, 0:32, :, 0, :], vaug[32 * s:32 * s + 32, :, 0, :], r=(B_vaug,), w=(B_VA[kcn],))
                for bi, (src, npp) in enumerate(src_blocks):
                    bk = ps_get()
                    P.mm([lambda e, kc=kc, bi=bi, npp=npp, bk=bk: e.matmul(psf[bk][0:npp, 0:NH], lhsT=xnT[:, kc, bi * 128:bi * 128 + npp], rhs=wf_t[:, kc, :],
                                                                          start=(kc == 0), stop=(kc == 7)) for kc in range(8)],
                         r=(B_wkv, B_xnT), w=(B_psf[bk],))
                    if not is_sample:
                        P.op("dve", lambda e, bk=bk, bi=bi: e.tensor_tensor(out=zp[:, kc_idx * 4 + bi, :], in0=psf[bk][:, 0:NH], in1=bfb[:, :], op=ALU.add),
                             r=(B_psf[bk], B_const), w=(B_zp,))
                    else:
                        P.op("dve", lambda e, bk=bk: e.tensor_tensor(out=zs[0:64, :], in0=psf[bk][0:64, 0:NH], in1=bfb[0:64, :], op=ALU.add),
                             r=(B_psf[bk], B_const), w=(B_zs,))
                    ps_free(bk)

            ktst2_ring = [(mixT[0:64, :, :], B_mixT),
                          (cacc[:, :, :].rearrange("p a b -> p (a b)").bitcast(BF16)[0:64, :].rearrange("p (h t) -> p h t", h=NH), B_cacc)]
            kt2_state = [0]
            vaug2 = actT[:, 16:21, :].rearrange("p a b -> p (a b)")[:, 0:NH * 4 * 65].rearrange("p (h b d) -> p h b d", h=NH, b=4)
            B_vaug2 = Buf()
            P.op("dve", lambda e: e.memset(vaug2, 1.0), w=(B_vaug2,))

            def cache_conv(s, cc):
                kcn = 8 + 9 * s + cc

                def kl(t, b):
                    P.dma("pool", t[:, 0:4, :], ck[s, cc * 512:(cc + 1) * 512, :].rearrange("(b p) n -> p b n", p=128), w=(b,))

                def vl(t, b):
                    P.dma("pool", t[:, 0:4, :], cv[s, cc * 512:(cc + 1) * 512, :].rearrange("(b p) n -> p b n", p=128), w=(b,))
                tk, Bk = WS.get(kl, depth=4)
                ktst2, B_kt2 = ktst2_ring[kt2_state[0]]; kt2_state[0] = 1 - kt2_state[0]
                for bi in range(4):
                    ti = pst_get()
                    pv = pst[ti][:, 0:512].rearrange("p (g t) -> p g t", g=4)
                    P.mm([lambda e, g=g, bi=bi, pv=pv: e.transpose(out=pv[:, g, :], in_=tk[:, bi, g * 128:(g + 1) * 128], identity=ident_b[:, :])
                          for g in range(4)], r=(Bk, B_const), w=(B_pst[ti],))
                    kv = ktst2[:, :, bi * 128:(bi + 1) * 128].rearrange("p (g e) t -> p g e t", e=2)
                    P.op("act", lambda e, pv=pv, kv=kv: e.copy(out=kv[:, :, 0, :], in_=pv[0:64, :, :]), r=(B_pst[ti],), w=(B_kt2,))
                    P.op("dve", lambda e, pv=pv, kv=kv: e.tensor_copy(out=kv[:, :, 1, :], in_=pv[64:128, :, :]), r=(B_pst[ti],), w=(B_kt2,))
                P.dma("sp", KT[kcn, 0:64, :, :], ktst2, r=(B_kt2,), w=(B_KT[kcn],))
                tv, Bv = WS.get(vl, depth=4)
                P.op("act", lambda e: e.copy(out=vaug2[:, :, :, 0:64].rearrange("p h b d -> p b h d"), in_=tv[:, 0:4, :].rearrange("p b (h d) -> p b h d", h=NH)),
                     r=(Bv,), w=(B_vaug2,))
                P.dma("sp", VA[kcn].rearrange("p h b d -> p (h b d)"), vaug2.rearrange("p h b d -> p (h b d)"), r=(B_vaug2,), w=(B_VA[kcn],))

            p1tiles = [(l, [(xloc[l, bi * 128:(bi + 1) * 128, :], 128) for bi in range(4)], l, (l // 2) if (l % 2 == 1) else None, False) for l in range(NLC)]
            p1tiles.append((NLC, [(xs[:, :], 64)], None, None, True))
            phase1_load(1, p1tiles[1][1])
            for kc in range(NKC):
                P.dma("sp", KT[kc, 67:70, :, :], negones[:, :].unsqueeze(1).to_broadcast([3, NH, 512]), r=(B_const,), w=(B_KTn[kc],))
            phase1_stats(*p1tiles[0])
            phase1_norm(*p1tiles[0])
            cconv = [(s_, c_) for s_ in range(2) for c_ in range(8)]
            for ti_, tl in enumerate(p1tiles):
                def mid(ti_=ti_):
                    if ti_ + 1 < len(p1tiles):
                        phase1_norm(*p1tiles[ti_ + 1])
                    if ti_ + 2 < len(p1tiles):
                        phase1_load(ti_ + 2, p1tiles[ti_ + 2][1])
                if ti_ + 1 < len(p1tiles):
                    phase1_stats(*p1tiles[ti_ + 1])
                phase1_tile(*tl, mid)
                for _ in range(2):
                    if cconv:
                        cache_conv(*cconv.pop(0))
            while cconv:
                cache_conv(*cconv.pop(0))
            B_actT.r.update(B_vaug2.r); B_actT.r.update(B_vaug2.w)

            tick_on = [True]
            bg_gen = [None]

            def bg_step(n):
                for _ in range(n):
                    if bg_gen[0] is not None:
                        try:
                            next(bg_gen[0])
                        except StopIteration:
                            bg_gen[0] = None

            def logf_section(tick):
                def dv(fn, **kw):
                    P.op("dve", fn, **kw)
                    if tick_on[0]:
                        tick()
                P.op("act", lambda e: e.activation(out=lfp[:], in_=zp[:], func=AF.Exp, scale=-1.0), r=(B_zp,), w=(B_lfp,))
                P.op("act", lambda e: e.activation(out=lfs[:], in_=zs[:], func=AF.Exp, scale=-1.0), r=(B_zs,), w=(B_lfs,))
                P.op("act", lambda e: e.activation(out=lfp[:], in_=lfp[:], func=AF.Ln, bias=1.0), r=(B_lfp,), w=(B_lfp,))
                P.op("act", lambda e: e.activation(out=lfs[:], in_=lfs[:], func=AF.Ln, bias=1.0), r=(B_lfs,), w=(B_lfs,))
                for l in range(NLC):
                    dv(lambda e, l=l: e.tensor_scalar(out=lfp[:, 4 * l:4 * l + 4, :], in0=lfp[:, 4 * l:4 * l + 4, :], scalar1=kmask[:, l:l + 1],
                                                              scalar2=None, op0=ALU.mult), r=(B_lfp, B_const), w=(B_lfp,))
                dv(lambda e: e.tensor_scalar(out=lfp[:], in0=lfp[:], scalar1=-1.0, scalar2=None, op0=ALU.mult), r=(B_lfp,), w=(B_lfp,))
                dv(lambda e: e.tensor_scalar(out=lfs[:], in0=lfs[:], scalar1=-1.0, scalar2=None, op0=ALU.mult), r=(B_lfs,), w=(B_lfs,))
                for i in range(4):
                    l = 2 * i + 1
                    P.dma("sp", lf_own[i].rearrange("(b p) h -> p b h", p=128), lfp[:, 4 * l:4 * l + 4, :], r=(B_lfp,), w=())
                P.dma("sp", lfs_o[:, :], lfs[0:64, :], r=(B_lfs,), w=())

                def cumsum_rows(LF, B_LF, NB, kc_list, maskcol):
                    b1 = ps_get(); b2 = ps_get()
                    lf2 = LF[:, 0:NB, :].rearrange("p b h -> p (b h)")
                    P.mm([lambda e: e.matmul(psf[b1][:, 0:NB * NH], lhsT=tri_f[:, :], rhs=lf2, start=True, stop=True)], r=(B_LF, B_const), w=(B_psf[b1],))
                    yield
                    P.mm([lambda e: e.matmul(psf[b2][:, 0:NB * NH], lhsT=ones_f[:, :], rhs=lf2, start=True, stop=True)], r=(B_LF, B_const), w=(B_psf[b2],))
                    yield
                    P.op("act", lambda e: e.copy(out=ctot[:, 0:NB, :].rearrange("p b h -> p (b h)"), in_=psf[b2][:, 0:NB * NH]), r=(B_psf[b2],), w=(B_ctot,))
                    yield
                    ps_free(b2)
                    dv(lambda e: e.memset(coff[:, 0, :], 0.0), w=(B_coff,))
                    yield
                    dv(lambda e: e.tensor_copy(out=coff[:, 1:NB, :], in_=ctot[:, 0:NB - 1, :]), r=(B_ctot,), w=(B_coff,))
                    yield
                    bufs = [(coff, B_coff), (ctot, B_ctot)]
                    for si, sh in enumerate([1, 2, 4, 8, 16, 32]):
                        (src, Bs), (dst, Bd) = bufs[si % 2], bufs[(si + 1) % 2]
                        m = min(sh, NB)
                        dv(lambda e, src=src, dst=dst, m=m: e.tensor_copy(out=dst[:, 0:m, :], in_=src[:, 0:m, :]), r=(Bs,), w=(Bd,))
                        yield
                        if sh < NB:
                            dv(lambda e, src=src, dst=dst, sh=sh: e.tensor_tensor(out=dst[:, sh:NB, :], in0=src[:, sh:NB, :], in1=src[:, 0:NB - sh, :], op=ALU.add),
                                 r=(Bs,), w=(Bd,))
                            yield
                    dv(lambda e: e.tensor_tensor(out=ctmp[:, 0:NB, :].rearrange("p b h -> p (b h)"), in0=psf[b1][:, 0:NB * NH],
                                                          in1=coff[:, 0:NB, :].rearrange("p b h -> p (b h)"), op=ALU.add), r=(B_psf[b1], B_coff), w=(B_ctmp,))
                    yield
                    ps_free(b1)
                    dv(lambda e: e.tensor_scalar(out=cres[:, 0:NB, :], in0=ctmp[:, 0:NB, :], scalar1=-1.0, scalar2=None, op0=ALU.mult),
                         r=(B_ctmp,), w=(B_cres,))
                    yield
                    if maskcol:
                        for l in range(NLC):
                            dv(lambda e, l=l: e.tensor_scalar(out=cres[:, 4 * l:4 * l + 4, :], in0=cres[:, 4 * l:4 * l + 4, :], scalar1=kmb[:, l:l + 1],
                                                                      scalar2=None, op0=ALU.add), r=(B_cres, B_const), w=(B_cres,))
                            yield
                    dv(lambda e: e.tensor_copy(out=spl[:, 0:NB, 0, :], in_=cres[:, 0:NB, :]), r=(B_cres,), w=(B_spl,))
                    yield
                    dv(lambda e: e.tensor_tensor(out=ctmp[:, 0:NB, :], in0=cres[:, 0:NB, :], in1=spl[:, 0:NB, 0, :], op=ALU.subtract),
                         r=(B_cres, B_spl), w=(B_ctmp,))
                    yield
                    dv(lambda e: e.tensor_copy(out=spl[:, 0:NB, 1, :], in_=ctmp[:, 0:NB, :]), r=(B_ctmp,), w=(B_spl,))
                    yield
                    dv(lambda e: e.tensor_tensor(out=cres[:, 0:NB, :], in0=ctmp[:, 0:NB, :], in1=spl[:, 0:NB, 1, :], op=ALU.subtract),
                         r=(B_ctmp, B_spl), w=(B_cres,))
                    yield
                    dv(lambda e: e.tensor_copy(out=spl[:, 0:NB, 2, :], in_=cres[:, 0:NB, :]), r=(B_cres,), w=(B_spl,))
                    yield
                    for ci, kc in enumerate(kc_list):
                        nbk = min(4, NB - 4 * ci)
                        ti = pst_get()
                        P.mm([lambda e, bi=bi, ci=ci, ti=ti: e.transpose(out=pst[ti][0:24, bi * 128:(bi + 1) * 128],
                                                                          in_=spl[:, 4 * ci + bi, :, :].rearrange("p j h -> p (j h)"), identity=ident_b[:, :])
                              for bi in range(nbk)], r=(B_spl, B_const), w=(B_pst[ti],))
                        yield
                        P.op("act", lambda e, ti=ti, nbk=nbk: e.copy(out=rowst[:, 0:nbk * 128], in_=pst[ti][0:24, 0:nbk * 128]), r=(B_pst[ti],), w=(B_rowst,))
                        yield
                        ncol = 512 if nbk == 4 else 32
                        P.dma("sp", KT[kc, 64:67, :, 0:ncol].rearrange("j h k -> (j h) k"), rowst[:, 0:ncol], r=(B_rowst,), w=(B_KTa[kc],))
                        yield

                for _ in cumsum_rows(lfp, B_lfp, 32, list(range(8)), True):
                    pass

                def sample_cs():
                    for s in range(2):
                        P.dma("sp", lfc[:, 0:32, :], clf[s].rearrange("(b p) h -> p b h", p=128), w=(B_lfc,))
                        dv(lambda e: e.memset(lfc[:, 32, :], 0.0), w=(B_lfc,))
                        dv(lambda e, s=s: e.tensor_copy(out=lfc[0:32, 32, :], in_=lfs[32 * s:32 * s + 32, :]), r=(B_lfs,), w=(B_lfc,))
                        yield
                        yield from cumsum_rows(lfc, B_lfc, 33, [8 + 9 * s + j for j in range(9)], False)
                tick_on[0] = False
                bg_gen[0] = sample_cs()

            def qaug(q0, nq, diag_kc):
                P.dma("sp", qT[67:70, :, q0:q0 + nq], KT[diag_kc, 64:67, :, 0:nq], r=(B_KTa[diag_kc],), w=(B_qTaug,))

            def finalize_a(bo, ncol):
                ob = fin_state[0]; fin_state[0] = 1 - ob
                P.op("act", lambda e: e.copy(out=oc[ob][:, 0:ncol], in_=psf[bo][0:65, 0:ncol]), r=(B_psf[bo],), w=(B_oc[ob],))
                ps_free(bo)
                P.op("dve", lambda e: e.reciprocal(out=oc[ob][64:65, 0:ncol], in_=oc[ob][64:65, 0:ncol]), r=(B_oc[ob],), w=(B_oc[ob],))
                return ob

            def finalize_b(ob, ncol, dst_fn):
                bb = ps_get()
                P.mm([lambda e: e.matmul(psf[bb][0:64, 0:ncol], lhsT=ones_f[64:65, 0:64], rhs=oc[ob][64:65, 0:ncol], start=True, stop=True)],
                     r=(B_oc[ob], B_const), w=(B_psf[bb],))
                P.op("act", lambda e: e.copy(out=rinv[:, 0:ncol], in_=psf[bb][0:64, 0:ncol]), r=(B_psf[bb],), w=(B_rinv,))
                ps_free(bb)
                dst_fn(oc[ob], B_oc[ob], bb)

            fin_state = [0]
            pt_state = [0]
            LOOK = 3

            def attention(nq, chunks, diag_kc, fillers, tail):
                allc = chunks + [diag_kc]
                qaug(0, nq, diag_kc)
                nfill = (len(fillers) + NH - 3) // (NH - 2)
                fpos = 0
                pending_fin = [None]
                tail_gen = [None]

                def flush_fin():
                    if pending_fin[0] is not None:
                        ob_, h_ = pending_fin[0]
                        pending_fin[0] = None

                        def dst(ocb, Bocb, bb, h_=h_):
                            P.op("dve", lambda e: e.tensor_tensor(out=aoT[:, h_, 0:nq], in0=ocb[0:64, 0:nq], in1=rinv[:, 0:nq], op=ALU.mult),
                                 r=(Bocb, B_rinv), w=(B_aoT,))
                        finalize_b(ob_, nq, dst)
                for h in range(NH):
                    bo = ps_get()
                    blocks = [(j, kb) for j in range(len(allc)) for kb in range(4)]
                    nb = len(blocks)
                    cur = {}
                    pend = {}
                    for idx in range(nb + LOOK):
                        if idx == 8:
                            flush_fin()
                        if idx < nb:
                            j, kb = blocks[idx]
                            kc = allc[j]
                            isdiag = (j == len(allc) - 1)
                            if kb == 0:
                                def loader(t, b, kc=kc, h=h):
                                    P.dma("sp", t[:, 0:512], KT[kc, :, h, :], r=(B_KT[kc], B_KTa[kc], B_KTn[kc]), w=(b,))

                                def vloader(t, b, kc=kc, h=h):
                                    P.dma("sp", t[:, :, :], VA[kc, :, h, :, :], r=(B_VA[kc],), w=(b,))
                                cur["k"] = KS.get(loader)
                                cur["v"] = VS.get(vloader)
                            kt, Bkt = cur["k"]
                            vt, Bvt = cur["v"]
                            c0 = kb * 128 if isdiag else 0
                            n = nq - c0
                            bs = ps_get()
                            P.mm([lambda e, kb=kb, c0=c0, n=n, bs=bs, kt=kt: e.matmul(psf[bs][:, 0:n], lhsT=kt[:, kb * 128:(kb + 1) * 128],
                                                                                  rhs=qT[:, h, c0:c0 + n], start=True, stop=True)],
                                 r=(Bkt, B_qT), w=(B_psf[bs],))
                            p = pt_state[0]; pt_state[0] = (p + 1) % NPT
                            P.op("act", lambda e, p=p, n=n, bs=bs: e.activation(out=pT[p][:, 0:n], in_=psf[bs][:, 0:n], func=AF.Exp),
                                 r=(B_psf[bs],), w=(B_pT[p],))
                            ps_free(bs)
                            if isdiag:
                                P.op("pool", lambda e, p=p: e.tensor_tensor(out=pT[p][:, 0:128], in0=pT[p][:, 0:128], in1=tri_b[:, :], op=ALU.mult),
                                     r=(B_pT[p], B_const), w=(B_pT[p],))
                            pend[idx] = (p, kb, c0, n, vt, Bvt)
                            if tail_gen[0] is not None:
                                next(tail_gen[0], None)
                        if idx >= LOOK:
                            i2 = idx - LOOK
                            p, kb, c0, n, vt, Bvt = pend.pop(i2)
                            P.mm([lambda e, p=p, kb=kb, c0=c0, n=n, vt=vt, i2=i2: e.matmul(
                                psf[bo][0:65, c0:c0 + n], lhsT=vt[:, kb, :], rhs=pT[p][:, 0:n], start=(i2 == 0), stop=(i2 == nb - 1))],
                                 r=(B_pT[p], Bvt), w=(B_psf[bo],))

                    flush_fin()
                    pending_fin[0] = (finalize_a(bo, nq), h)
                    for f in fillers[fpos:fpos + nfill]:
                        f()
                    fpos += nfill
                    if h == NH - 2:
                        for f in fillers[fpos:]:
                            f()
                        fpos = len(fillers)
                        tail_gen[0] = tail()
                flush_fin()
                if tail_gen[0] is not None:
                    for _ in tail_gen[0]:
                        pass

            def attention_sample(s, fillers=None):
                q0 = 32 * s
                kcs = [8 + 9 * s + j for j in range(9)]
                qaug(q0, 32, kcs[8])
                bo = ps_get()
                P.mm([lambda e: e.matmul(psf[bo][0:65, 0:256], lhsT=zrow[0:1, 0:65], rhs=zrow[0:1, 0:256], start=True, stop=False)],
                     r=(B_const,), w=(B_psf[bo],))
                blocks = [(j, kb) for j in range(8) for kb in range(4)] + [(8, 0)]
                nb = len(blocks)
                cur = {}
                pend = {}
                for idx in range(nb + LOOK):
                    if idx < nb:
                        j, kb = blocks[idx]
                        kc = kcs[j]
                        isdiag = (j == 8)
                        kw = 32 if isdiag else 128
                        if kb == 0:
                            def loader(t, b, kc=kc, isdiag=isdiag):
                                if isdiag:
                                    P.dma("sp", t[0:70, :, 0:32], KT[kc, :, :, 0:32], r=(B_KT[kc], B_KTa[kc], B_KTn[kc]), w=(b,))
                                else:
                                    P.dma("sp", t[0:70, :, :], KT[kc], r=(B_KT[kc], B_KTa[kc], B_KTn[kc]), w=(b,))

                            def vloader(t, b, kc=kc, isdiag=isdiag):
                                if isdiag:
                                    P.dma("sp", t[0:32, :, 0:65], VA[kc, 0:32, :, 0, :], r=(B_VA[kc],), w=(b,))
                                else:
                                    P.dma("sp", t[:, :, 0:260], VA[kc].rearrange("p h b d -> p h (b d)"), r=(B_VA[kc],), w=(b,))
                            cur["k"] = WS.get(loader)
                            cur["v"] = WS.get(vloader)
                        kt, Bkt = cur["k"]
                        vt, Bvt = cur["v"]
                        bs = ps_get()
                        P.mm([lambda e, h=h, kb=kb, kw=kw, bs=bs, kt=kt: e.matmul(psf[bs][0:kw, h * 32:(h + 1) * 32], lhsT=kt[0:70, h, kb * 128:kb * 128 + kw],
                                                                             rhs=qT[:, h, q0:q0 + 32], start=True, stop=True) for h in range(NH)],
                             r=(Bkt, B_qT), w=(B_psf[bs],))
                        p = pt_state[0]; pt_state[0] = (p + 1) % NPT
                        P.op("act", lambda e, p=p, kw=kw, bs=bs: e.activation(out=pT[p][0:kw, 0:256], in_=psf[bs][0:kw, 0:256], func=AF.Exp),
                             r=(B_psf[bs],), w=(B_pT[p],))
                        ps_free(bs)
                        if isdiag:
                            P.op("dve", lambda e, p=p: e.tensor_tensor(out=pT[p][0:32, 0:256].rearrange("p (h q) -> p h q", h=NH),
                                                                        in0=pT[p][0:32, 0:256].rearrange("p (h q) -> p h q", h=NH),
                                                                        in1=tri_b[0:32, 0:32].unsqueeze(1).to_broadcast([32, NH, 32]), op=ALU.mult),
                                 r=(B_pT[p], B_const), w=(B_pT[p],))
                        pend[idx] = (p, kb, kw, vt, Bvt)
                        if fillers:
                            for _ in range(4):
                                if fillers:
                                    fillers.pop(0)()
                    if idx >= LOOK:
                        i2 = idx - LOOK
                        p, kb, kw, vt, Bvt = pend.pop(i2)
                        P.mm([lambda e, h=h, p=p, kb=kb, kw=kw, vt=vt, i2=i2: e.matmul(
                            psf[bo][0:65, h * 32:(h + 1) * 32], lhsT=vt[0:kw, h, kb * 65:(kb + 1) * 65], rhs=pT[p][0:kw, h * 32:(h + 1) * 32],
                            start=False, stop=(i2 == nb - 1 and h == NH - 1)) for h in range(NH)],
                             r=(B_pT[p], Bvt), w=(B_psf[bo],))

                def dst(ocb, Bocb, bb):
                    P.op("dve", lambda e: e.tensor_tensor(out=aoT[:, :, q0:q0 + 32], in0=ocb[0:64, 0:256].rearrange("p (h q) -> p h q", h=NH),
                                                          in1=rinv[:, 0:256].rearrange("p (h q) -> p h q", h=NH), op=ALU.mult),
                         r=(Bocb, B_rinv), w=(B_aoT,))
                finalize_b(finalize_a(bo, 256), 256, dst)

            def tile2_load(ti, is_sample, l):
                xtile = xt[ti % 2]; Bx = B_xt[ti % 2]
                if not is_sample:
                    for bi in range(4):
                        P.dma("sp", xtile[:, bi, :], xloc[l, bi * 128:(bi + 1) * 128, :], w=(Bx,))
                    P.dma("sp", xhalo[:, :], xloc[l - 1, 480:512, :], w=(B_xhalo,))
                else:
                    P.dma("sp", xtile[0:64, 0, :], xs[:, :], w=(Bx,))

            pre_stats = [False]
            pre_apply = [False]

            def norm1_blocks(ti, is_sample):
                xtile = xt[ti % 2]
                if not is_sample:
                    return [(xhalo[:, :], 32, 0)] + [(xtile[:, bi, :], 128, 32 + bi * 128) for bi in range(4)], (B_xhalo, B_xt[ti % 2])
                return [(xtile[0:64, 0, :], 64, 0)], (B_xt[ti % 2],)

            def tile2(ti, is_sample, l, own_i, nxt):
                NT = 64 if is_sample else 512
                xtile = xt[ti % 2]; Bx = B_xt[ti % 2]
                n1b, n1r = norm1_blocks(ti, is_sample)
                if not pre_stats[0]:
                    norm_stats(n1b, n1r, stat2, B_stat2)
                pre_stats[0] = False
                if not pre_apply[0]:
                    norm_apply(n1b, g1c, xnT, B_xnT, n1r, stat2, B_stat2)
                pre_apply[0] = False
                if not is_sample:
                    tblocks = [(128, bi * 128) for bi in range(4)]
                    xc0 = 32; glu_n = 544; glu_c0 = 0
                    segs = [(32, 512)]
                    cu_dst0 = 0
                else:
                    tblocks = [(64, 0)]
                    xc0 = 0; glu_n = 64; glu_c0 = 0
                    segs = [(32, 32), (96, 32)]
                    P.op("dve", lambda e: e.memset(cuT[:, :, 0:128], 0.0), w=(B_cuT,))
                    for s in range(2):
                        P.dma("sp", schist[0:30, :], sconv[s], w=(B_schist,))
                        for cc in range(4):
                            bk = ps_get()
                            P.mm([lambda e, cc=cc, bk=bk: e.matmul(psf[bk][:, 0:30], lhsT=schist[0:30, cc * 128:(cc + 1) * 128], rhs=ident_f[0:30, 0:30],
                                                                  start=True, stop=True)], r=(B_schist, B_const), w=(B_psf[bk],))
                            P.op("act", lambda e, cc=cc, bk=bk, s=s: e.copy(out=cuT[:, cc, 64 * s + 2:64 * s + 32], in_=psf[bk][:, 0:30]), r=(B_psf[bk],), w=(B_cuT,))
                            ps_free(bk)

                wa, Bwa = wload(w_in, 0, 8, 0, 512)
                wg, Bwg = wload(w_in, 0, 8, 512, 512)
                for cc in range(4):
                    col = 0
                    while col < glu_n:
                        n = min(512, glu_n - col)
                        ba = ps_get(); bg = ps_get()
                        mm_fm(wg, Bwg, 8, cc * 128, 128, xnT, B_xnT, glu_c0 + col, n, bg)
                        mm_fm(wa, Bwa, 8, cc * 128, 128, xnT, B_xnT, glu_c0 + col, n, ba)
                        P.op("act", lambda e, bg=bg, n=n: e.activation(out=tg[:, 0:n], in_=psf[bg][:, 0:n], func=AF.Tanh, scale=0.5), r=(B_psf[bg],), w=(B_tg,))
                        ps_free(bg)
                        if not is_sample:
                            dsts = [(cuT[:, cc, col:col + n], 0, n)]
                        else:
                            dsts = [(cuT[:, cc, 32:64], 0, 32), (cuT[:, cc, 96:128], 32, 32)]
                        for dst, s0, sn in dsts:
                            P.op("dve", lambda e, dst=dst, s0=s0, sn=sn, ba=ba: e.scalar_tensor_tensor(out=dst, in0=tg[:, s0:s0 + sn], scalar=1.0, in1=psf[ba][:, s0:s0 + sn],
                                                                                                  op0=ALU.add, op1=ALU.mult), r=(B_tg, B_psf[ba]), w=(B_cuT,))
                        ps_free(ba)
                        col += n
                if not is_sample:
                    P.op("act", lambda e: e.activation(out=cuT[:, :, 0:544], in_=cuT[:, :, 0:544], func=AF.Copy, scale=0.5), r=(B_cuT,), w=(B_cuT,))
                else:
                    for s in range(2):
                        P.op("act", lambda e, s=s: e.activation(out=cuT[:, :, 64 * s + 32:64 * s + 64], in_=cuT[:, :, 64 * s + 32:64 * s + 64], func=AF.Copy, scale=0.5),
                             r=(B_cuT,), w=(B_cuT,))
                def hist_out(colbase, dst):
                    for cc in range(4):
                        bk = ps_get()
                        P.mm([lambda e, cc=cc, bk=bk: e.matmul(psf[bk][0:30, 0:128], lhsT=cuT[:, cc, colbase:colbase + 30], rhs=ident_f[:, :], start=True, stop=True)],
                             r=(B_cuT, B_const), w=(B_psf[bk],))
                        P.op("act", lambda e, cc=cc, bk=bk: e.copy(out=chst[0:30, cc * 128:(cc + 1) * 128], in_=psf[bk][0:30, 0:128]), r=(B_psf[bk],), w=(B_chst,))
                        ps_free(bk)
                    P.dma("sp", dst, chst[0:30, :], r=(B_chst,), w=())
                if is_sample:
                    for s in range(2):
                        hist_out(64 * s + 34, convs_o[s])
                elif own_i == 3:
                    hist_out(514, conv_own[:, :])

                wq, Bwq = wload(w_in, 0, 8, 1024, 512)
                for g in range(4):
                    bk = ps_get()
                    mm_fm(wq, Bwq, 8, g * 128, 128, xnT, B_xnT, xc0, NT, bk)
                    P.op("act", lambda e, g=g, bk=bk: e.activation(out=qT[0:64, 2 * g, 0:NT], in_=psf[bk][0:64, 0:NT], func=AF.Copy, scale=0.125), r=(B_psf[bk],), w=(B_qT,))
                    P.op("dve", lambda e, g=g, bk=bk: e.tensor_scalar(out=qT[0:64, 2 * g + 1, 0:NT], in0=psf[bk][64:128, 0:NT], scalar1=0.125, scalar2=None, op0=ALU.mult),
                         r=(B_psf[bk],), w=(B_qT,))
                    ps_free(bk)
                P.op("dve", lambda e: e.memset(qT[64:67, :, 0:NT], 1.0), w=(B_qT,))

                fillers = []
                for cc in range(4):
                    for (o0, n) in segs:
                        d0 = o0 - 32 if not is_sample else (0 if o0 == 32 else 32)
                        for j in range(CW):
                            src = cuT[:, cc, o0 - 30 + j:o0 - 30 + j + n]
                            if j == 0:
                                fillers.append(lambda src=src, cc=cc, d0=d0, n=n: P.op(
                                    "dve", lambda e: e.tensor_scalar(out=cacc[:, cc, d0:d0 + n], in0=src, scalar1=wdw[:, cc, 0:1],
                                                                     scalar2=bdw[:, cc:cc + 1], op0=ALU.mult, op1=ALU.add),
                                    r=(B_cuT, B_const), w=(B_cacc,)))
                            else:
                                fillers.append(lambda src=src, cc=cc, d0=d0, n=n, j=j: P.op(
                                    "dve", lambda e: e.scalar_tensor_tensor(out=cacc[:, cc, d0:d0 + n], in0=src, scalar=wdw[:, cc, j:j + 1],
                                                                            in1=cacc[:, cc, d0:d0 + n], op0=ALU.mult, op1=ALU.add),
                                    r=(B_cuT, B_const, B_cacc), w=(B_cacc,)))
                def ln_block():
                    b1 = ps_get(); b2 = ps_get()
                    P.mm([lambda e, cc=cc: e.matmul(psf[b1][:, 0:NT], lhsT=ones_f[:, :], rhs=cacc[:, cc, 0:NT], start=(cc == 0), stop=(cc == 3)) for cc in range(4)],
                         r=(B_cacc, B_const), w=(B_psf[b1],))
                    yield
                    for cc in range(4):
                        P.op("act", lambda e, cc=cc: e.activation(out=csq[:, 0:NT], in_=cacc[:, cc, 0:NT], func=AF.Square), r=(B_cacc,), w=(B_csq,))
                        yield
                        P.mm([lambda e, cc=cc: e.matmul(psf[b2][:, 0:NT], lhsT=ones_f[:, :], rhs=csq[:, 0:NT], start=(cc == 0), stop=(cc == 3))],
                             r=(B_csq, B_const), w=(B_psf[b2],))
                        yield
                    P.op("act", lambda e: e.activation(out=lnm[:, 0:NT], in_=psf[b1][:, 0:NT], func=AF.Copy, scale=1.0 / 512), r=(B_psf[b1],), w=(B_lnm,))
                    yield
                    ps_free(b1)
                    P.op("dve", lambda e: e.tensor_tensor(out=lnr[:, 0:NT], in0=lnm[:, 0:NT], in1=lnm[:, 0:NT], op=ALU.mult), r=(B_lnm,), w=(B_lnr,))
                    yield
                    P.op("dve", lambda e: e.scalar_tensor_tensor(out=lnr[:, 0:NT], in0=psf[b2][:, 0:NT], scalar=1.0 / 512, in1=lnr[:, 0:NT], op0=ALU.mult, op1=ALU.subtract),
                         r=(B_psf[b2], B_lnr), w=(B_lnr,))
                    yield
                    ps_free(b2)
                    P.op("dve", lambda e: e.tensor_scalar(out=lnr[:, 0:NT], in0=lnr[:, 0:NT], scalar1=1e-5, scalar2=None, op0=ALU.add), r=(B_lnr,), w=(B_lnr,))
                    yield
                    P.op("act", lambda e: e.activation(out=lnr[:, 0:NT], in_=lnr[:, 0:NT], func=AF.Sqrt), r=(B_lnr,), w=(B_lnr,))
                    yield
                    P.op("dve", lambda e: e.reciprocal(out=lnr[:, 0:NT], in_=lnr[:, 0:NT]), r=(B_lnr,), w=(B_lnr,))
                    yield
                    for cc in range(4):
                        P.op("dve", lambda e, cc=cc: e.tensor_tensor(out=cacc[:, cc, 0:NT], in0=cacc[:, cc, 0:NT], in1=lnm[:, 0:NT], op=ALU.subtract),
                             r=(B_cacc, B_lnm), w=(B_cacc,))
                        yield
                        P.op("dve", lambda e, cc=cc: e.tensor_tensor(out=cacc[:, cc, 0:NT], in0=cacc[:, cc, 0:NT], in1=lnr[:, 0:NT], op=ALU.mult),
                             r=(B_cacc, B_lnr), w=(B_cacc,))
                        yield
                        P.op("act", lambda e, cc=cc: e.activation(out=cacc[:, cc, 0:NT], in_=cacc[:, cc, 0:NT], func=AF.Identity, scale=lngh[:, cc:cc + 1], bias=lnbh[:, cc:cc + 1]),
                             r=(B_cacc, B_const), w=(B_cacc,))
                        yield
                        P.op("act", lambda e, cc=cc: e.activation(out=csq[:, 0:NT], in_=cacc[:, cc, 0:NT], func=AF.Tanh), r=(B_cacc,), w=(B_csq,))
                        yield
                        P.op("dve", lambda e, cc=cc: e.scalar_tensor_tensor(out=sT[:, cc, 0:NT], in0=csq[:, 0:NT], scalar=1.0, in1=cacc[:, cc, 0:NT], op0=ALU.add, op1=ALU.mult),
                             r=(B_csq, B_cacc), w=(B_sT,))
                        yield


                yield fillers
                if not is_sample:
                    attention(512, list(range(l)), l, fillers, ln_block)
                else:
                    for s in range(2):
                        attention_sample(s, fillers)
                    while fillers:
                        fillers.pop(0)()
                    for _ in ln_block():
                        pass

                if nxt is not None:
                    tile2_load(*nxt)

                def wao_load(c0):
                    src = w_ao[:, c0:c0 + 512].rearrange("(h p) n -> p h n", p=64)
                    return wload_generic(("w_ao", c0), [64, 8, 512], src, 64)
                wpw = {}; wgc = {}; wao = {}; wga = {}
                for half in range(2):
                    wgc[half] = wload(w_in, 0, 8, 2568 + half * 512, 512)
                    wpw[half] = wload(w_pw, 0, 4, half * 512, 512)
                    wga[half] = wload(w_in, 0, 8, 3592 + half * 512, 512)
                    wao[half] = wao_load(half * 512)
                    for f4 in range(4):
                        fc = half * 4 + f4
                        bg = ps_get()
                        mm_fm(wgc[half][0], wgc[half][1], 8, f4 * 128, 128, xnT, B_xnT, xc0, NT, bg)
                        P.op("act", lambda e, bg=bg: e.activation(out=thc[:, 0:NT], in_=psf[bg][:, 0:NT], func=AF.Tanh, scale=0.5), r=(B_psf[bg],), w=(B_thc,))
                        ps_free(bg)
                        by = ps_get()
                        mm_fm(wpw[half][0], wpw[half][1], 4, f4 * 128, 128, sT, B_sT, 0, NT, by)
                        P.op("dve", lambda e, by=by: e.scalar_tensor_tensor(out=m1[:, 0:NT], in0=thc[:, 0:NT], scalar=1.0, in1=psf[by][:, 0:NT], op0=ALU.add, op1=ALU.mult),
                             r=(B_thc, B_psf[by]), w=(B_m1,))
                        ps_free(by)
                        bg = ps_get()
                        mm_fm(wga[half][0], wga[half][1], 8, f4 * 128, 128, xnT, B_xnT, xc0, NT, bg)
                        P.op("act", lambda e, bg=bg: e.activation(out=thc[:, 0:NT], in_=psf[bg][:, 0:NT], func=AF.Tanh, scale=0.5), r=(B_psf[bg],), w=(B_thc,))
                        ps_free(bg)
                        by = ps_get()
                        wt, Bwt = wao[half]
                        P.mm([lambda e, h=h, by=by, wt=wt, f4=f4: e.matmul(psf[by][:, 0:NT], lhsT=wt[0:64, h, f4 * 128:(f4 + 1) * 128], rhs=aoT[:, h, 0:NT],
                                                                           start=(h == 0), stop=(h == NH - 1)) for h in range(NH)], r=(Bwt, B_aoT), w=(B_psf[by],))
                        P.op("dve", lambda e, by=by: e.scalar_tensor_tensor(out=thc[:, 0:NT], in0=thc[:, 0:NT], scalar=1.0, in1=psf[by][:, 0:NT], op0=ALU.add, op1=ALU.mult),
                             r=(B_thc, B_psf[by]), w=(B_thc,))
                        ps_free(by)
                        P.op("dve", lambda e, fc=fc: e.tensor_tensor(out=mixT[:, fc, 0:NT], in0=m1[:, 0:NT], in1=thc[:, 0:NT], op=ALU.add), r=(B_m1, B_thc), w=(B_mixT,))

                for half in range(2):
                    wo_t, wo_b = wload(w_out, 0, 8, half * 512, 512)
                    for bi, (npp, c0) in enumerate(tblocks):
                        bk = ps_get()
                        mm_tm(mixT, B_mixT, 8, c0, npp, wo_t, wo_b, 0, 512, bk)
                        P.op("dve", lambda e, bk=bk, bi=bi, npp=npp, half=half: e.scalar_tensor_tensor(out=xtile[0:npp, bi, half * 512:(half + 1) * 512], in0=psf[bk][0:npp, :], scalar=0.5,
                                                                                                  in1=xtile[0:npp, bi, half * 512:(half + 1) * 512], op0=ALU.mult, op1=ALU.add),
                             r=(B_psf[bk], Bx), w=(Bx,))
                        ps_free(bk)
                norm_T([(xtile[0:npp, bi, :], npp, c0) for bi, (npp, c0) in enumerate(tblocks)], g2c, zT, B_zT, extra_r=(Bx,))
                for c5 in range(6):
                    ncols = 512 if c5 < 5 else 256
                    wgt, Bwgt = wload(w_gate, 0, 8, c5 * 512, ncols)
                    wut, Bwut = wload(w_up, 0, 8, c5 * 512, ncols)
                    for f4 in range(ncols // 128):
                        fc = c5 * 4 + f4
                        bg = ps_get(); bu = ps_get()
                        mm_fm(wgt, Bwgt, 8, f4 * 128, 128, zT, B_zT, 0, NT, bg)
                        mm_fm(wut, Bwut, 8, f4 * 128, 128, zT, B_zT, 0, NT, bu)
                        P.op("act", lambda e, bg=bg: e.activation(out=thc[:, 0:NT], in_=psf[bg][:, 0:NT], func=AF.Tanh, scale=0.5), r=(B_psf[bg],), w=(B_thc,))
                        P.op("dve", lambda e, bg=bg: e.scalar_tensor_tensor(out=m1[:, 0:NT], in0=thc[:, 0:NT], scalar=1.0, in1=psf[bg][:, 0:NT], op0=ALU.add, op1=ALU.mult),
                             r=(B_thc, B_psf[bg]), w=(B_m1,))
                        ps_free(bg)
                        P.op("dve", lambda e, bu=bu, fc=fc: e.scalar_tensor_tensor(out=actT[:, fc, 0:NT], in0=m1[:, 0:NT], scalar=0.5, in1=psf[bu][:, 0:NT], op0=ALU.mult, op1=ALU.mult),
                             r=(B_m1, B_psf[bu]), w=(B_actT,))
                        ps_free(bu)
                        bg_step(3)
                if nxt is not None:
                    nb_, nr_ = norm1_blocks(nxt[0], nxt[1])
                    norm_stats(nb_, nr_, stat2, B_stat2)
                    pre_stats[0] = True
                kgroups = [(0, 8), (8, 8), (16, 6)]
                for half in range(2):
                    banks = [ps_get() for _ in tblocks]
                    for gi, (k0, nk) in enumerate(kgroups):
                        wd, Bwd = wload(w_down, k0, nk, half * 512, 512)
                        for bi, (npp, c0) in enumerate(tblocks):
                            P.mm([lambda e, kc=kc, bi=bi, npp=npp, c0=c0, k0=k0, nk=nk, gi=gi, wd=wd: e.matmul(
                                psf[banks[bi]][0:npp, :], lhsT=actT[:, k0 + kc, c0:c0 + npp], rhs=wd[:, kc, 0:512],
                                start=(gi == 0 and kc == 0), stop=(gi == 2 and kc == nk - 1)) for kc in range(nk)],
                                 r=(Bwd, B_actT), w=(B_psf[banks[bi]],))
                    for bi, (npp, c0) in enumerate(tblocks):
                        bk = banks[bi]
                        P.op("dve", lambda e, bk=bk, bi=bi, npp=npp, half=half: e.tensor_tensor(out=xtile[0:npp, bi, half * 512:(half + 1) * 512], in0=psf[bk][0:npp, :],
                                                                                               in1=xtile[0:npp, bi, half * 512:(half + 1) * 512], op=ALU.add),
                             r=(B_psf[bk], Bx), w=(Bx,))
                        ps_free(bk)
                if nxt is not None:
                    nb_, nr_ = norm1_blocks(nxt[0], nxt[1])
                    norm_apply(nb_, g1c, xnT, B_xnT, nr_, stat2, B_stat2)
                    pre_apply[0] = True
                P.op("dve", lambda e: e.memset(stat[:, 0:8], 0.0), w=(B_stat,))
                for bi, (npp, c0) in enumerate(tblocks):
                    P.op("act", lambda e, bi=bi, npp=npp: e.activation(out=junk[0:npp, :], in_=xtile[0:npp, bi, :], func=AF.Square, accum_out=stat[0:npp, bi:bi + 1]),
                         r=(Bx,), w=(B_junk, B_stat))
                rstd_from_ss(len(tblocks), D, 1e-6)
                for bi, (npp, c0) in enumerate(tblocks):
                    for hf, (yb, B_yb) in enumerate(((lnr, B_lnr), (kvout, B_kvout))):
                        P.op("dve", lambda e, bi=bi, npp=npp, hf=hf, yb=yb: e.scalar_tensor_tensor(
                            out=yb[0:npp, :], in0=xtile[0:npp, bi, hf * 512:(hf + 1) * 512], scalar=stat[0:npp, 16 + bi:17 + bi],
                            in1=gfin[0:npp, hf * 512:(hf + 1) * 512], op0=ALU.mult, op1=ALU.mult), r=(Bx, B_stat, B_const), w=(B_yb,))
                        if is_sample:
                            P.dma("sp", ys_o[:, hf * 512:(hf + 1) * 512], yb[0:64, :], r=(B_yb,), w=())
                        else:
                            P.dma("sp", y_own[own_i, bi * 128:(bi + 1) * 128, hf * 512:(hf + 1) * 512], yb[:, :], r=(B_yb,), w=())

            tile2_load(0, False, 1)
            g0 = tile2(0, False, 1, 0, (1, False, 3))
            fl0 = next(g0)

            def tick():
                for _ in range(2):
                    if fl0:
                        fl0.pop(0)()
            logf_section(tick)
            for _ in g0:
                pass
            for i in range(1, 4):
                nxt = (i + 1, False, 2 * i + 3) if i < 3 else (4, True, None)
                for _ in tile2(i, False, 2 * i + 1, i, nxt):
                    pass
            bg_step(100000)
            for _ in tile2(4, True, None, None, None):
                pass
            if not P.dry:
                P.finish()

        P.dry = True
        program()
        P.reset()
        psstate.update({"free": [True] * 6, "nxt": 0, "tn": 0})
        WS.start_real(); KS.start_real(); VS.start_real(); wl["n"] = 0
        program()
    return nc


_NC_CACHE = {}


def kernel(x_prompt, x_sample, cache_k, cache_v, cache_logf, state_conv, norm_mix_g, w_in, b_f, w_dw, b_dw, ln_g, ln_b,
           w_conv_pw, w_attn_o, w_out, norm_ffn_g, w_gate, w_up, w_down, final_norm_g):
    f = lambda a: np.ascontiguousarray(np.asarray(a, dtype=np.float32))
    x_prompt = f(x_prompt); x_sample = f(x_sample)
    cache_k = f(cache_k)[0].reshape(16, 4096, 512); cache_v = f(cache_v)[0].reshape(16, 4096, 512)
    cache_logf = f(cache_logf)[0]; state_conv = f(state_conv)[0]
    col = lambda v, n: np.ascontiguousarray(f(v).reshape(n, 128).T)
    shared = {
        "w_in": f(w_in)[0], "w_pw": f(w_conv_pw)[0], "w_ao": f(w_attn_o)[0], "w_out": f(w_out)[0],
        "w_gate": f(w_gate)[0], "w_up": f(w_up)[0], "w_down": f(w_down)[0],
        "g1c": col(norm_mix_g, 8), "g2c": col(norm_ffn_g, 8),
        "gfin": np.ascontiguousarray(np.broadcast_to(f(final_norm_g).reshape(1, D), (128, D))),
        "bfb": np.ascontiguousarray(np.broadcast_to(f(b_f).reshape(1, NH), (128, NH))),
        "wdw": np.ascontiguousarray(f(w_dw)[0].reshape(CW, 4, 128).transpose(2, 1, 0)),
        "bdw": col(b_dw, 4), "lng": col(ln_g, 4), "lnb": col(ln_b, 4),
        "ident": np.eye(128, dtype=np.float32), "tri": np.triu(np.ones((128, 128), np.float32)),
    }
    in_maps = []
    for c in range(8):
        b, p = c // 2, c % 2
        xc = x_prompt[b].reshape(8, 512, D)
        if p == 1:
            xl = xc
            km = np.ones((128, NLC), np.float32)
        else:
            xl = np.concatenate([np.zeros((1, 512, D), np.float32), xc[:7]], axis=0)
            km = np.ones((128, NLC), np.float32); km[:, 0] = 0.0
        m = dict(shared)
        m.update({"xloc": np.ascontiguousarray(xl), "xs": np.ascontiguousarray(x_sample[2 * c:2 * c + 2].reshape(64, D)),
                  "ck": np.ascontiguousarray(cache_k[2 * c:2 * c + 2]), "cv": np.ascontiguousarray(cache_v[2 * c:2 * c + 2]),
                  "clf": np.ascontiguousarray(cache_logf[2 * c:2 * c + 2]), "sconv": np.ascontiguousarray(state_conv[2 * c:2 * c + 2]),
                  "kmask": km})
        in_maps.append(m)
    if "nc" not in _NC_CACHE:
        _NC_CACHE["nc"] = build_nc()
    res = run_bass_kernel_spmd(_NC_CACHE["nc"], in_maps, core_ids=list(range(8)))
    R = res.results
    y_p = np.zeros((4, 4096, D), np.float32); k_p = np.zeros((1, 4, 4096, NH, HD), np.float32); v_p = np.zeros_like(k_p)
    lf_p = np.zeros((1, 4, 4096, NH), np.float32); cv_p = np.zeros((1, 4, 30, 512), np.float32)
    y_s = np.zeros((16, 32, D), np.float32); k_s = np.zeros((1, 16, 32, NH, HD), np.float32); v_s = np.zeros_like(k_s)
    lf_s = np.zeros((1, 16, 32, NH), np.float32); cv_s = np.zeros((1, 16, 30, 512), np.float32)
    for c in range(8):
        b, p = c // 2, c % 2
        r = R[c]
        for i in range(4):
            gch = 2 * i + 1 - (1 - p)
            sl = slice(gch * 512, (gch + 1) * 512)
            y_p[b, sl] = r["y_own"][i]
            k_p[0, b, sl] = r["k_own"][i].reshape(512, NH, HD)
            v_p[0, b, sl] = r["v_own"][i].reshape(512, NH, HD)
            lf_p[0, b, sl] = r["lf_own"][i]
        if p == 1:
            cv_p[0, b] = r["conv_own"]
        y_s[2 * c:2 * c + 2] = r["ys"].reshape(2, 32, D)
        k_s[0, 2 * c:2 * c + 2] = r["ks"].reshape(2, 32, NH, HD)
        v_s[0, 2 * c:2 * c + 2] = r["vs"].reshape(2, 32, NH, HD)
        lf_s[0, 2 * c:2 * c + 2] = r["lfs"].reshape(2, 32, NH)
        cv_s[0, 2 * c:2 * c + 2] = r["convs"]
    return (y_p, y_s, k_p, v_p, lf_p, cv_p, k_s, v_s, lf_s, cv_s)
```

```python
import numpy as np
from contextlib import ExitStack
import concourse.bass as bass
import concourse.mybir as mybir
from concourse.bass_utils import run_bass_kernel_spmd

F32 = mybir.dt.float32
BF16 = mybir.dt.bfloat16
AF = mybir.ActivationFunctionType
ALU = mybir.AluOpType

D = 1024; DIN = 4616; DFF = 2816; NH = 8; HD = 64; CW = 31
NLC = 8
NKC = 8 + 2 * 9
NDS = 64
NEGBIG = -30000.0


class Buf:
    __slots__ = ("w", "r")

    def __init__(self):
        self.w = {}
        self.r = {}


class Prog:
    def __init__(self, nc, es):
        self.nc = nc
        self.eng = {"pe": nc.tensor, "act": nc.scalar, "dve": nc.vector, "pool": nc.gpsimd, "sp": nc.sync}
        self.sem = {k: es.enter_context(nc.semaphore("s_" + k)) for k in self.eng}
        self.dsem = [es.enter_context(nc.semaphore("d%d" % i)) for i in range(NDS)]
        self.qrange = {"sp": (0, NDS - 16), "pool": (NDS - 16, NDS)}
        self.reset()

    def reset(self):
        self.cnt = {k: 0 for k in self.eng}
        self.seen = {k: {} for k in self.eng}
        self.dval = [0] * NDS
        self.dlast = [None] * NDS
        self.dnext = {"sp": 0, "pool": NDS - 16}
        self.dry = False
        self.nwait = 0

    def _wait(self, e, key, val):
        if key == ("e", "pe") and e == "pe":
            return
        s = self.seen[e]
        if s.get(key, 0) >= val:
            return
        s[key] = val
        sem = self.sem[key[1]] if key[0] == "e" else self.dsem[key[1]]
        self.eng[e].wait_ge(sem, val)
        self.nwait += 1

    def _deps(self, e, r, w):
        for b in r:
            for k, v in b.w.items():
                self._wait(e, k, v)
        for b in w:
            for k, v in b.w.items():
                self._wait(e, k, v)
            for k, v in b.r.items():
                self._wait(e, k, v)

    def _upd(self, key, val, r, w):
        for b in r:
            if b.r.get(key, 0) < val:
                b.r[key] = val
        for b in w:
            b.w = {key: val}
            b.r = {}

    def op(self, e, fn, r=(), w=()):
        if self.dry:
            return
        self._deps(e, r, w)
        ins = fn(self.eng[e])
        self.cnt[e] += 1
        ins.then_inc(self.sem[e], 1)
        self._upd(("e", e), self.cnt[e], r, w)

    def mm(self, fns, r=(), w=()):
        if self.dry:
            return
        self._deps("pe", r, w)
        ins = None
        for fn in fns:
            ins = fn(self.eng["pe"])
        self.cnt["pe"] += 1
        ins.then_inc(self.sem["pe"], 1)
        self._upd(("e", "pe"), self.cnt["pe"], r, w)

    def dma(self, q, out, in_, r=(), w=()):
        if self.dry:
            return
        for b in r:
            for k, v in b.w.items():
                self._wait(q, k, v)
        for b in w:
            for k, v in b.w.items():
                if k[0] != "d":
                    self._wait(q, k, v)
            for k, v in b.r.items():
                self._wait(q, k, v)
        i = self.dnext[q]
        lo, hi = self.qrange[q]
        self.dnext[q] = lo + (i + 1 - lo) % (hi - lo)
        if self.dlast[i] is not None:
            self._wait(q, ("d", i), self.dlast[i])
        ins = self.eng[q].dma_start(out=out, in_=in_)
        self.dval[i] += 16
        ins.then_inc(self.dsem[i], 16)
        self.dlast[i] = self.dval[i]
        key = ("d", i)
        for b in r:
            if b.r.get(key, 0) < self.dval[i]:
                b.r[key] = self.dval[i]
        for b in w:
            keep = {k: v for k, v in b.w.items() if k[0] == "d"}
            keep[key] = self.dval[i]
            b.w = keep
            b.r = {}

    def finish(self):
        for i in range(NDS):
            if self.dlast[i] is not None:
                self._wait("sp", ("d", i), self.dlast[i])


class Stream:
    def __init__(self, P, slots, depth):
        self.P = P
        self.slots = slots
        self.depth = depth
        self.rec = []
        self.issued = 0
        self.req = 0

    def start_real(self):
        self.issued = 0
        self.req = 0

    def get(self, loader, depth=None):
        depth = self.depth if depth is None else depth
        P = self.P
        n = len(self.slots)
        if P.dry:
            self.rec.append(loader)
            i = len(self.rec) - 1
            return self.slots[i % n]
        i = self.req
        self.req += 1
        lim = min(len(self.rec), i + 1 + depth)
        while self.issued < lim:
            j = self.issued
            t, b = self.slots[j % n]
            self.rec[j](t, b)
            self.issued += 1
        return self.slots[i % n]


def build_nc():
    nc = bass.Bass("TRN2", target_bir_lowering=False)

    def din(name, shape, dt=F32):
        return nc.dram_tensor(name, list(shape), dt, kind="ExternalInput").ap()

    def dout(name, shape, dt=F32):
        return nc.dram_tensor(name, list(shape), dt, kind="ExternalOutput").ap()

    xloc = din("xloc", [NLC, 512, D])
    xs = din("xs", [64, D])
    ck = din("ck", [2, 4096, 512]); cv = din("cv", [2, 4096, 512]); clf = din("clf", [2, 4096, NH])
    sconv = din("sconv", [2, 30, 512])
    kmask_d = din("kmask", [128, NLC])
    w_in = din("w_in", [D, DIN]); w_pw = din("w_pw", [512, D]); w_ao = din("w_ao", [512, D]); w_out = din("w_out", [D, D])
    w_gate = din("w_gate", [D, DFF]); w_up = din("w_up", [D, DFF]); w_down = din("w_down", [DFF, D])
    g1c_d = din("g1c", [128, 8]); g2c_d = din("g2c", [128, 8]); gfin_d = din("gfin", [128, D]); bf_d = din("bfb", [128, NH])
    wdw_d = din("wdw", [128, 4, CW]); bdw_d = din("bdw", [128, 4]); lng_d = din("lng", [128, 4]); lnb_d = din("lnb", [128, 4])
    ident_d = din("ident", [128, 128]); tri_d = din("tri", [128, 128])

    y_own = dout("y_own", [4, 512, D]); k_own = dout("k_own", [4, 512, 512]); v_own = dout("v_own", [4, 512, 512])
    lf_own = dout("lf_own", [4, 512, NH]); conv_own = dout("conv_own", [30, 512])
    ys_o = dout("ys", [64, D]); ks_o = dout("ks", [64, 512]); vs_o = dout("vs", [64, 512]); lfs_o = dout("lfs", [64, NH])
    convs_o = dout("convs", [2, 30, 512])

    KT = nc.dram_tensor("KT", [NKC, 70, NH, 512], BF16, kind="Internal").ap()
    VA = nc.dram_tensor("VA", [NKC, 128, NH, 4, 65], BF16, kind="Internal").ap()

    with ExitStack() as es:
        def sb(name, shape, dt=F32):
            return es.enter_context(nc.sbuf_tensor("sb_" + name, list(shape), dt))

        P = Prog(nc, es)
        ident_f = sb("ident_f", [128, 128]); ident_b = sb("ident_b", [128, 128], BF16)
        tri_f = sb("tri_f", [128, 128]); tri_b = sb("tri_b", [128, 128], BF16)
        ones_f = sb("ones_f", [128, 128])
        negones = sb("negones", [3, 512], BF16)
        g1c = sb("g1c", [128, 8]); g2c = sb("g2c", [128, 8]); gfin = sb("gfin", [128, D]); bfb = sb("bfb", [128, NH])
        wdw = sb("wdw", [128, 4, CW]); bdw = sb("bdw", [128, 4]); lng = sb("lng", [128, 4]); lnb = sb("lnb", [128, 4])
        lngh = sb("lngh", [128, 4]); lnbh = sb("lnbh", [128, 4])
        kmask = sb("kmask", [128, NLC]); kmb = sb("kmb", [128, NLC])
        B_const = Buf()

        xt = [sb("xt%d" % i, [128, 4, D]) for i in range(2)]
        B_xt = [Buf(), Buf()]
        xnb_ring = [(sb("xnb%d" % i, [128, D], BF16), Buf()) for i in range(2)]
        xnb_state = [0]
        junk = sb("junk", [128, D], BF16); B_junk = Buf()
        stat = sb("stat", [128, 32]); B_stat = Buf()
        stat2 = sb("stat2", [128, 32]); B_stat2 = Buf()
        st_sel = [(stat, B_stat), (stat2, B_stat2)]
        xnT = sb("xnT", [128, 8, 544], BF16); B_xnT = Buf()

        wslots = [(sb("wsl%d" % i, [128, 8, 512], BF16), Buf()) for i in range(6)]
        WS = Stream(P, wslots, 2)
        actT = sb("actT", [128, 22, 512], BF16); B_actT = Buf()
        wk_t = actT[:, 0:8, :]; wv_t = actT[:, 8:16, :]; wf_t = sb("wf_t", [128, 8, NH], BF16)
        B_wkv = B_actT

        aoT = sb("aoT", [64, NH, 512], BF16); B_aoT = Buf()
        ktst = aoT; B_ktst = B_aoT
        vaug = sb("vaug", [128, NH, 4, 65], BF16); B_vaug = Buf()
        kvout = sb("kvout", [128, 512]); B_kvout = Buf()
        sT = sb("sT", [128, 4, 512], BF16); B_sT = Buf()
        ckb = sT; B_ckb = B_sT

        zp = sb("zp", [128, 32, NH]); B_zp = Buf()
        lfp = sb("lfp", [128, 32, NH]); B_lfp = Buf()
        zs = sb("zs", [128, NH]); B_zs = Buf()
        lfs = sb("lfs", [128, NH]); B_lfs = Buf()
        lfc = sb("lfc", [128, 33, NH]); B_lfc = Buf()
        ctmp = sb("ctmp", [128, 33, NH]); B_ctmp = Buf()
        coff = sb("coff", [128, 33, NH]); B_coff = Buf()
        ctot = sb("ctot", [128, 33, NH]); B_ctot = Buf()
        cres = sb("cres", [128, 33, NH]); B_cres = Buf()
        spl = sb("spl", [128, 33, 3, NH], BF16); B_spl = Buf()
        rowst = sb("rowst", [24, 512], BF16); B_rowst = Buf()

        cuT = sb("cuT", [128, 4, 544]); B_cuT = Buf()
        cacc = sb("cacc", [128, 4, 512]); B_cacc = Buf()
        xhalo = cacc[:, :, :].rearrange("p a b -> p (a b)")[0:32, 0:D]; B_xhalo = B_cacc
        thc = sb("thc", [128, 512]); B_thc = Buf()
        tg = thc; B_tg = B_thc
        csq = thc; B_csq = B_thc
        m1 = sb("m1", [128, 512]); B_m1 = Buf()
        lnm = m1; B_lnm = B_m1
        lnr = sb("lnr", [128, 512]); B_lnr = Buf()
        mixT = sb("mixT", [128, 8, 512], BF16); B_mixT = Buf()
        qT = mixT[0:70, :, :]; B_qT = B_mixT; B_qTaug = B_mixT
        vsl = [(sb("vsl%d" % i, [128, 4, 65], BF16), Buf()) for i in range(6)]
        VS = Stream(P, vsl, 3)
        ktsl = [(sb("ktsl%d" % i, [70, 512], BF16), Buf()) for i in range(6)]
        KS = Stream(P, ktsl, 3)
        NPT = 5
        pT = [sb("pT%d" % i, [128, 512], BF16) for i in range(NPT)]; B_pT = [Buf() for _ in range(NPT)]
        oc = [sb("oc%d" % i, [65, 512]) for i in range(2)]; B_oc = [Buf(), Buf()]
        rinv = sb("rinv", [64, 512]); B_rinv = Buf()
        zrow = sb("zrow", [1, 512], BF16)
        zT = xnT[:, :, 0:512]; B_zT = B_xnT
        yout = cuT[:, :, :].rearrange("p a b -> p (a b)")[:, 0:D]; B_yout = B_cuT
        chst = sb("chst", [32, 512]); B_chst = Buf()
        schist = chst; B_schist = B_chst

        pst = [es.enter_context(nc.psum_tensor("pst%d" % i, [128, 1024], BF16)) for i in range(2)]
        B_pst = [Buf(), Buf()]
        psf = [es.enter_context(nc.psum_tensor("psf%d" % i, [128, 512], F32)) for i in range(6)]
        B_psf = [Buf() for _ in range(6)]
        psstate = {"free": [True] * 6, "nxt": 0, "tn": 0}

        def ps_get():
            for _ in range(6):
                i = psstate["nxt"]
                psstate["nxt"] = (i + 1) % 6
                if psstate["free"][i]:
                    psstate["free"][i] = False
                    return i
            raise RuntimeError("psum exhausted")

        def ps_free(i):
            psstate["free"][i] = True

        def pst_get():
            i = psstate["tn"]
            psstate["tn"] = 1 - i
            return i

        wl = {"n": 0, "flags": [], "scr": {}}

        def wload_generic(key, shape, src, part, depth=None):
            if P.dry:
                first = key not in wl["scr"]
                if first:
                    wl["scr"][key] = (nc.dram_tensor("wsc%d" % len(wl["scr"]), list(shape), BF16, kind="Internal").ap(), Buf())
                wl["flags"].append(first)
            else:
                first = wl["flags"][wl["n"]]
                wl["n"] += 1
            scr, Bscr = wl["scr"][key]

            def view(t):
                return t[0:part, 0:shape[1], 0:shape[2]]
            if first:
                def loader(t, b):
                    P.dma("pool", view(t), src, r=(), w=(b,))
            else:
                def loader(t, b):
                    P.dma("sp", view(t), scr, r=(Bscr,), w=(b,))
            t, b = WS.get(loader, depth=depth)
            if first:
                P.dma("sp", scr, view(t), r=(b,), w=(Bscr,))
            return t, b

        def wload(W, k0, nk, c0, ncols, wn=None, depth=None):
            src = W[k0 * 128:(k0 + nk) * 128, c0:c0 + ncols].rearrange("(k p) n -> p k n", p=128)
            return wload_generic((W.tensor.name, k0, nk, c0, ncols), [128, nk, ncols], src, 128, depth)

        def rstd_from_ss(ncol, nfeat, eps, st=None, B_st=None):
            st = stat if st is None else st
            B_st = B_stat if B_st is None else B_st
            P.op("dve", lambda e: e.tensor_scalar(out=st[:, 8:8 + ncol], in0=st[:, 0:ncol], scalar1=1.0 / nfeat, scalar2=eps,
                                                 op0=ALU.mult, op1=ALU.add), r=(B_st,), w=(B_st,))
            P.op("act", lambda e: e.activation(out=st[:, 24:24 + ncol], in_=st[:, 8:8 + ncol], func=AF.Sqrt), r=(B_st,), w=(B_st,))
            P.op("dve", lambda e: e.reciprocal(out=st[:, 16:16 + ncol], in_=st[:, 24:24 + ncol]), r=(B_st,), w=(B_st,))

        def norm_stats(blocks, extra_r=(), st=None, B_st=None):
            st = stat if st is None else st
            B_st = B_stat if B_st is None else B_st
            nb = len(blocks)
            P.op("dve", lambda e: e.memset(st[:, 0:8], 0.0), w=(B_st,))
            for bi, (src, npp, c0) in enumerate(blocks):
                P.op("act", lambda e, src=src, npp=npp, bi=bi: e.activation(out=junk[0:npp, :], in_=src, func=AF.Square,
                                                                            accum_out=st[0:npp, bi:bi + 1]),
                     r=extra_r, w=(B_junk, B_st))
            rstd_from_ss(nb, D, 1e-6, st, B_st)

        def norm_apply(blocks, gcol, dstT, B_dstT, extra_r=(), st=None, B_st=None):
            st = stat if st is None else st
            B_st = B_stat if B_st is None else B_st
            for bi, (src, npp, c0) in enumerate(blocks):
                xnb, B_xnb = xnb_ring[xnb_state[0]]; xnb_state[0] = 1 - xnb_state[0]
                P.op("act", lambda e, src=src, npp=npp, bi=bi, xnb=xnb: e.activation(out=xnb[0:npp, :], in_=src, func=AF.Copy, scale=st[0:npp, 16 + bi:17 + bi]),
                     r=extra_r + (B_st,), w=(B_xnb,))
                ti = pst_get()
                pv = pst[ti][:, :].rearrange("p (k t) -> p k t", k=8)
                P.mm([lambda e, kc=kc, npp=npp, pv=pv, xnb=xnb: e.transpose(out=pv[:, kc, 0:npp], in_=xnb[0:npp, kc * 128:(kc + 1) * 128],
                                                                   identity=ident_b[0:npp, 0:npp]) for kc in range(8)],
                     r=(B_xnb, B_const), w=(B_pst[ti],))
                P.op("dve", lambda e, pv=pv, npp=npp, c0=c0: e.tensor_tensor(out=dstT[:, :, c0:c0 + npp], in0=pv[:, :, 0:npp],
                                                                            in1=gcol[:, 0:8].unsqueeze(2).to_broadcast([128, 8, npp]), op=ALU.mult),
                     r=(B_pst[ti], B_const), w=(B_dstT,))

        def norm_T(blocks, gcol, dstT, B_dstT, extra_r=(), st=None, B_st=None):
            norm_stats(blocks, extra_r, st, B_st)
            norm_apply(blocks, gcol, dstT, B_dstT, extra_r, st, B_st)

        def mm_fm(wt, B_wt, nk, wc0, M, actTile, B_act, ac0, N, bank):
            P.mm([lambda e, kc=kc: e.matmul(psf[bank][0:M, 0:N], lhsT=wt[:, kc, wc0:wc0 + M], rhs=actTile[:, kc, ac0:ac0 + N],
                                             start=(kc == 0), stop=(kc == nk - 1)) for kc in range(nk)],
                 r=(B_wt, B_act), w=(B_psf[bank],))

        def mm_tm(actTile, B_act, nk, ac0, M, wt, B_wt, wc0, N, bank, first=True, last=True, kp=128):
            P.mm([lambda e, kc=kc: e.matmul(psf[bank][0:M, 0:N], lhsT=actTile[0:kp, kc, ac0:ac0 + M], rhs=wt[0:kp, kc, wc0:wc0 + N],
                                             start=(first and kc == 0), stop=(last and kc == nk - 1)) for kc in range(nk)],
                 r=(B_wt, B_act), w=(B_psf[bank],))

        def program():
            for bi in range(4):
                P.dma("sp", xt[0][:, bi, :], xloc[0, bi * 128:(bi + 1) * 128, :], w=(B_xt[0],))
            P.dma("pool", wk_t, w_in[:, 1536:2048].rearrange("(k p) n -> p k n", p=128), w=(B_wkv,))
            for t, d in ((ident_f, ident_d), (tri_f, tri_d), (g1c, g1c_d), (g2c, g2c_d), (gfin, gfin_d), (bfb, bf_d),
                         (bdw, bdw_d), (lng, lng_d), (lnb, lnb_d), (kmask, kmask_d)):
                P.dma("sp", t[:], d, w=(B_const,))
            P.dma("sp", wdw[:], wdw_d, w=(B_const,))
            P.op("dve", lambda e: e.tensor_copy(out=ident_b[:], in_=ident_f[:]), r=(B_const,), w=(B_const,))
            P.op("dve", lambda e: e.tensor_copy(out=tri_b[:], in_=tri_f[:]), r=(B_const,), w=(B_const,))
            P.op("dve", lambda e: e.memset(ones_f[:], 1.0), w=(B_const,))
            P.op("dve", lambda e: e.memset(negones[:], -1.0), w=(B_const,))
            P.op("dve", lambda e: e.tensor_scalar(out=lngh[:], in0=lng[:], scalar1=0.5, scalar2=None, op0=ALU.mult), r=(B_const,), w=(B_const,))
            P.op("dve", lambda e: e.tensor_scalar(out=lnbh[:], in0=lnb[:], scalar1=0.5, scalar2=None, op0=ALU.mult), r=(B_const,), w=(B_const,))
            P.op("dve", lambda e: e.tensor_scalar(out=kmb[:], in0=kmask[:], scalar1=-1.0, scalar2=-NEGBIG, op0=ALU.add, op1=ALU.mult),
                 r=(B_const,), w=(B_const,))
            P.op("dve", lambda e: e.memset(vaug[:], 1.0), w=(B_vaug,))
            P.op("dve", lambda e: e.memset(zrow[:], 0.0), w=(B_const,))
            P.op("dve", lambda e: e.memset(zs[:], 0.0), w=(B_zs,))
            B_KT = [Buf() for _ in range(NKC)]
            B_KTa = [Buf() for _ in range(NKC)]
            B_KTn = [Buf() for _ in range(NKC)]
            B_VA = [Buf() for _ in range(NKC)]
            P.dma("pool", wv_t, w_in[:, 2048:2560].rearrange("(k p) n -> p k n", p=128), w=(B_wkv,))
            P.dma("pool", wf_t[:], w_in[:, 2560:2568].rearrange("(k p) n -> p k n", p=128), w=(B_wkv,))

            def phase1_load(ti, src_blocks):
                xtile = xt[ti % 2]; Bx = B_xt[ti % 2]
                for bi, (src, npp) in enumerate(src_blocks):
                    P.dma("sp", xtile[0:npp, bi, :], src, w=(Bx,))

            xnT2 = cuT[:, :, :].rearrange("p a b -> p (a b)").bitcast(BF16).rearrange("p (k t) -> p k t", k=8)
            xn_sel = [(xnT, B_xnT), (xnT2, B_cuT)]

            def phase1_stats(ti, src_blocks, kc_idx, own_i, is_sample):
                xtile = xt[ti % 2]; Bx = B_xt[ti % 2]
                st, B_st = st_sel[ti % 2]
                norm_stats([(xtile[0:npp, bi, :], npp, bi * 128) for bi, (src, npp) in enumerate(src_blocks)], (Bx,), st, B_st)

            def phase1_norm(ti, src_blocks, kc_idx, own_i, is_sample):
                xtile = xt[ti % 2]; Bx = B_xt[ti % 2]
                xnT, B_xnT = xn_sel[ti % 2]
                st, B_st = st_sel[ti % 2]
                norm_apply([(xtile[0:npp, bi, :], npp, bi * 128) for bi, (src, npp) in enumerate(src_blocks)], g1c, xnT, B_xnT, (Bx,), st, B_st)

            def phase1_tile(ti, src_blocks, kc_idx, own_i, is_sample, mid):
                xtile = xt[ti % 2]; Bx = B_xt[ti % 2]
                xnT, B_xnT = xn_sel[ti % 2]
                nb = len(src_blocks)
                NT = sum(b[1] for b in src_blocks)
                for g in range(4):
                    bk = ps_get()
                    mm_fm(wk_t, B_wkv, 8, g * 128, 128, xnT, B_xnT, 0, NT, bk)
                    P.op("act", lambda e, g=g, bk=bk: e.copy(out=ktst[0:64, 2 * g, 0:NT], in_=psf[bk][0:64, 0:NT]), r=(B_psf[bk],), w=(B_ktst,))
                    P.op("dve", lambda e, g=g, bk=bk: e.tensor_copy(out=ktst[0:64, 2 * g + 1, 0:NT], in_=psf[bk][64:128, 0:NT]), r=(B_psf[bk],), w=(B_ktst,))
                    ps_free(bk)
                if not is_sample:
                    P.dma("sp", KT[kc_idx, 0:64, :, :], ktst[:, :, :], r=(B_ktst,), w=(B_KT[kc_idx],))
                else:
                    for s in range(2):
                        kcn = 8 + 9 * s + 8
                        P.dma("sp", KT[kcn, 0:64, :, 0:32], ktst[:, :, 32 * s:32 * s + 32], r=(B_ktst,), w=(B_KT[kcn],))
                mid()
                for bi, (src, npp) in enumerate(src_blocks):
                    if own_i is not None or is_sample:
                        bk = ps_get()
                        mm_tm(xnT, B_xnT, 8, bi * 128, npp, wk_t, B_wkv, 0, 512, bk)
                        P.op("act", lambda e, bk=bk, npp=npp, bi=bi: e.copy(out=kvout[0:npp, :], in_=psf[bk][0:npp, :]), r=(B_psf[bk],), w=(B_kvout,))
                        ps_free(bk)
                        if own_i is not None:
                            P.dma("sp", k_own[own_i, bi * 128:(bi + 1) * 128, :], kvout[:, :], r=(B_kvout,), w=())
                        else:
                            P.dma("sp", ks_o[:, :], kvout[0:64, :], r=(B_kvout,), w=())
                for bi, (src, npp) in enumerate(src_blocks):
                    bk = ps_get()
                    mm_tm(xnT, B_xnT, 8, bi * 128, npp, wv_t, B_wkv, 0, 512, bk)
                    P.op("dve", lambda e, bk=bk, npp=npp, bi=bi: e.tensor_copy(out=vaug[0:npp, :, bi, 0:64],
                                                                               in_=psf[bk][0:npp, :].rearrange("p (h d) -> p h d", h=NH)),
                         r=(B_psf[bk],), w=(B_vaug,))
                    if own_i is not None or is_sample:
                        P.op("act", lambda e, bk=bk, npp=npp, bi=bi: e.copy(out=kvout[0:npp, :], in_=psf[bk][0:npp, :]), r=(B_psf[bk],), w=(B_kvout,))
                        if own_i is not None:
                            P.dma("sp", v_own[own_i, bi * 128:(bi + 1) * 128, :], kvout[:, :], r=(B_kvout,), w=())
                        else:
                            P.dma("sp", vs_o[:, :], kvout[0:64, :], r=(B_kvout,), w=())
                    ps_free(bk)
                if not is_sample:
                    P.dma("sp", VA[kc_idx].rearrange("p h b d -> p (h b d)"), vaug[:, :, :, :].rearrange("p h b d -> p (h b d)"), r=(B_vaug,), w=(B_VA[kc_idx],))
                else:
                    for s in range(2):
                        kcn = 8 + 9 * s + 8
                        P.dma("sp", VA[kcn, 0:32, :, 0, :], vaug[32 * s:32 * s + 32, :, 0, :], r=(B_vaug,), w=(B_VA[kcn],))
                for bi, (src, npp) in enumerate(src_blocks):
                    bk = ps_get()
                    P.mm([lambda e, kc=kc, bi=bi, npp=npp, bk=bk: e.matmul(psf[bk][0:npp, 0:NH], lhsT=xnT[:, kc, bi * 128:bi * 128 + npp], rhs=wf_t[:, kc, :],
                                                                          start=(kc == 0), stop=(kc == 7)) for kc in range(8)],
                         r=(B_wkv, B_xnT), w=(B_psf[bk],))
                    if not is_sample:
                        P.op("dve", lambda e, bk=bk, bi=bi: e.tensor_tensor(out=zp[:, kc_idx * 4 + bi, :], in0=psf[bk][:, 0:NH], in1=bfb[:, :], op=ALU.add),
                             r=(B_psf[bk], B_const), w=(B_zp,))
                    else:
                        P.op("dve", lambda e, bk=bk: e.tensor_tensor(out=zs[0:64, :], in0=psf[bk][0:64, 0:NH], in1=bfb[0:64, :], op=ALU.add),
                             r=(B_psf[bk], B_const), w=(B_zs,))
                    ps_free(bk)

            ktst2_ring = [(mixT[0:64, :, :], B_mixT),
                          (cacc[:, :, :].rearrange("p a b -> p (a b)").bitcast(BF16)[0:64, :].rearrange("p (h t) -> p h t", h=NH), B_cacc)]
            kt2_state = [0]
            vaug2 = actT[:, 16:21, :].rearrange("p a b -> p (a b)")[:, 0:NH * 4 * 65].rearrange("p (h b d) -> p h b d", h=NH, b=4)
            B_vaug2 = Buf()
            P.op("dve", lambda e: e.memset(vaug2, 1.0), w=(B_vaug2,))

            def cache_conv(s, cc):
                kcn = 8 + 9 * s + cc

                def kl(t, b):
                    P.dma("pool", t[:, 0:4, :], ck[s, cc * 512:(cc + 1) * 512, :].rearrange("(b p) n -> p b n", p=128), w=(b,))

                def vl(t, b):
                    P.dma("pool", t[:, 0:4, :], cv[s, cc * 512:(cc + 1) * 512, :].rearrange("(b p) n -> p b n", p=128), w=(b,))
                tk, Bk = WS.get(kl, depth=4)
                ktst2, B_kt2 = ktst2_ring[kt2_state[0]]; kt2_state[0] = 1 - kt2_state[0]
                for bi in range(4):
                    ti = pst_get()
                    pv = pst[ti][:, 0:512].rearrange("p (g t) -> p g t", g=4)
                    P.mm([lambda e, g=g, bi=bi, pv=pv: e.transpose(out=pv[:, g, :], in_=tk[:, bi, g * 128:(g + 1) * 128], identity=ident_b[:, :])
                          for g in range(4)], r=(Bk, B_const), w=(B_pst[ti],))
                    kv = ktst2[:, :, bi * 128:(bi + 1) * 128].rearrange("p (g e) t -> p g e t", e=2)
                    P.op("act", lambda e, pv=pv, kv=kv: e.copy(out=kv[:, :, 0, :], in_=pv[0:64, :, :]), r=(B_pst[ti],), w=(B_kt2,))
                    P.op("dve", lambda e, pv=pv, kv=kv: e.tensor_copy(out=kv[:, :, 1, :], in_=pv[64:128, :, :]), r=(B_pst[ti],), w=(B_kt2,))
                P.dma("sp", KT[kcn, 0:64, :, :], ktst2, r=(B_kt2,), w=(B_KT[kcn],))
                tv, Bv = WS.get(vl, depth=4)
                P.op("act", lambda e: e.copy(out=vaug2[:, :, :, 0:64].rearrange("p h b d -> p b h d"), in_=tv[:, 0:4, :].rearrange("p b (h d) -> p b h d", h=NH)),
                     r=(Bv,), w=(B_vaug2,))
                P.dma("sp", VA[kcn].rearrange("p h b d -> p (h b d)"), vaug2.rearrange("p h b d -> p (h b d)"), r=(B_vaug2,), w=(B_VA[kcn],))

            p1tiles = [(l, [(xloc[l, bi * 128:(bi + 1) * 128, :], 128) for bi in range(4)], l, (l // 2) if (l % 2 == 1) else None, False) for l in range(NLC)]
            p1tiles.append((NLC, [(xs[:, :], 64)], None, None, True))
            phase1_load(1, p1tiles[1][1])
            for kc in range(NKC):
                P.dma("sp", KT[kc, 67:70, :, :], negones[:, :].unsqueeze(1).to_broadcast([3, NH, 512]), r=(B_const,), w=(B_KTn[kc],))
            phase1_stats(*p1tiles[0])
            phase1_norm(*p1tiles[0])
            cconv = [(s_, c_) for s_ in range(2) for c_ in range(8)]
            for ti_, tl in enumerate(p1tiles):
                def mid(ti_=ti_):
                    if ti_ + 1 < len(p1tiles):
                        phase1_norm(*p1tiles[ti_ + 1])
                    if ti_ + 2 < len(p1tiles):
                        phase1_load(ti_ + 2, p1tiles[ti_ + 2][1])
                if ti_ + 1 < len(p1tiles):
                    phase1_stats(*p1tiles[ti_ + 1])
                phase1_tile(*tl, mid)
                for _ in range(2):
                    if cconv:
                        cache_conv(*cconv.pop(0))
            while cconv:
                cache_conv(*cconv.pop(0))
            B_actT.r.update(B_vaug2.r); B_actT.r.update(B_vaug2.w)

            tick_on = [True]
            bg_gen = [None]

            def bg_step(n):
                for _ in range(n):
                    if bg_gen[0] is not None:
                        try:
                            next(bg_gen[0])
                        except StopIteration:
                            bg_gen[0] = None

            def logf_section(tick):
                def dv(fn, **kw):
                    P.op("dve", fn, **kw)
                    if tick_on[0]:
                        tick()
                P.op("act", lambda e: e.activation(out=lfp[:], in_=zp[:], func=AF.Exp, scale=-1.0), r=(B_zp,), w=(B_lfp,))
                P.op("act", lambda e: e.activation(out=lfs[:], in_=zs[:], func=AF.Exp, scale=-1.0), r=(B_zs,), w=(B_lfs,))
                P.op("act", lambda e: e.activation(out=lfp[:], in_=lfp[:], func=AF.Ln, bias=1.0), r=(B_lfp,), w=(B_lfp,))
                P.op("act", lambda e: e.activation(out=lfs[:], in_=lfs[:], func=AF.Ln, bias=1.0), r=(B_lfs,), w=(B_lfs,))
                for l in range(NLC):
                    dv(lambda e, l=l: e.tensor_scalar(out=lfp[:, 4 * l:4 * l + 4, :], in0=lfp[:, 4 * l:4 * l + 4, :], scalar1=kmask[:, l:l + 1],
                                                              scalar2=None, op0=ALU.mult), r=(B_lfp, B_const), w=(B_lfp,))
                dv(lambda e: e.tensor_scalar(out=lfp[:], in0=lfp[:], scalar1=-1.0, scalar2=None, op0=ALU.mult), r=(B_lfp,), w=(B_lfp,))
                dv(lambda e: e.tensor_scalar(out=lfs[:], in0=lfs[:], scalar1=-1.0, scalar2=None, op0=ALU.mult), r=(B_lfs,), w=(B_lfs,))
                for i in range(4):
                    l = 2 * i + 1
                    P.dma("sp", lf_own[i].rearrange("(b p) h -> p b h", p=128), lfp[:, 4 * l:4 * l + 4, :], r=(B_lfp,), w=())
                P.dma("sp", lfs_o[:, :], lfs[0:64, :], r=(B_lfs,), w=())

                def cumsum_rows(LF, B_LF, NB, kc_list, maskcol):
                    b1 = ps_get(); b2 = ps_get()
                    lf2 = LF[:, 0:NB, :].rearrange("p b h -> p (b h)")
                    P.mm([lambda e: e.matmul(psf[b1][:, 0:NB * NH], lhsT=tri_f[:, :], rhs=lf2, start=True, stop=True)], r=(B_LF, B_const), w=(B_psf[b1],))
                    yield
                    P.mm([lambda e: e.matmul(psf[b2][:, 0:NB * NH], lhsT=ones_f[:, :], rhs=lf2, start=True, stop=True)], r=(B_LF, B_const), w=(B_psf[b2],))
                    yield
                    P.op("act", lambda e: e.copy(out=ctot[:, 0:NB, :].rearrange("p b h -> p (b h)"), in_=psf[b2][:, 0:NB * NH]), r=(B_psf[b2],), w=(B_ctot,))
                    yield
                    ps_free(b2)
                    dv(lambda e: e.memset(coff[:, 0, :], 0.0), w=(B_coff,))
                    yield
                    dv(lambda e: e.tensor_copy(out=coff[:, 1:NB, :], in_=ctot[:, 0:NB - 1, :]), r=(B_ctot,), w=(B_coff,))
                    yield
                    bufs = [(coff, B_coff), (ctot, B_ctot)]
                    for si, sh in enumerate([1, 2, 4, 8, 16, 32]):
                        (src, Bs), (dst, Bd) = bufs[si % 2], bufs[(si + 1) % 2]
                        m = min(sh, NB)
                        dv(lambda e, src=src, dst=dst, m=m: e.tensor_copy(out=dst[:, 0:m, :], in_=src[:, 0:m, :]), r=(Bs,), w=(Bd,))
                        yield
                        if sh < NB:
                            dv(lambda e, src=src, dst=dst, sh=sh: e.tensor_tensor(out=dst[:, sh:NB, :], in0=src[:, sh:NB, :], in1=src[:, 0:NB - sh, :], op=ALU.add),
                                 r=(Bs,), w=(Bd,))
                            yield
                    dv(lambda e: e.tensor_tensor(out=ctmp[:, 0:NB, :].rearrange("p b h -> p (b h)"), in0=psf[b1][:, 0:NB * NH],
                                                          in1=coff[:, 0:NB, :].rearrange("p b h -> p (b h)"), op=ALU.add), r=(B_psf[b1], B_coff), w=(B_ctmp,))
                    yield
                    ps_free(b1)
                    dv(lambda e: e.tensor_scalar(out=cres[:, 0:NB, :], in0=ctmp[:, 0:NB, :], scalar1=-1.0, scalar2=None, op0=ALU.mult),
                         r=(B_ctmp,), w=(B_cres,))
                    yield
                    if maskcol:
                        for l in range(NLC):
                            dv(lambda e, l=l: e.tensor_scalar(out=cres[:, 4 * l:4 * l + 4, :], in0=cres[:, 4 * l:4 * l + 4, :], scalar1=kmb[:, l:l + 1],
                                                                      scalar2=None, op0=ALU.add), r=(B_cres, B_const), w=(B_cres,))
                            yield
                    dv(lambda e: e.tensor_copy(out=spl[:, 0:NB, 0, :], in_=cres[:, 0:NB, :]), r=(B_cres,), w=(B_spl,))
                    yield
                    dv(lambda e: e.tensor_tensor(out=ctmp[:, 0:NB, :], in0=cres[:, 0:NB, :], in1=spl[:, 0:NB, 0, :], op=ALU.subtract),
                         r=(B_cres, B_spl), w=(B_ctmp,))
                    yield
                    dv(lambda e: e.tensor_copy(out=spl[:, 0:NB, 1, :], in_=ctmp[:, 0:NB, :]), r=(B_ctmp,), w=(B_spl,))
                    yield
                    dv(lambda e: e.tensor_tensor(out=cres[:, 0:NB, :], in0=ctmp[:, 0:NB, :], in1=spl[:, 0:NB, 1, :], op=ALU.subtract),
                         r=(B_ctmp, B_spl), w=(B_cres,))
                    yield
                    dv(lambda e: e.tensor_copy(out=spl[:, 0:NB, 2, :], in_=cres[:, 0:NB, :]), r=(B_cres,), w=(B_spl,))
                    yield
                    for ci, kc in enumerate(kc_list):
                        nbk = min(4, NB - 4 * ci)
                        ti = pst_get()
                        P.mm([lambda e, bi=bi, ci=ci, ti=ti: e.transpose(out=pst[ti][0:24, bi * 128:(bi + 1) * 128],
                                                                          in_=spl[:, 4 * ci + bi, :, :].rearrange("p j h -> p (j h)"), identity=ident_b[:, :])
                              for bi in range(nbk)], r=(B_spl, B_const), w=(B_pst[ti],))
                        yield
                        P.op("act", lambda e, ti=ti, nbk=nbk: e.copy(out=rowst[:, 0:nbk * 128], in_=pst[ti][0:24, 0:nbk * 128]), r=(B_pst[ti],), w=(B_rowst,))
                        yield
                        ncol = 512 if nbk == 4 else 32
                        P.dma("sp", KT[kc, 64:67, :, 0:ncol].rearrange("j h k -> (j h) k"), rowst[:, 0:ncol], r=(B_rowst,), w=(B_KTa[kc],))
                        yield

                for _ in cumsum_rows(lfp, B_lfp, 32, list(range(8)), True):
                    pass

                def sample_cs():
                    for s in range(2):
                        P.dma("sp", lfc[:, 0:32, :], clf[s].rearrange("(b p) h -> p b h", p=128), w=(B_lfc,))
                        dv(lambda e: e.memset(lfc[:, 32, :], 0.0), w=(B_lfc,))
                        dv(lambda e, s=s: e.tensor_copy(out=lfc[0:32, 32, :], in_=lfs[32 * s:32 * s + 32, :]), r=(B_lfs,), w=(B_lfc,))
                        yield
                        yield from cumsum_rows(lfc, B_lfc, 33, [8 + 9 * s + j for j in range(9)], False)
                tick_on[0] = False
                bg_gen[0] = sample_cs()

            def qaug(q0, nq, diag_kc):
                P.dma("sp", qT[67:70, :, q0:q0 + nq], KT[diag_kc, 64:67, :, 0:nq], r=(B_KTa[diag_kc],), w=(B_qTaug,))

            def finalize_a(bo, ncol):
                ob = fin_state[0]; fin_state[0] = 1 - ob
                P.op("act", lambda e: e.copy(out=oc[ob][:, 0:ncol], in_=psf[bo][0:65, 0:ncol]), r=(B_psf[bo],), w=(B_oc[ob],))
                ps_free(bo)
                P.op("dve", lambda e: e.reciprocal(out=oc[ob][64:65, 0:ncol], in_=oc[ob][64:65, 0:ncol]), r=(B_oc[ob],), w=(B_oc[ob],))
                return ob

            def finalize_b(ob, ncol, dst_fn):
                bb = ps_get()
                P.mm([lambda e: e.matmul(psf[bb][0:64, 0:ncol], lhsT=ones_f[64:65, 0:64], rhs=oc[ob][64:65, 0:ncol], start=True, stop=True)],
                     r=(B_oc[ob], B_const), w=(B_psf[bb],))
                P.op("act", lambda e: e.copy(out=rinv[:, 0:ncol], in_=psf[bb][0:64, 0:ncol]), r=(B_psf[bb],), w=(B_rinv,))
                ps_free(bb)
                dst_fn(oc[ob], B_oc[ob], bb)

            fin_state = [0]
            pt_state = [0]
            LOOK = 3

            def attention(nq, chunks, diag_kc, fillers, tail):
                allc = chunks + [diag_kc]
                qaug(0, nq, diag_kc)
                nfill = (len(fillers) + NH - 3) // (NH - 2)
                fpos = 0
                pending_fin = [None]
                tail_gen = [None]

                def flush_fin():
                    if pending_fin[0] is not None:
                        ob_, h_ = pending_fin[0]
                        pending_fin[0] = None

                        def dst(ocb, Bocb, bb, h_=h_):
                            P.op("dve", lambda e: e.tensor_tensor(out=aoT[:, h_, 0:nq], in0=ocb[0:64, 0:nq], in1=rinv[:, 0:nq], op=ALU.mult),
                                 r=(Bocb, B_rinv), w=(B_aoT,))
                        finalize_b(ob_, nq, dst)
                for h in range(NH):
                    bo = ps_get()
                    blocks = [(j, kb) for j in range(len(allc)) for kb in range(4)]
                    nb = len(blocks)
                    cur = {}
                    pend = {}
                    for idx in range(nb + LOOK):
                        if idx == 8:
                            flush_fin()
                        if idx < nb:
                            j, kb = blocks[idx]
                            kc = allc[j]
                            isdiag = (j == len(allc) - 1)
                            if kb == 0:
                                def loader(t, b, kc=kc, h=h):
                                    P.dma("sp", t[:, 0:512], KT[kc, :, h, :], r=(B_KT[kc], B_KTa[kc], B_KTn[kc]), w=(b,))

                                def vloader(t, b, kc=kc, h=h):
                                    P.dma("sp", t[:, :, :], VA[kc, :, h, :, :], r=(B_VA[kc],), w=(b,))
                                cur["k"] = KS.get(loader)
                                cur["v"] = VS.get(vloader)
                            kt, Bkt = cur["k"]
                            vt, Bvt = cur["v"]
                            c0 = kb * 128 if isdiag else 0
                            n = nq - c0
                            bs = ps_get()
                            P.mm([lambda e, kb=kb, c0=c0, n=n, bs=bs, kt=kt: e.matmul(psf[bs][:, 0:n], lhsT=kt[:, kb * 128:(kb + 1) * 128],
                                                                                  rhs=qT[:, h, c0:c0 + n], start=True, stop=True)],
                                 r=(Bkt, B_qT), w=(B_psf[bs],))
                            p = pt_state[0]; pt_state[0] = (p + 1) % NPT
                            P.op("act", lambda e, p=p, n=n, bs=bs: e.activation(out=pT[p][:, 0:n], in_=psf[bs][:, 0:n], func=AF.Exp),
                                 r=(B_psf[bs],), w=(B_pT[p],))
                            ps_free(bs)
                            if isdiag:
                                P.op("pool", lambda e, p=p: e.tensor_tensor(out=pT[p][:, 0:128], in0=pT[p][:, 0:128], in1=tri_b[:, :], op=ALU.mult),
                                     r=(B_pT[p], B_const), w=(B_pT[p],))
                            pend[idx] = (p, kb, c0, n, vt, Bvt)
                            if tail_gen[0] is not None:
                                next(tail_gen[0], None)
                        if idx >= LOOK:
                            i2 = idx - LOOK
                            p, kb, c0, n, vt, Bvt = pend.pop(i2)
                            P.mm([lambda e, p=p, kb=kb, c0=c0, n=n, vt=vt, i2=i2: e.matmul(
                                psf[bo][0:65, c0:c0 + n], lhsT=vt[:, kb, :], rhs=pT[p][:, 0:n], start=(i2 == 0), stop=(i2 == nb - 1))],
                                 r=(B_pT[p], Bvt), w=(B_psf[bo],))

                    flush_fin()
                    pending_fin[0] = (finalize_a(bo, nq), h)
                    for f in fillers[fpos:fpos + nfill]:
                        f()
                    fpos += nfill
                    if h == NH - 2:
                        for f in fillers[fpos:]:
                            f()
                        fpos = len(fillers)
                        tail_gen[0] = tail()
                flush_fin()
                if tail_gen[0] is not None:
                    for _ in tail_gen[0]:
                        pass

            def attention_sample(s, fillers=None):
                q0 = 32 * s
                kcs = [8 + 9 * s + j for j in range(9)]
                qaug(q0, 32, kcs[8])
                bo = ps_get()
                P.mm([lambda e: e.matmul(psf[bo][0:65, 0:256], lhsT=zrow[0:1, 0:65], rhs=zrow[0:1, 0:256], start=True, stop=False)],
                     r=(B_const,), w=(B_psf[bo],))
                blocks = [(j, kb) for j in range(8) for kb in range(4)] + [(8, 0)]
                nb = len(blocks)
                cur = {}
                pend = {}
                for idx in range(nb + LOOK):
                    if idx < nb:
                        j, kb = blocks[idx]
                        kc = kcs[j]
                        isdiag = (j == 8)
                        kw = 32 if isdiag else 128
                        if kb == 0:
                            def loader(t, b, kc=kc, isdiag=isdiag):
                                if isdiag:
                                    P.dma("sp", t[0:70, :, 0:32], KT[kc, :, :, 0:32], r=(B_KT[kc], B_KTa[kc], B_KTn[kc]), w=(b,))
                                else:
                                    P.dma("sp", t[0:70, :, :], KT[kc], r=(B_KT[kc], B_KTa[kc], B_KTn[kc]), w=(b,))

                            def vloader(t, b, kc=kc, isdiag=isdiag):
                                if isdiag:
                                    P.dma("sp", t[0:32, :, 0:65], VA[kc, 0:32, :, 0, :], r=(B_VA[kc],), w=(b,))
                                else:
                                    P.dma("sp", t[:, :, 0:260], VA[kc].rearrange("p h b d -> p h (b d)"), r=(B_VA[kc],), w=(b,))
                            cur["k"] = WS.get(loader)
                            cur["v"] = WS.get(vloader)
                        kt, Bkt = cur["k"]
                        vt, Bvt = cur["v"]
                        bs = ps_get()
                        P.mm([lambda e, h=h, kb=kb, kw=kw, bs=bs, kt=kt: e.matmul(psf[bs][0:kw, h * 32:(h + 1) * 32], lhsT=kt[0:70, h, kb * 128:kb * 128 + kw],
                                                                             rhs=qT[:, h, q0:q0 + 32], start=True, stop=True) for h in range(NH)],
                             r=(Bkt, B_qT), w=(B_psf[bs],))
                        p = pt_state[0]; pt_state[0] = (p + 1) % NPT
                        P.op("act", lambda e, p=p, kw=kw, bs=bs: e.activation(out=pT[p][0:kw, 0:256], in_=psf[bs][0:kw, 0:256], func=AF.Exp),
                             r=(B_psf[bs],), w=(B_pT[p],))
                        ps_free(bs)
                        if isdiag:
                            P.op("dve", lambda e, p=p: e.tensor_tensor(out=pT[p][0:32, 0:256].rearrange("p (h q) -> p h q", h=NH),
                                                                        in0=pT[p][0:32, 0:256].rearrange("p (h q) -> p h q", h=NH),
                                                                        in1=tri_b[0:32, 0:32].unsqueeze(1).to_broadcast([32, NH, 32]), op=ALU.mult),
                                 r=(B_pT[p], B_const), w=(B_pT[p],))
                        pend[idx] = (p, kb, kw, vt, Bvt)
                        if fillers:
                            for _ in range(4):
                                if fillers:
                                    fillers.pop(0)()
                    if idx >= LOOK:
                        i2 = idx - LOOK
                        p, kb, kw, vt, Bvt = pend.pop(i2)
                        P.mm([lambda e, h=h, p=p, kb=kb, kw=kw, vt=vt, i2=i2: e.matmul(
                            psf[bo][0:65, h * 32:(h + 1) * 32], lhsT=vt[0:kw, h, kb * 65:(kb + 1) * 65], rhs=pT[p][0:kw, h * 32:(h + 1) * 32],
                            start=False, stop=(i2 == nb - 1 and h == NH - 1)) for h in range(NH)],
                             r=(B_pT[p], Bvt), w=(B_psf[bo],))

                def dst(ocb, Bocb, bb):
                    P.op("dve", lambda e: e.tensor_tensor(out=aoT[:, :, q0:q0 + 32], in0=ocb[0:64, 0:256].rearrange("p (h q) -> p h q", h=NH),
                                                          in1=rinv[:, 0:256].rearrange("p (h q) -> p h q", h=NH), op=ALU.mult),
                         r=(Bocb, B_rinv), w=(B_aoT,))
                finalize_b(finalize_a(bo, 256), 256, dst)

            def tile2_load(ti, is_sample, l):
                xtile = xt[ti % 2]; Bx = B_xt[ti % 2]
                if not is_sample:
                    for bi in range(4):
                        P.dma("sp", xtile[:, bi, :], xloc[l, bi * 128:(bi + 1) * 128, :], w=(Bx,))
                    P.dma("sp", xhalo[:, :], xloc[l - 1, 480:512, :], w=(B_xhalo,))
                else:
                    P.dma("sp", xtile[0:64, 0, :], xs[:, :], w=(Bx,))

            pre_stats = [False]
            pre_apply = [False]

            def norm1_blocks(ti, is_sample):
                xtile = xt[ti % 2]
                if not is_sample:
                    return [(xhalo[:, :], 32, 0)] + [(xtile[:, bi, :], 128, 32 + bi * 128) for bi in range(4)], (B_xhalo, B_xt[ti % 2])
                return [(xtile[0:64, 0, :], 64, 0)], (B_xt[ti % 2],)

            def tile2(ti, is_sample, l, own_i, nxt):
                NT = 64 if is_sample else 512
                xtile = xt[ti % 2]; Bx = B_xt[ti % 2]
                n1b, n1r = norm1_blocks(ti, is_sample)
                if not pre_stats[0]:
                    norm_stats(n1b, n1r, stat2, B_stat2)
                pre_stats[0] = False
                if not pre_apply[0]:
                    norm_apply(n1b, g1c, xnT, B_xnT, n1r, stat2, B_stat2)
                pre_apply[0] = False
                if not is_sample:
                    tblocks = [(128, bi * 128) for bi in range(4)]
                    xc0 = 32; glu_n = 544; glu_c0 = 0
                    segs = [(32, 512)]
                    cu_dst0 = 0
                else:
                    tblocks = [(64, 0)]
                    xc0 = 0; glu_n = 64; glu_c0 = 0
                    segs = [(32, 32), (96, 32)]
                    P.op("dve", lambda e: e.memset(cuT[:, :, 0:128], 0.0), w=(B_cuT,))
                    for s in range(2):
                        P.dma("sp", schist[0:30, :], sconv[s], w=(B_schist,))
                        for cc in range(4):
                            bk = ps_get()
                            P.mm([lambda e, cc=cc, bk=bk: e.matmul(psf[bk][:, 0:30], lhsT=schist[0:30, cc * 128:(cc + 1) * 128], rhs=ident_f[0:30, 0:30],
                                                                  start=True, stop=True)], r=(B_schist, B_const), w=(B_psf[bk],))
                            P.op("act", lambda e, cc=cc, bk=bk, s=s: e.copy(out=cuT[:, cc, 64 * s + 2:64 * s + 32], in_=psf[bk][:, 0:30]), r=(B_psf[bk],), w=(B_cuT,))
                            ps_free(bk)

                wa, Bwa = wload(w_in, 0, 8, 0, 512)
                wg, Bwg = wload(w_in, 0, 8, 512, 512)
                for cc in range(4):
                    col = 0
                    while col < glu_n:
                        n = min(512, glu_n - col)
                        ba = ps_get(); bg = ps_get()
                        mm_fm(wg, Bwg, 8, cc * 128, 128, xnT, B_xnT, glu_c0 + col, n, bg)
                        mm_fm(wa, Bwa, 8, cc * 128, 128, xnT, B_xnT, glu_c0 + col, n, ba)
                        P.op("act", lambda e, bg=bg, n=n: e.activation(out=tg[:, 0:n], in_=psf[bg][:, 0:n], func=AF.Tanh, scale=0.5), r=(B_psf[bg],), w=(B_tg,))
                        ps_free(bg)
                        if not is_sample:
                            dsts = [(cuT[:, cc, col:col + n], 0, n)]
                        else:
                            dsts = [(cuT[:, cc, 32:64], 0, 32), (cuT[:, cc, 96:128], 32, 32)]
                        for dst, s0, sn in dsts:
                            P.op("dve", lambda e, dst=dst, s0=s0, sn=sn, ba=ba: e.scalar_tensor_tensor(out=dst, in0=tg[:, s0:s0 + sn], scalar=1.0, in1=psf[ba][:, s0:s0 + sn],
                                                                                                  op0=ALU.add, op1=ALU.mult), r=(B_tg, B_psf[ba]), w=(B_cuT,))
                        ps_free(ba)
                        col += n
                if not is_sample:
                    P.op("act", lambda e: e.activation(out=cuT[:, :, 0:544], in_=cuT[:, :, 0:544], func=AF.Copy, scale=0.5), r=(B_cuT,), w=(B_cuT,))
                else:
                    for s in range(2):
                        P.op("act", lambda e, s=s: e.activation(out=cuT[:, :, 64 * s + 32:64 * s + 64], in_=cuT[:, :, 64 * s + 32:64 * s + 64], func=AF.Copy, scale=0.5),
                             r=(B_cuT,), w=(B_cuT,))
                def hist_out(colbase, dst):
                    for cc in range(4):
                        bk = ps_get()
                        P.mm([lambda e, cc=cc, bk=bk: e.matmul(psf[bk][0:30, 0:128], lhsT=cuT[:, cc, colbase:colbase + 30], rhs=ident_f[:, :], start=True, stop=True)],
                             r=(B_cuT, B_const), w=(B_psf[bk],))
                        P.op("act", lambda e, cc=cc, bk=bk: e.copy(out=chst[0:30, cc * 128:(cc + 1) * 128], in_=psf[bk][0:30, 0:128]), r=(B_psf[bk],), w=(B_chst,))
                        ps_free(bk)
                    P.dma("sp", dst, chst[0:30, :], r=(B_chst,), w=())
                if is_sample:
                    for s in range(2):
                        hist_out(64 * s + 34, convs_o[s])
                elif own_i == 3:
                    hist_out(514, conv_own[:, :])

                wq, Bwq = wload(w_in, 0, 8, 1024, 512)
                for g in range(4):
                    bk = ps_get()
                    mm_fm(wq, Bwq, 8, g * 128, 128, xnT, B_xnT, xc0, NT, bk)
                    P.op("act", lambda e, g=g, bk=bk: e.activation(out=qT[0:64, 2 * g, 0:NT], in_=psf[bk][0:64, 0:NT], func=AF.Copy, scale=0.125), r=(B_psf[bk],), w=(B_qT,))
                    P.op("dve", lambda e, g=g, bk=bk: e.tensor_scalar(out=qT[0:64, 2 * g + 1, 0:NT], in0=psf[bk][64:128, 0:NT], scalar1=0.125, scalar2=None, op0=ALU.mult),
                         r=(B_psf[bk],), w=(B_qT,))
                    ps_free(bk)
                P.op("dve", lambda e: e.memset(qT[64:67, :, 0:NT], 1.0), w=(B_qT,))

                fillers = []
                for cc in range(4):
                    for (o0, n) in segs:
                        d0 = o0 - 32 if not is_sample else (0 if o0 == 32 else 32)
                        for j in range(CW):
                            src = cuT[:, cc, o0 - 30 + j:o0 - 30 + j + n]
                            if j == 0:
                                fillers.append(lambda src=src, cc=cc, d0=d0, n=n: P.op(
                                    "dve", lambda e: e.tensor_scalar(out=cacc[:, cc, d0:d0 + n], in0=src, scalar1=wdw[:, cc, 0:1],
                                                                     scalar2=bdw[:, cc:cc + 1], op0=ALU.mult, op1=ALU.add),
                                    r=(B_cuT, B_const), w=(B_cacc,)))
                            else:
                                fillers.append(lambda src=src, cc=cc, d0=d0, n=n, j=j: P.op(
                                    "dve", lambda e: e.scalar_tensor_tensor(out=cacc[:, cc, d0:d0 + n], in0=src, scalar=wdw[:, cc, j:j + 1],
                                                                            in1=cacc[:, cc, d0:d0 + n], op0=ALU.mult, op1=ALU.add),
                                    r=(B_cuT, B_const, B_cacc), w=(B_cacc,)))
                def ln_block():
                    b1 = ps_get(); b2 = ps_get()
                    P.mm([lambda e, cc=cc: e.matmul(psf[b1][:, 0:NT], lhsT=ones_f[:, :], rhs=cacc[:, cc, 0:NT], start=(cc == 0), stop=(cc == 3)) for cc in range(4)],
                         r=(B_cacc, B_const), w=(B_psf[b1],))
                    yield
                    for cc in range(4):
                        P.op("act", lambda e, cc=cc: e.activation(out=csq[:, 0:NT], in_=cacc[:, cc, 0:NT], func=AF.Square), r=(B_cacc,), w=(B_csq,))
                        yield
                        P.mm([lambda e, cc=cc: e.matmul(psf[b2][:, 0:NT], lhsT=ones_f[:, :], rhs=csq[:, 0:NT], start=(cc == 0), stop=(cc == 3))],
                             r=(B_csq, B_const), w=(B_psf[b2],))
                        yield
                    P.op("act", lambda e: e.activation(out=lnm[:, 0:NT], in_=psf[b1][:, 0:NT], func=AF.Copy, scale=1.0 / 512), r=(B_psf[b1],), w=(B_lnm,))
                    yield
                    ps_free(b1)
                    P.op("dve", lambda e: e.tensor_tensor(out=lnr[:, 0:NT], in0=lnm[:, 0:NT], in1=lnm[:, 0:NT], op=ALU.mult), r=(B_lnm,), w=(B_lnr,))
                    yield
                    P.op("dve", lambda e: e.scalar_tensor_tensor(out=lnr[:, 0:NT], in0=psf[b2][:, 0:NT], scalar=1.0 / 512, in1=lnr[:, 0:NT], op0=ALU.mult, op1=ALU.subtract),
                         r=(B_psf[b2], B_lnr), w=(B_lnr,))
                    yield
                    ps_free(b2)
                    P.op("dve", lambda e: e.tensor_scalar(out=lnr[:, 0:NT], in0=lnr[:, 0:NT], scalar1=1e-5, scalar2=None, op0=ALU.add), r=(B_lnr,), w=(B_lnr,))
                    yield
                    P.op("act", lambda e: e.activation(out=lnr[:, 0:NT], in_=lnr[:, 0:NT], func=AF.Sqrt), r=(B_lnr,), w=(B_lnr,))
                    yield
                    P.op("dve", lambda e: e.reciprocal(out=lnr[:, 0:NT], in_=lnr[:, 0:NT]), r=(B_lnr,), w=(B_lnr,))
                    yield
                    for cc in range(4):
                        P.op("dve", lambda e, cc=cc: e.tensor_tensor(out=cacc[:, cc, 0:NT], in0=cacc[:, cc, 0:NT], in1=lnm[:, 0:NT], op=ALU.subtract),
                             r=(B_cacc, B_lnm), w=(B_cacc,))
                        yield
                        P.op("dve", lambda e, cc=cc: e.tensor_tensor(out=cacc[:, cc, 0:NT], in0=cacc[:, cc, 0:NT], in1=lnr[:, 0:NT], op=ALU.mult),
                             r=(B_cacc, B_lnr), w=(B_cacc,))
                        yield
                        P.op("act", lambda e, cc=cc: e.activation(out=cacc[:, cc, 0:NT], in_=cacc[:, cc, 0:NT], func=AF.Identity, scale=lngh[:, cc:cc + 1], bias=lnbh[:, cc:cc + 1]),
                             r=(B_cacc, B_const), w=(B_cacc,))
                        yield
                        P.op("act", lambda e, cc=cc: e.activation(out=csq[:, 0:NT], in_=cacc[:, cc, 0:NT], func=AF.Tanh), r=(B_cacc,), w=(B_csq,))
                        yield
                        P.op("dve", lambda e, cc=cc: e.scalar_tensor_tensor(out=sT[:, cc, 0:NT], in0=csq[:, 0:NT], scalar=1.0, in1=cacc[:, cc, 0:NT], op0=ALU.add, op1=ALU.mult),
                             r=(B_csq, B_cacc), w=(B_sT,))
                        yield


                yield fillers
                if not is_sample:
                    attention(512, list(range(l)), l, fillers, ln_block)
                else:
                    for s in range(2):
                        attention_sample(s, fillers)
                    while fillers:
                        fillers.pop(0)()
                    for _ in ln_block():
                        pass

                if nxt is not None:
                    tile2_load(*nxt)

                def wao_load(c0):
                    src = w_ao[:, c0:c0 + 512].rearrange("(h p) n -> p h n", p=64)
                    return wload_generic(("w_ao", c0), [64, 8, 512], src, 64)
                wpw = {}; wgc = {}; wao = {}; wga = {}
                for half in range(2):
                    wgc[half] = wload(w_in, 0, 8, 2568 + half * 512, 512)
                    wpw[half] = wload(w_pw, 0, 4, half * 512, 512)
                    wga[half] = wload(w_in, 0, 8, 3592 + half * 512, 512)
                    wao[half] = wao_load(half * 512)
                    for f4 in range(4):
                        fc = half * 4 + f4
                        bg = ps_get()
                        mm_fm(wgc[half][0], wgc[half][1], 8, f4 * 128, 128, xnT, B_xnT, xc0, NT, bg)
                        P.op("act", lambda e, bg=bg: e.activation(out=thc[:, 0:NT], in_=psf[bg][:, 0:NT], func=AF.Tanh, scale=0.5), r=(B_psf[bg],), w=(B_thc,))
                        ps_free(bg)
                        by = ps_get()
                        mm_fm(wpw[half][0], wpw[half][1], 4, f4 * 128, 128, sT, B_sT, 0, NT, by)
                        P.op("dve", lambda e, by=by: e.scalar_tensor_tensor(out=m1[:, 0:NT], in0=thc[:, 0:NT], scalar=1.0, in1=psf[by][:, 0:NT], op0=ALU.add, op1=ALU.mult),
                             r=(B_thc, B_psf[by]), w=(B_m1,))
                        ps_free(by)
                        bg = ps_get()
                        mm_fm(wga[half][0], wga[half][1], 8, f4 * 128, 128, xnT, B_xnT, xc0, NT, bg)
                        P.op("act", lambda e, bg=bg: e.activation(out=thc[:, 0:NT], in_=psf[bg][:, 0:NT], func=AF.Tanh, scale=0.5), r=(B_psf[bg],), w=(B_thc,))
                        ps_free(bg)
                        by = ps_get()
                        wt, Bwt = wao[half]
                        P.mm([lambda e, h=h, by=by, wt=wt, f4=f4: e.matmul(psf[by][:, 0:NT], lhsT=wt[0:64, h, f4 * 128:(f4 + 1) * 128], rhs=aoT[:, h, 0:NT],
                                                                           start=(h == 0), stop=(h == NH - 1)) for h in range(NH)], r=(Bwt, B_aoT), w=(B_psf[by],))
                        P.op("dve", lambda e, by=by: e.scalar_tensor_tensor(out=thc[:, 0:NT], in0=thc[:, 0:NT], scalar=1.0, in1=psf[by][:, 0:NT], op0=ALU.add, op1=ALU.mult),
                             r=(B_thc, B_psf[by]), w=(B_thc,))
                        ps_free(by)
                        P.op("dve", lambda e, fc=fc: e.tensor_tensor(out=mixT[:, fc, 0:NT], in0=m1[:, 0:NT], in1=thc[:, 0:NT], op=ALU.add), r=(B_m1, B_thc), w=(B_mixT,))

                for half in range(2):
                    wo_t, wo_b = wload(w_out, 0, 8, half * 512, 512)
                    for bi, (npp, c0) in enumerate(tblocks):
                        bk = ps_get()
                        mm_tm(mixT, B_mixT, 8, c0, npp, wo_t, wo_b, 0, 512, bk)
                        P.op("dve", lambda e, bk=bk, bi=bi, npp=npp, half=half: e.scalar_tensor_tensor(out=xtile[0:npp, bi, half * 512:(half + 1) * 512], in0=psf[bk][0:npp, :], scalar=0.5,
                                                                                                  in1=xtile[0:npp, bi, half * 512:(half + 1) * 512], op0=ALU.mult, op1=ALU.add),
                             r=(B_psf[bk], Bx), w=(Bx,))
                        ps_free(bk)
                norm_T([(xtile[0:npp, bi, :], npp, c0) for bi, (npp, c0) in enumerate(tblocks)], g2c, zT, B_zT, extra_r=(Bx,))
                for c5 in range(6):
                    ncols = 512 if c5 < 5 else 256
                    wgt, Bwgt = wload(w_gate, 0, 8, c5 * 512, ncols, depth=3)
                    wut, Bwut = wload(w_up, 0, 8, c5 * 512, ncols, depth=3)
                    for f4 in range(ncols // 128):
                        fc = c5 * 4 + f4
                        bg = ps_get(); bu = ps_get()
                        mm_fm(wgt, Bwgt, 8, f4 * 128, 128, zT, B_zT, 0, NT, bg)
                        mm_fm(wut, Bwut, 8, f4 * 128, 128, zT, B_zT, 0, NT, bu)
                        P.op("act", lambda e, bg=bg: e.activation(out=thc[:, 0:NT], in_=psf[bg][:, 0:NT], func=AF.Tanh, scale=0.5), r=(B_psf[bg],), w=(B_thc,))
                        P.op("dve", lambda e, bg=bg: e.scalar_tensor_tensor(out=m1[:, 0:NT], in0=thc[:, 0:NT], scalar=1.0, in1=psf[bg][:, 0:NT], op0=ALU.add, op1=ALU.mult),
                             r=(B_thc, B_psf[bg]), w=(B_m1,))
                        ps_free(bg)
                        P.op("dve", lambda e, bu=bu, fc=fc: e.scalar_tensor_tensor(out=actT[:, fc, 0:NT], in0=m1[:, 0:NT], scalar=0.5, in1=psf[bu][:, 0:NT], op0=ALU.mult, op1=ALU.mult),
                             r=(B_m1, B_psf[bu]), w=(B_actT,))
                        ps_free(bu)
                        bg_step(3)
                if nxt is not None:
                    nb_, nr_ = norm1_blocks(nxt[0], nxt[1])
                    norm_stats(nb_, nr_, stat2, B_stat2)
                    pre_stats[0] = True
                kgroups = [(0, 8), (8, 8), (16, 6)]
                for half in range(2):
                    banks = [ps_get() for _ in tblocks]
                    for gi, (k0, nk) in enumerate(kgroups):
                        wd, Bwd = wload(w_down, k0, nk, half * 512, 512, depth=3)
                        for bi, (npp, c0) in enumerate(tblocks):
                            P.mm([lambda e, kc=kc, bi=bi, npp=npp, c0=c0, k0=k0, nk=nk, gi=gi, wd=wd: e.matmul(
                                psf[banks[bi]][0:npp, :], lhsT=actT[:, k0 + kc, c0:c0 + npp], rhs=wd[:, kc, 0:512],
                                start=(gi == 0 and kc == 0), stop=(gi == 2 and kc == nk - 1)) for kc in range(nk)],
                                 r=(Bwd, B_actT), w=(B_psf[banks[bi]],))
                    for bi, (npp, c0) in enumerate(tblocks):
                        bk = banks[bi]
                        P.op("dve", lambda e, bk=bk, bi=bi, npp=npp, half=half: e.tensor_tensor(out=xtile[0:npp, bi, half * 512:(half + 1) * 512], in0=psf[bk][0:npp, :],
                                                                                               in1=xtile[0:npp, bi, half * 512:(half + 1) * 512], op=ALU.add),
                             r=(B_psf[bk], Bx), w=(Bx,))
                        ps_free(bk)
                if nxt is not None:
                    nb_, nr_ = norm1_blocks(nxt[0], nxt[1])
                    norm_apply(nb_, g1c, xnT, B_xnT, nr_, stat2, B_stat2)
                    pre_apply[0] = True
                P.op("dve", lambda e: e.memset(stat[:, 0:8], 0.0), w=(B_stat,))
                for bi, (npp, c0) in enumerate(tblocks):
                    P.op("act", lambda e, bi=bi, npp=npp: e.activation(out=junk[0:npp, :], in_=xtile[0:npp, bi, :], func=AF.Square, accum_out=stat[0:npp, bi:bi + 1]),
                         r=(Bx,), w=(B_junk, B_stat))
                rstd_from_ss(len(tblocks), D, 1e-6)
                for bi, (npp, c0) in enumerate(tblocks):
                    for hf, (yb, B_yb) in enumerate(((lnr, B_lnr), (kvout, B_kvout))):
                        P.op("dve", lambda e, bi=bi, npp=npp, hf=hf, yb=yb: e.scalar_tensor_tensor(
                            out=yb[0:npp, :], in0=xtile[0:npp, bi, hf * 512:(hf + 1) * 512], scalar=stat[0:npp, 16 + bi:17 + bi],
                            in1=gfin[0:npp, hf * 512:(hf + 1) * 512], op0=ALU.mult, op1=ALU.mult), r=(Bx, B_stat, B_const), w=(B_yb,))
                        if is_sample:
                            P.dma("sp", ys_o[:, hf * 512:(hf + 1) * 512], yb[0:64, :], r=(B_yb,), w=())
                        else:
                            P.dma("sp", y_own[own_i, bi * 128:(bi + 1) * 128, hf * 512:(hf + 1) * 512], yb[:, :], r=(B_yb,), w=())

            tile2_load(0, False, 1)
            g0 = tile2(0, False, 1, 0, (1, False, 3))
            fl0 = next(g0)

            def tick():
                for _ in range(2):
                    if fl0:
                        fl0.pop(0)()
            logf_section(tick)
            for _ in g0:
                pass
            for i in range(1, 4):
                nxt = (i + 1, False, 2 * i + 3) if i < 3 else (4, True, None)
                for _ in tile2(i, False, 2 * i + 1, i, nxt):
                    pass
            bg_step(100000)
            for _ in tile2(4, True, None, None, None):
                pass
            if not P.dry:
                P.finish()

        P.dry = True
        program()
        P.reset()
        psstate.update({"free": [True] * 6, "nxt": 0, "tn": 0})
        WS.start_real(); KS.start_real(); VS.start_real(); wl["n"] = 0
        program()
    return nc


_NC_CACHE = {}


def kernel(x_prompt, x_sample, cache_k, cache_v, cache_logf, state_conv, norm_mix_g, w_in, b_f, w_dw, b_dw, ln_g, ln_b,
           w_conv_pw, w_attn_o, w_out, norm_ffn_g, w_gate, w_up, w_down, final_norm_g):
    f = lambda a: np.ascontiguousarray(np.asarray(a, dtype=np.float32))
    x_prompt = f(x_prompt); x_sample = f(x_sample)
    cache_k = f(cache_k)[0].reshape(16, 4096, 512); cache_v = f(cache_v)[0].reshape(16, 4096, 512)
    cache_logf = f(cache_logf)[0]; state_conv = f(state_conv)[0]
    col = lambda v, n: np.ascontiguousarray(f(v).reshape(n, 128).T)
    shared = {
        "w_in": f(w_in)[0], "w_pw": f(w_conv_pw)[0], "w_ao": f(w_attn_o)[0], "w_out": f(w_out)[0],
        "w_gate": f(w_gate)[0], "w_up": f(w_up)[0], "w_down": f(w_down)[0],
        "g1c": col(norm_mix_g, 8), "g2c": col(norm_ffn_g, 8),
        "gfin": np.ascontiguousarray(np.broadcast_to(f(final_norm_g).reshape(1, D), (128, D))),
        "bfb": np.ascontiguousarray(np.broadcast_to(f(b_f).reshape(1, NH), (128, NH))),
        "wdw": np.ascontiguousarray(f(w_dw)[0].reshape(CW, 4, 128).transpose(2, 1, 0)),
        "bdw": col(b_dw, 4), "lng": col(ln_g, 4), "lnb": col(ln_b, 4),
        "ident": np.eye(128, dtype=np.float32), "tri": np.triu(np.ones((128, 128), np.float32)),
    }
    in_maps = []
    for c in range(8):
        b, p = c // 2, c % 2
        xc = x_prompt[b].reshape(8, 512, D)
        if p == 1:
            xl = xc
            km = np.ones((128, NLC), np.float32)
        else:
            xl = np.concatenate([np.zeros((1, 512, D), np.float32), xc[:7]], axis=0)
            km = np.ones((128, NLC), np.float32); km[:, 0] = 0.0
        m = dict(shared)
        m.update({"xloc": np.ascontiguousarray(xl), "xs": np.ascontiguousarray(x_sample[2 * c:2 * c + 2].reshape(64, D)),
                  "ck": np.ascontiguousarray(cache_k[2 * c:2 * c + 2]), "cv": np.ascontiguousarray(cache_v[2 * c:2 * c + 2]),
                  "clf": np.ascontiguousarray(cache_logf[2 * c:2 * c + 2]), "sconv": np.ascontiguousarray(state_conv[2 * c:2 * c + 2]),
                  "kmask": km})
        in_maps.append(m)
    if "nc" not in _NC_CACHE:
        _NC_CACHE["nc"] = build_nc()
    res = run_bass_kernel_spmd(_NC_CACHE["nc"], in_maps, core_ids=list(range(8)))
    R = res.results
    y_p = np.zeros((4, 4096, D), np.float32); k_p = np.zeros((1, 4, 4096, NH, HD), np.float32); v_p = np.zeros_like(k_p)
    lf_p = np.zeros((1, 4, 4096, NH), np.float32); cv_p = np.zeros((1, 4, 30, 512), np.float32)
    y_s = np.zeros((16, 32, D), np.float32); k_s = np.zeros((1, 16, 32, NH, HD), np.float32); v_s = np.zeros_like(k_s)
    lf_s = np.zeros((1, 16, 32, NH), np.float32); cv_s = np.zeros((1, 16, 30, 512), np.float32)
    for c in range(8):
        b, p = c // 2, c % 2
        r = R[c]
        for i in range(4):
            gch = 2 * i + 1 - (1 - p)
            sl = slice(gch * 512, (gch + 1) * 512)
            y_p[b, sl] = r["y_own"][i]
            k_p[0, b, sl] = r["k_own"][i].reshape(512, NH, HD)
            v_p[0, b, sl] = r["v_own"][i].reshape(512, NH, HD)
            lf_p[0, b, sl] = r["lf_own"][i]
        if p == 1:
            cv_p[0, b] = r["conv_own"]
        y_s[2 * c:2 * c + 2] = r["ys"].reshape(2, 32, D)
        k_s[0, 2 * c:2 * c + 2] = r["ks"].reshape(2, 32, NH, HD)
        v_s[0, 2 * c:2 * c + 2] = r["vs"].reshape(2, 32, NH, HD)
        lf_s[0, 2 * c:2 * c + 2] = r["lfs"].reshape(2, 32, NH)
        cv_s[0, 2 * c:2 * c + 2] = r["convs"]
    return (y_p, y_s, k_p, v_p, lf_p, cv_p, k_s, v_s, lf_s, cv_s)
```

```python
import numpy as np
from contextlib import ExitStack
import concourse.bass as bass
import concourse.mybir as mybir
from concourse.bass_utils import run_bass_kernel_spmd

F32 = mybir.dt.float32
BF16 = mybir.dt.bfloat16
AF = mybir.ActivationFunctionType
ALU = mybir.AluOpType

D = 1024; DIN = 4616; DFF = 2816; NH = 8; HD = 64; CW = 31
NLC = 8
NKC = 8 + 2 * 9
NDS = 64
NEGBIG = -30000.0


class Buf:
    __slots__ = ("w", "r")

    def __init__(self):
        self.w = {}
        self.r = {}


class Prog:
    def __init__(self, nc, es):
        self.nc = nc
        self.eng = {"pe": nc.tensor, "act": nc.scalar, "dve": nc.vector, "pool": nc.gpsimd, "sp": nc.sync}
        self.sem = {k: es.enter_context(nc.semaphore("s_" + k)) for k in self.eng}
        self.dsem = [es.enter_context(nc.semaphore("d%d" % i)) for i in range(NDS)]
        self.qrange = {"sp": (0, NDS - 16), "pool": (NDS - 16, NDS)}
        self.reset()

    def reset(self):
        self.cnt = {k: 0 for k in self.eng}
        self.seen = {k: {} for k in self.eng}
        self.dval = [0] * NDS
        self.dlast = [None] * NDS
        self.dnext = {"sp": 0, "pool": NDS - 16}
        self.dry = False
        self.nwait = 0

    def _wait(self, e, key, val):
        if key == ("e", "pe") and e == "pe":
            return
        s = self.seen[e]
        if s.get(key, 0) >= val:
            return
        s[key] = val
        sem = self.sem[key[1]] if key[0] == "e" else self.dsem[key[1]]
        self.eng[e].wait_ge(sem, val)
        self.nwait += 1

    def _deps(self, e, r, w):
        for b in r:
            for k, v in b.w.items():
                self._wait(e, k, v)
        for b in w:
            for k, v in b.w.items():
                self._wait(e, k, v)
            for k, v in b.r.items():
                self._wait(e, k, v)

    def _upd(self, key, val, r, w):
        for b in r:
            if b.r.get(key, 0) < val:
                b.r[key] = val
        for b in w:
            b.w = {key: val}
            b.r = {}

    def op(self, e, fn, r=(), w=()):
        if self.dry:
            return
        self._deps(e, r, w)
        ins = fn(self.eng[e])
        self.cnt[e] += 1
        ins.then_inc(self.sem[e], 1)
        self._upd(("e", e), self.cnt[e], r, w)

    def mm(self, fns, r=(), w=()):
        if self.dry:
            return
        self._deps("pe", r, w)
        ins = None
        for fn in fns:
            ins = fn(self.eng["pe"])
        self.cnt["pe"] += 1
        ins.then_inc(self.sem["pe"], 1)
        self._upd(("e", "pe"), self.cnt["pe"], r, w)

    def dma(self, q, out, in_, r=(), w=()):
        if self.dry:
            return
        for b in r:
            for k, v in b.w.items():
                self._wait(q, k, v)
        for b in w:
            for k, v in b.w.items():
                if k[0] != "d":
                    self._wait(q, k, v)
            for k, v in b.r.items():
                self._wait(q, k, v)
        i = self.dnext[q]
        lo, hi = self.qrange[q]
        self.dnext[q] = lo + (i + 1 - lo) % (hi - lo)
        if self.dlast[i] is not None:
            self._wait(q, ("d", i), self.dlast[i])
        ins = self.eng[q].dma_start(out=out, in_=in_)
        self.dval[i] += 16
        ins.then_inc(self.dsem[i], 16)
        self.dlast[i] = self.dval[i]
        key = ("d", i)
        for b in r:
            if b.r.get(key, 0) < self.dval[i]:
                b.r[key] = self.dval[i]
        for b in w:
            keep = {k: v for k, v in b.w.items() if k[0] == "d"}
            keep[key] = self.dval[i]
            b.w = keep
            b.r = {}

    def finish(self):
        for i in range(NDS):
            if self.dlast[i] is not None:
                self._wait("sp", ("d", i), self.dlast[i])


class Stream:
    def __init__(self, P, slots, depth):
        self.P = P
        self.slots = slots
        self.depth = depth
        self.rec = []
        self.issued = 0
        self.req = 0

    def start_real(self):
        self.issued = 0
        self.req = 0

    def get(self, loader, depth=None):
        depth = self.depth if depth is None else depth
        P = self.P
        n = len(self.slots)
        if P.dry:
            self.rec.append(loader)
            i = len(self.rec) - 1
            return self.slots[i % n]
        i = self.req
        self.req += 1
        lim = min(len(self.rec), i + 1 + depth)
        while self.issued < lim:
            j = self.issued
            t, b = self.slots[j % n]
            self.rec[j](t, b)
            self.issued += 1
        return self.slots[i % n]


def build_nc():
    nc = bass.Bass("TRN2", target_bir_lowering=False)

    def din(name, shape, dt=F32):
        return nc.dram_tensor(name, list(shape), dt, kind="ExternalInput").ap()

    def dout(name, shape, dt=F32):
        return nc.dram_tensor(name, list(shape), dt, kind="ExternalOutput").ap()

    xloc = din("xloc", [NLC, 512, D])
    xs = din("xs", [64, D])
    ck = din("ck", [2, 4096, 512]); cv = din("cv", [2, 4096, 512]); clf = din("clf", [2, 4096, NH])
    sconv = din("sconv", [2, 30, 512])
    kmask_d = din("kmask", [128, NLC])
    w_in = din("w_in", [D, DIN]); w_pw = din("w_pw", [512, D]); w_ao = din("w_ao", [512, D]); w_out = din("w_out", [D, D])
    w_gate = din("w_gate", [D, DFF]); w_up = din("w_up", [D, DFF]); w_down = din("w_down", [DFF, D])
    g1c_d = din("g1c", [128, 8]); g2c_d = din("g2c", [128, 8]); gfin_d = din("gfin", [128, D]); bf_d = din("bfb", [128, NH])
    wdw_d = din("wdw", [128, 4, CW]); bdw_d = din("bdw", [128, 4]); lng_d = din("lng", [128, 4]); lnb_d = din("lnb", [128, 4])
    ident_d = din("ident", [128, 128]); tri_d = din("tri", [128, 128])

    y_own = dout("y_own", [4, 512, D]); k_own = dout("k_own", [4, 512, 512]); v_own = dout("v_own", [4, 512, 512])
    lf_own = dout("lf_own", [4, 512, NH]); conv_own = dout("conv_own", [30, 512])
    ys_o = dout("ys", [64, D]); ks_o = dout("ks", [64, 512]); vs_o = dout("vs", [64, 512]); lfs_o = dout("lfs", [64, NH])
    convs_o = dout("convs", [2, 30, 512])

    KT = nc.dram_tensor("KT", [NKC, 70, NH, 512], BF16, kind="Internal").ap()
    VA = nc.dram_tensor("VA", [NKC, 128, NH, 4, 65], BF16, kind="Internal").ap()

    with ExitStack() as es:
        def sb(name, shape, dt=F32):
            return es.enter_context(nc.sbuf_tensor("sb_" + name, list(shape), dt))

        P = Prog(nc, es)
        ident_f = sb("ident_f", [128, 128]); ident_b = sb("ident_b", [128, 128], BF16)
        tri_f = sb("tri_f", [128, 128]); tri_b = sb("tri_b", [128, 128], BF16)
        ones_f = sb("ones_f", [128, 128])
        negones = sb("negones", [3, 512], BF16)
        g1c = sb("g1c", [128, 8]); g2c = sb("g2c", [128, 8]); gfin = sb("gfin", [128, D]); bfb = sb("bfb", [128, NH])
        wdw = sb("wdw", [128, 4, CW]); bdw = sb("bdw", [128, 4]); lng = sb("lng", [128, 4]); lnb = sb("lnb", [128, 4])
        lngh = sb("lngh", [128, 4]); lnbh = sb("lnbh", [128, 4])
        kmask = sb("kmask", [128, NLC]); kmb = sb("kmb", [128, NLC])
        B_const = Buf()

        xt = [sb("xt%d" % i, [128, 4, D]) for i in range(2)]
        B_xt = [Buf(), Buf()]
        xnb_ring = [(sb("xnb%d" % i, [128, D], BF16), Buf()) for i in range(2)]
        xnb_state = [0]
        junk = sb("junk", [128, D], BF16); B_junk = Buf()
        stat = sb("stat", [128, 32]); B_stat = Buf()
        stat2 = sb("stat2", [128, 32]); B_stat2 = Buf()
        st_sel = [(stat, B_stat), (stat2, B_stat2)]
        xnT = sb("xnT", [128, 8, 544], BF16); B_xnT = Buf()

        wslots = [(sb("wsl%d" % i, [128, 8, 512], BF16), Buf()) for i in range(6)]
        WS = Stream(P, wslots, 2)
        actT = sb("actT", [128, 22, 512], BF16); B_actT = Buf()
        wk_t = actT[:, 0:8, :]; wv_t = actT[:, 8:16, :]; wf_t = sb("wf_t", [128, 8, NH], BF16)
        B_wkv = B_actT

        aoT = sb("aoT", [64, NH, 512], BF16); B_aoT = Buf()
        ktst = aoT; B_ktst = B_aoT
        vaug = sb("vaug", [128, NH, 4, 65], BF16); B_vaug = Buf()
        kvout = sb("kvout", [128, 512]); B_kvout = Buf()
        sT = sb("sT", [128, 4, 512], BF16); B_sT = Buf()
        ckb = sT; B_ckb = B_sT

        zp = sb("zp", [128, 32, NH]); B_zp = Buf()
        lfp = sb("lfp", [128, 32, NH]); B_lfp = Buf()
        zs = sb("zs", [128, NH]); B_zs = Buf()
        lfs = sb("lfs", [128, NH]); B_lfs = Buf()
        lfc = sb("lfc", [128, 33, NH]); B_lfc = Buf()
        ctmp = sb("ctmp", [128, 33, NH]); B_ctmp = Buf()
        coff = sb("coff", [128, 33, NH]); B_coff = Buf()
        ctot = sb("ctot", [128, 33, NH]); B_ctot = Buf()
        cres = sb("cres", [128, 33, NH]); B_cres = Buf()
        spl = sb("spl", [128, 33, 3, NH], BF16); B_spl = Buf()
        rowst = sb("rowst", [24, 512], BF16); B_rowst = Buf()

        cuT = sb("cuT", [128, 4, 544]); B_cuT = Buf()
        cacc = sb("cacc", [128, 4, 512]); B_cacc = Buf()
        xhalo = cacc[:, :, :].rearrange("p a b -> p (a b)")[0:32, 0:D]; B_xhalo = B_cacc
        thc = sb("thc", [128, 512]); B_thc = Buf()
        tg = thc; B_tg = B_thc
        csq = thc; B_csq = B_thc
        m1 = sb("m1", [128, 512]); B_m1 = Buf()
        lnm = m1; B_lnm = B_m1
        lnr = sb("lnr", [128, 512]); B_lnr = Buf()
        mixT = sb("mixT", [128, 8, 512], BF16); B_mixT = Buf()
        qT = mixT[0:70, :, :]; B_qT = B_mixT; B_qTaug = B_mixT
        vsl = [(sb("vsl%d" % i, [128, 4, 65], BF16), Buf()) for i in range(6)]
        VS = Stream(P, vsl, 3)
        ktsl = [(sb("ktsl%d" % i, [70, 512], BF16), Buf()) for i in range(6)]
        KS = Stream(P, ktsl, 3)
        NPT = 5
        pT = [sb("pT%d" % i, [128, 512], BF16) for i in range(NPT)]; B_pT = [Buf() for _ in range(NPT)]
        oc = [sb("oc%d" % i, [65, 512]) for i in range(2)]; B_oc = [Buf(), Buf()]
        rinv = sb("rinv", [64, 512]); B_rinv = Buf()
        zrow = sb("zrow", [1, 512], BF16)
        zT = xnT[:, :, 0:512]; B_zT = B_xnT
        yout = cuT[:, :, :].rearrange("p a b -> p (a b)")[:, 0:D]; B_yout = B_cuT
        chst = sb("chst", [32, 512]); B_chst = Buf()
        schist = chst; B_schist = B_chst

        pst = [es.enter_context(nc.psum_tensor("pst%d" % i, [128, 1024], BF16)) for i in range(2)]
        B_pst = [Buf(), Buf()]
        psf = [es.enter_context(nc.psum_tensor("psf%d" % i, [128, 512], F32)) for i in range(6)]
        B_psf = [Buf() for _ in range(6)]
        psstate = {"free": [True] * 6, "nxt": 0, "tn": 0}

        def ps_get():
            for _ in range(6):
                i = psstate["nxt"]
                psstate["nxt"] = (i + 1) % 6
                if psstate["free"][i]:
                    psstate["free"][i] = False
                    return i
            raise RuntimeError("psum exhausted")

        def ps_free(i):
            psstate["free"][i] = True

        def pst_get():
            i = psstate["tn"]
            psstate["tn"] = 1 - i
            return i

        wl = {"n": 0, "flags": [], "scr": {}}

        def wload_generic(key, shape, src, part):
            if P.dry:
                first = key not in wl["scr"]
                if first:
                    wl["scr"][key] = (nc.dram_tensor("wsc%d" % len(wl["scr"]), list(shape), BF16, kind="Internal").ap(), Buf())
                wl["flags"].append(first)
            else:
                first = wl["flags"][wl["n"]]
                wl["n"] += 1
            scr, Bscr = wl["scr"][key]

            def view(t):
                return t[0:part, 0:shape[1], 0:shape[2]]
            if first:
                def loader(t, b):
                    P.dma("pool", view(t), src, r=(), w=(b,))
            else:
                def loader(t, b):
                    P.dma("sp", view(t), scr, r=(Bscr,), w=(b,))
            t, b = WS.get(loader)
            if first:
                P.dma("sp", scr, view(t), r=(b,), w=(Bscr,))
            return t, b

        def wload(W, k0, nk, c0, ncols, wn=None):
            src = W[k0 * 128:(k0 + nk) * 128, c0:c0 + ncols].rearrange("(k p) n -> p k n", p=128)
            return wload_generic((W.tensor.name, k0, nk, c0, ncols), [128, nk, ncols], src, 128)

        def rstd_from_ss(ncol, nfeat, eps, st=None, B_st=None):
            st = stat if st is None else st
            B_st = B_stat if B_st is None else B_st
            P.op("dve", lambda e: e.tensor_scalar(out=st[:, 8:8 + ncol], in0=st[:, 0:ncol], scalar1=1.0 / nfeat, scalar2=eps,
                                                 op0=ALU.mult, op1=ALU.add), r=(B_st,), w=(B_st,))
            P.op("act", lambda e: e.activation(out=st[:, 24:24 + ncol], in_=st[:, 8:8 + ncol], func=AF.Sqrt), r=(B_st,), w=(B_st,))
            P.op("dve", lambda e: e.reciprocal(out=st[:, 16:16 + ncol], in_=st[:, 24:24 + ncol]), r=(B_st,), w=(B_st,))

        def norm_stats(blocks, extra_r=(), st=None, B_st=None):
            st = stat if st is None else st
            B_st = B_stat if B_st is None else B_st
            nb = len(blocks)
            P.op("dve", lambda e: e.memset(st[:, 0:8], 0.0), w=(B_st,))
            for bi, (src, npp, c0) in enumerate(blocks):
                P.op("act", lambda e, src=src, npp=npp, bi=bi: e.activation(out=junk[0:npp, :], in_=src, func=AF.Square,
                                                                            accum_out=st[0:npp, bi:bi + 1]),
                     r=extra_r, w=(B_junk, B_st))
            rstd_from_ss(nb, D, 1e-6, st, B_st)

        def norm_apply(blocks, gcol, dstT, B_dstT, extra_r=(), st=None, B_st=None):
            st = stat if st is None else st
            B_st = B_stat if B_st is None else B_st
            for bi, (src, npp, c0) in enumerate(blocks):
                xnb, B_xnb = xnb_ring[xnb_state[0]]; xnb_state[0] = 1 - xnb_state[0]
                P.op("act", lambda e, src=src, npp=npp, bi=bi, xnb=xnb: e.activation(out=xnb[0:npp, :], in_=src, func=AF.Copy, scale=st[0:npp, 16 + bi:17 + bi]),
                     r=extra_r + (B_st,), w=(B_xnb,))
                ti = pst_get()
                pv = pst[ti][:, :].rearrange("p (k t) -> p k t", k=8)
                P.mm([lambda e, kc=kc, npp=npp, pv=pv, xnb=xnb: e.transpose(out=pv[:, kc, 0:npp], in_=xnb[0:npp, kc * 128:(kc + 1) * 128],
                                                                   identity=ident_b[0:npp, 0:npp]) for kc in range(8)],
                     r=(B_xnb, B_const), w=(B_pst[ti],))
                P.op("dve", lambda e, pv=pv, npp=npp, c0=c0: e.tensor_tensor(out=dstT[:, :, c0:c0 + npp], in0=pv[:, :, 0:npp],
                                                                            in1=gcol[:, 0:8].unsqueeze(2).to_broadcast([128, 8, npp]), op=ALU.mult),
                     r=(B_pst[ti], B_const), w=(B_dstT,))

        def norm_T(blocks, gcol, dstT, B_dstT, extra_r=(), st=None, B_st=None):
            norm_stats(blocks, extra_r, st, B_st)
            norm_apply(blocks, gcol, dstT, B_dstT, extra_r, st, B_st)

        def mm_fm(wt, B_wt, nk, wc0, M, actTile, B_act, ac0, N, bank):
            P.mm([lambda e, kc=kc: e.matmul(psf[bank][0:M, 0:N], lhsT=wt[:, kc, wc0:wc0 + M], rhs=actTile[:, kc, ac0:ac0 + N],
                                             start=(kc == 0), stop=(kc == nk - 1)) for kc in range(nk)],
                 r=(B_wt, B_act), w=(B_psf[bank],))

        def mm_tm(actTile, B_act, nk, ac0, M, wt, B_wt, wc0, N, bank, first=True, last=True, kp=128):
            P.mm([lambda e, kc=kc: e.matmul(psf[bank][0:M, 0:N], lhsT=actTile[0:kp, kc, ac0:ac0 + M], rhs=wt[0:kp, kc, wc0:wc0 + N],
                                             start=(first and kc == 0), stop=(last and kc == nk - 1)) for kc in range(nk)],
                 r=(B_wt, B_act), w=(B_psf[bank],))

        def program():
            for bi in range(4):
                P.dma("sp", xt[0][:, bi, :], xloc[0, bi * 128:(bi + 1) * 128, :], w=(B_xt[0],))
            P.dma("pool", wk_t, w_in[:, 1536:2048].rearrange("(k p) n -> p k n", p=128), w=(B_wkv,))
            for t, d in ((ident_f, ident_d), (tri_f, tri_d), (g1c, g1c_d), (g2c, g2c_d), (gfin, gfin_d), (bfb, bf_d),
                         (bdw, bdw_d), (lng, lng_d), (lnb, lnb_d), (kmask, kmask_d)):
                P.dma("sp", t[:], d, w=(B_const,))
            P.dma("sp", wdw[:], wdw_d, w=(B_const,))
            P.op("dve", lambda e: e.tensor_copy(out=ident_b[:], in_=ident_f[:]), r=(B_const,), w=(B_const,))
            P.op("dve", lambda e: e.tensor_copy(out=tri_b[:], in_=tri_f[:]), r=(B_const,), w=(B_const,))
            P.op("dve", lambda e: e.memset(ones_f[:], 1.0), w=(B_const,))
            P.op("dve", lambda e: e.memset(negones[:], -1.0), w=(B_const,))
            P.op("dve", lambda e: e.tensor_scalar(out=lngh[:], in0=lng[:], scalar1=0.5, scalar2=None, op0=ALU.mult), r=(B_const,), w=(B_const,))
            P.op("dve", lambda e: e.tensor_scalar(out=lnbh[:], in0=lnb[:], scalar1=0.5, scalar2=None, op0=ALU.mult), r=(B_const,), w=(B_const,))
            P.op("dve", lambda e: e.tensor_scalar(out=kmb[:], in0=kmask[:], scalar1=-1.0, scalar2=-NEGBIG, op0=ALU.add, op1=ALU.mult),
                 r=(B_const,), w=(B_const,))
            P.op("dve", lambda e: e.memset(vaug[:], 1.0), w=(B_vaug,))
            P.op("dve", lambda e: e.memset(zrow[:], 0.0), w=(B_const,))
            P.op("dve", lambda e: e.memset(zs[:], 0.0), w=(B_zs,))
            B_KT = [Buf() for _ in range(NKC)]
            B_KTa = [Buf() for _ in range(NKC)]
            B_KTn = [Buf() for _ in range(NKC)]
            B_VA = [Buf() for _ in range(NKC)]
            P.dma("pool", wv_t, w_in[:, 2048:2560].rearrange("(k p) n -> p k n", p=128), w=(B_wkv,))
            P.dma("pool", wf_t[:], w_in[:, 2560:2568].rearrange("(k p) n -> p k n", p=128), w=(B_wkv,))

            def phase1_load(ti, src_blocks):
                xtile = xt[ti % 2]; Bx = B_xt[ti % 2]
                for bi, (src, npp) in enumerate(src_blocks):
                    P.dma("sp", xtile[0:npp, bi, :], src, w=(Bx,))

            xnT2 = cuT[:, :, :].rearrange("p a b -> p (a b)").bitcast(BF16).rearrange("p (k t) -> p k t", k=8)
            xn_sel = [(xnT, B_xnT), (xnT2, B_cuT)]

            def phase1_stats(ti, src_blocks, kc_idx, own_i, is_sample):
                xtile = xt[ti % 2]; Bx = B_xt[ti % 2]
                st, B_st = st_sel[ti % 2]
                norm_stats([(xtile[0:npp, bi, :], npp, bi * 128) for bi, (src, npp) in enumerate(src_blocks)], (Bx,), st, B_st)

            def phase1_norm(ti, src_blocks, kc_idx, own_i, is_sample):
                xtile = xt[ti % 2]; Bx = B_xt[ti % 2]
                xnT, B_xnT = xn_sel[ti % 2]
                st, B_st = st_sel[ti % 2]
                norm_apply([(xtile[0:npp, bi, :], npp, bi * 128) for bi, (src, npp) in enumerate(src_blocks)], g1c, xnT, B_xnT, (Bx,), st, B_st)

            def phase1_tile(ti, src_blocks, kc_idx, own_i, is_sample, mid):
                xtile = xt[ti % 2]; Bx = B_xt[ti % 2]
                xnT, B_xnT = xn_sel[ti % 2]
                nb = len(src_blocks)
                NT = sum(b[1] for b in src_blocks)
                for g in range(4):
                    bk = ps_get()
                    mm_fm(wk_t, B_wkv, 8, g * 128, 128, xnT, B_xnT, 0, NT, bk)
                    P.op("act", lambda e, g=g, bk=bk: e.copy(out=ktst[0:64, 2 * g, 0:NT], in_=psf[bk][0:64, 0:NT]), r=(B_psf[bk],), w=(B_ktst,))
                    P.op("dve", lambda e, g=g, bk=bk: e.tensor_copy(out=ktst[0:64, 2 * g + 1, 0:NT], in_=psf[bk][64:128, 0:NT]), r=(B_psf[bk],), w=(B_ktst,))
                    ps_free(bk)
                    cstep()
                if not is_sample:
                    P.dma("sp", KT[kc_idx, 0:64, :, :], ktst[:, :, :], r=(B_ktst,), w=(B_KT[kc_idx],))
                else:
                    for s in range(2):
                        kcn = 8 + 9 * s + 8
                        P.dma("sp", KT[kcn, 0:64, :, 0:32], ktst[:, :, 32 * s:32 * s + 32], r=(B_ktst,), w=(B_KT[kcn],))
                mid()
                for bi, (src, npp) in enumerate(src_blocks):
                    if own_i is not None or is_sample:
                        bk = ps_get()
                        mm_tm(xnT, B_xnT, 8, bi * 128, npp, wk_t, B_wkv, 0, 512, bk)
                        P.op("act", lambda e, bk=bk, npp=npp, bi=bi: e.copy(out=kvout[0:npp, :], in_=psf[bk][0:npp, :]), r=(B_psf[bk],), w=(B_kvout,))
                        ps_free(bk)
                        cstep()
                        if own_i is not None:
                            P.dma("sp", k_own[own_i, bi * 128:(bi + 1) * 128, :], kvout[:, :], r=(B_kvout,), w=())
                        else:
                            P.dma("sp", ks_o[:, :], kvout[0:64, :], r=(B_kvout,), w=())
                for bi, (src, npp) in enumerate(src_blocks):
                    bk = ps_get()
                    mm_tm(xnT, B_xnT, 8, bi * 128, npp, wv_t, B_wkv, 0, 512, bk)
                    P.op("dve", lambda e, bk=bk, npp=npp, bi=bi: e.tensor_copy(out=vaug[0:npp, :, bi, 0:64],
                                                                               in_=psf[bk][0:npp, :].rearrange("p (h d) -> p h d", h=NH)),
                         r=(B_psf[bk],), w=(B_vaug,))
                    if own_i is not None or is_sample:
                        P.op("act", lambda e, bk=bk, npp=npp, bi=bi: e.copy(out=kvout[0:npp, :], in_=psf[bk][0:npp, :]), r=(B_psf[bk],), w=(B_kvout,))
                        if own_i is not None:
                            P.dma("sp", v_own[own_i, bi * 128:(bi + 1) * 128, :], kvout[:, :], r=(B_kvout,), w=())
                        else:
                            P.dma("sp", vs_o[:, :], kvout[0:64, :], r=(B_kvout,), w=())
                    ps_free(bk)
                    cstep()
                if not is_sample:
                    P.dma("sp", VA[kc_idx].rearrange("p h b d -> p (h b d)"), vaug[:, :, :, :].rearrange("p h b d -> p (h b d)"), r=(B_vaug,), w=(B_VA[kc_idx],))
                else:
                    for s in range(2):
                        kcn = 8 + 9 * s + 8
                        P.dma("sp", VA[kcn, 0:32, :, 0, :], vaug[32 * s:32 * s + 32, :, 0, :], r=(B_vaug,), w=(B_VA[kcn],))
                for bi, (src, npp) in enumerate(src_blocks):
                    bk = ps_get()
                    P.mm([lambda e, kc=kc, bi=bi, npp=npp, bk=bk: e.matmul(psf[bk][0:npp, 0:NH], lhsT=xnT[:, kc, bi * 128:bi * 128 + npp], rhs=wf_t[:, kc, :],
                                                                          start=(kc == 0), stop=(kc == 7)) for kc in range(8)],
                         r=(B_wkv, B_xnT), w=(B_psf[bk],))
                    if not is_sample:
                        P.op("dve", lambda e, bk=bk, bi=bi: e.tensor_tensor(out=zp[:, kc_idx * 4 + bi, :], in0=psf[bk][:, 0:NH], in1=bfb[:, :], op=ALU.add),
                             r=(B_psf[bk], B_const), w=(B_zp,))
                    else:
                        P.op("dve", lambda e, bk=bk: e.tensor_tensor(out=zs[0:64, :], in0=psf[bk][0:64, 0:NH], in1=bfb[0:64, :], op=ALU.add),
                             r=(B_psf[bk], B_const), w=(B_zs,))
                    ps_free(bk)
                    cstep()

            ktst2_ring = [(mixT[0:64, :, :], B_mixT),
                          (cacc[:, :, :].rearrange("p a b -> p (a b)").bitcast(BF16)[0:64, :].rearrange("p (h t) -> p h t", h=NH), B_cacc)]
            kt2_state = [0]
            vaug2 = actT[:, 16:21, :].rearrange("p a b -> p (a b)")[:, 0:NH * 4 * 65].rearrange("p (h b d) -> p h b d", h=NH, b=4)
            B_vaug2 = Buf()
            P.op("dve", lambda e: e.memset(vaug2, 1.0), w=(B_vaug2,))

            def cache_conv(s, cc):
                kcn = 8 + 9 * s + cc

                def kl(t, b):
                    P.dma("pool", t[:, 0:4, :], ck[s, cc * 512:(cc + 1) * 512, :].rearrange("(b p) n -> p b n", p=128), w=(b,))

                def vl(t, b):
                    P.dma("pool", t[:, 0:4, :], cv[s, cc * 512:(cc + 1) * 512, :].rearrange("(b p) n -> p b n", p=128), w=(b,))
                tk, Bk = WS.get(kl, depth=4)
                ktst2, B_kt2 = ktst2_ring[kt2_state[0]]; kt2_state[0] = 1 - kt2_state[0]
                for bi in range(4):
                    ti = pst_get()
                    pv = pst[ti][:, 0:512].rearrange("p (g t) -> p g t", g=4)
                    P.mm([lambda e, g=g, bi=bi, pv=pv: e.transpose(out=pv[:, g, :], in_=tk[:, bi, g * 128:(g + 1) * 128], identity=ident_b[:, :])
                          for g in range(4)], r=(Bk, B_const), w=(B_pst[ti],))
                    kv = ktst2[:, :, bi * 128:(bi + 1) * 128].rearrange("p (g e) t -> p g e t", e=2)
                    P.op("act", lambda e, pv=pv, kv=kv: e.copy(out=kv[:, :, 0, :], in_=pv[0:64, :, :]), r=(B_pst[ti],), w=(B_kt2,))
                    P.op("dve", lambda e, pv=pv, kv=kv: e.tensor_copy(out=kv[:, :, 1, :], in_=pv[64:128, :, :]), r=(B_pst[ti],), w=(B_kt2,))
                    yield
                P.dma("sp", KT[kcn, 0:64, :, :], ktst2, r=(B_kt2,), w=(B_KT[kcn],))
                tv, Bv = WS.get(vl, depth=4)
                P.op("act", lambda e: e.copy(out=vaug2[:, :, :, 0:64].rearrange("p h b d -> p b h d"), in_=tv[:, 0:4, :].rearrange("p b (h d) -> p b h d", h=NH)),
                     r=(Bv,), w=(B_vaug2,))
                P.dma("sp", VA[kcn].rearrange("p h b d -> p (h b d)"), vaug2.rearrange("p h b d -> p (h b d)"), r=(B_vaug2,), w=(B_VA[kcn],))
                yield

            cgen = [None]

            def cstep():
                while True:
                    if cgen[0] is None:
                        if not cconv:
                            return
                        cgen[0] = cache_conv(*cconv.pop(0))
                    try:
                        next(cgen[0])
                        return
                    except StopIteration:
                        cgen[0] = None

            p1tiles = [(l, [(xloc[l, bi * 128:(bi + 1) * 128, :], 128) for bi in range(4)], l, (l // 2) if (l % 2 == 1) else None, False) for l in range(NLC)]
            p1tiles.append((NLC, [(xs[:, :], 64)], None, None, True))
            phase1_load(1, p1tiles[1][1])
            for kc in range(NKC):
                P.dma("sp", KT[kc, 67:70, :, :], negones[:, :].unsqueeze(1).to_broadcast([3, NH, 512]), r=(B_const,), w=(B_KTn[kc],))
            phase1_stats(*p1tiles[0])
            phase1_norm(*p1tiles[0])
            cconv = [(s_, c_) for s_ in range(2) for c_ in range(8)]
            for ti_, tl in enumerate(p1tiles):
                def mid(ti_=ti_):
                    if ti_ + 1 < len(p1tiles):
                        phase1_norm(*p1tiles[ti_ + 1])
                    if ti_ + 2 < len(p1tiles):
                        phase1_load(ti_ + 2, p1tiles[ti_ + 2][1])
                if ti_ + 1 < len(p1tiles):
                    phase1_stats(*p1tiles[ti_ + 1])
                phase1_tile(*tl, mid)
            while cconv or cgen[0] is not None:
                cstep()
            B_actT.r.update(B_vaug2.r); B_actT.r.update(B_vaug2.w)

            tick_on = [True]
            bg_gen = [None]

            def bg_step(n):
                for _ in range(n):
                    if bg_gen[0] is not None:
                        try:
                            next(bg_gen[0])
                        except StopIteration:
                            bg_gen[0] = None

            def logf_section(tick):
                def dv(fn, **kw):
                    P.op("dve", fn, **kw)
                    if tick_on[0]:
                        tick()
                P.op("act", lambda e: e.activation(out=lfp[:], in_=zp[:], func=AF.Exp, scale=-1.0), r=(B_zp,), w=(B_lfp,))
                P.op("act", lambda e: e.activation(out=lfs[:], in_=zs[:], func=AF.Exp, scale=-1.0), r=(B_zs,), w=(B_lfs,))
                P.op("act", lambda e: e.activation(out=lfp[:], in_=lfp[:], func=AF.Ln, bias=1.0), r=(B_lfp,), w=(B_lfp,))
                P.op("act", lambda e: e.activation(out=lfs[:], in_=lfs[:], func=AF.Ln, bias=1.0), r=(B_lfs,), w=(B_lfs,))
                for l in range(NLC):
                    dv(lambda e, l=l: e.tensor_scalar(out=lfp[:, 4 * l:4 * l + 4, :], in0=lfp[:, 4 * l:4 * l + 4, :], scalar1=kmask[:, l:l + 1],
                                                              scalar2=None, op0=ALU.mult), r=(B_lfp, B_const), w=(B_lfp,))
                dv(lambda e: e.tensor_scalar(out=lfp[:], in0=lfp[:], scalar1=-1.0, scalar2=None, op0=ALU.mult), r=(B_lfp,), w=(B_lfp,))
                dv(lambda e: e.tensor_scalar(out=lfs[:], in0=lfs[:], scalar1=-1.0, scalar2=None, op0=ALU.mult), r=(B_lfs,), w=(B_lfs,))
                for i in range(4):
                    l = 2 * i + 1
                    P.dma("sp", lf_own[i].rearrange("(b p) h -> p b h", p=128), lfp[:, 4 * l:4 * l + 4, :], r=(B_lfp,), w=())
                P.dma("sp", lfs_o[:, :], lfs[0:64, :], r=(B_lfs,), w=())

                def cumsum_rows(LF, B_LF, NB, kc_list, maskcol):
                    b1 = ps_get(); b2 = ps_get()
                    lf2 = LF[:, 0:NB, :].rearrange("p b h -> p (b h)")
                    P.mm([lambda e: e.matmul(psf[b1][:, 0:NB * NH], lhsT=tri_f[:, :], rhs=lf2, start=True, stop=True)], r=(B_LF, B_const), w=(B_psf[b1],))
                    yield
                    P.mm([lambda e: e.matmul(psf[b2][:, 0:NB * NH], lhsT=ones_f[:, :], rhs=lf2, start=True, stop=True)], r=(B_LF, B_const), w=(B_psf[b2],))
                    yield
                    P.op("act", lambda e: e.copy(out=ctot[:, 0:NB, :].rearrange("p b h -> p (b h)"), in_=psf[b2][:, 0:NB * NH]), r=(B_psf[b2],), w=(B_ctot,))
                    yield
                    ps_free(b2)
                    dv(lambda e: e.memset(coff[:, 0, :], 0.0), w=(B_coff,))
                    yield
                    dv(lambda e: e.tensor_copy(out=coff[:, 1:NB, :], in_=ctot[:, 0:NB - 1, :]), r=(B_ctot,), w=(B_coff,))
                    yield
                    bufs = [(coff, B_coff), (ctot, B_ctot)]
                    for si, sh in enumerate([1, 2, 4, 8, 16, 32]):
                        (src, Bs), (dst, Bd) = bufs[si % 2], bufs[(si + 1) % 2]
                        m = min(sh, NB)
                        dv(lambda e, src=src, dst=dst, m=m: e.tensor_copy(out=dst[:, 0:m, :], in_=src[:, 0:m, :]), r=(Bs,), w=(Bd,))
                        yield
                        if sh < NB:
                            dv(lambda e, src=src, dst=dst, sh=sh: e.tensor_tensor(out=dst[:, sh:NB, :], in0=src[:, sh:NB, :], in1=src[:, 0:NB - sh, :], op=ALU.add),
                                 r=(Bs,), w=(Bd,))
                            yield
                    dv(lambda e: e.tensor_tensor(out=ctmp[:, 0:NB, :].rearrange("p b h -> p (b h)"), in0=psf[b1][:, 0:NB * NH],
                                                          in1=coff[:, 0:NB, :].rearrange("p b h -> p (b h)"), op=ALU.add), r=(B_psf[b1], B_coff), w=(B_ctmp,))
                    yield
                    ps_free(b1)
                    dv(lambda e: e.tensor_scalar(out=cres[:, 0:NB, :], in0=ctmp[:, 0:NB, :], scalar1=-1.0, scalar2=None, op0=ALU.mult),
                         r=(B_ctmp,), w=(B_cres,))
                    yield
                    if maskcol:
                        for l in range(NLC):
                            dv(lambda e, l=l: e.tensor_scalar(out=cres[:, 4 * l:4 * l + 4, :], in0=cres[:, 4 * l:4 * l + 4, :], scalar1=kmb[:, l:l + 1],
                                                                      scalar2=None, op0=ALU.add), r=(B_cres, B_const), w=(B_cres,))
                            yield
                    dv(lambda e: e.tensor_copy(out=spl[:, 0:NB, 0, :], in_=cres[:, 0:NB, :]), r=(B_cres,), w=(B_spl,))
                    yield
                    dv(lambda e: e.tensor_tensor(out=ctmp[:, 0:NB, :], in0=cres[:, 0:NB, :], in1=spl[:, 0:NB, 0, :], op=ALU.subtract),
                         r=(B_cres, B_spl), w=(B_ctmp,))
                    yield
                    dv(lambda e: e.tensor_copy(out=spl[:, 0:NB, 1, :], in_=ctmp[:, 0:NB, :]), r=(B_ctmp,), w=(B_spl,))
                    yield
                    dv(lambda e: e.tensor_tensor(out=cres[:, 0:NB, :], in0=ctmp[:, 0:NB, :], in1=spl[:, 0:NB, 1, :], op=ALU.subtract),
                         r=(B_ctmp, B_spl), w=(B_cres,))
                    yield
                    dv(lambda e: e.tensor_copy(out=spl[:, 0:NB, 2, :], in_=cres[:, 0:NB, :]), r=(B_cres,), w=(B_spl,))
                    yield
                    for ci, kc in enumerate(kc_list):
                        nbk = min(4, NB - 4 * ci)
                        ti = pst_get()
                        P.mm([lambda e, bi=bi, ci=ci, ti=ti: e.transpose(out=pst[ti][0:24, bi * 128:(bi + 1) * 128],
                                                                          in_=spl[:, 4 * ci + bi, :, :].rearrange("p j h -> p (j h)"), identity=ident_b[:, :])
                              for bi in range(nbk)], r=(B_spl, B_const), w=(B_pst[ti],))
                        yield
                        P.op("act", lambda e, ti=ti, nbk=nbk: e.copy(out=rowst[:, 0:nbk * 128], in_=pst[ti][0:24, 0:nbk * 128]), r=(B_pst[ti],), w=(B_rowst,))
                        yield
                        ncol = 512 if nbk == 4 else 32
                        P.dma("sp", KT[kc, 64:67, :, 0:ncol].rearrange("j h k -> (j h) k"), rowst[:, 0:ncol], r=(B_rowst,), w=(B_KTa[kc],))
                        yield

                for _ in cumsum_rows(lfp, B_lfp, 32, list(range(8)), True):
                    pass

                def sample_cs():
                    for s in range(2):
                        P.dma("sp", lfc[:, 0:32, :], clf[s].rearrange("(b p) h -> p b h", p=128), w=(B_lfc,))
                        dv(lambda e: e.memset(lfc[:, 32, :], 0.0), w=(B_lfc,))
                        dv(lambda e, s=s: e.tensor_copy(out=lfc[0:32, 32, :], in_=lfs[32 * s:32 * s + 32, :]), r=(B_lfs,), w=(B_lfc,))
                        yield
                        yield from cumsum_rows(lfc, B_lfc, 33, [8 + 9 * s + j for j in range(9)], False)
                tick_on[0] = False
                bg_gen[0] = sample_cs()

            def qaug(q0, nq, diag_kc):
                P.dma("sp", qT[67:70, :, q0:q0 + nq], KT[diag_kc, 64:67, :, 0:nq], r=(B_KTa[diag_kc],), w=(B_qTaug,))

            def finalize_a(bo, ncol):
                ob = fin_state[0]; fin_state[0] = 1 - ob
                P.op("act", lambda e: e.copy(out=oc[ob][:, 0:ncol], in_=psf[bo][0:65, 0:ncol]), r=(B_psf[bo],), w=(B_oc[ob],))
                ps_free(bo)
                P.op("dve", lambda e: e.reciprocal(out=oc[ob][64:65, 0:ncol], in_=oc[ob][64:65, 0:ncol]), r=(B_oc[ob],), w=(B_oc[ob],))
                return ob

            def finalize_b(ob, ncol, dst_fn):
                bb = ps_get()
                P.mm([lambda e: e.matmul(psf[bb][0:64, 0:ncol], lhsT=ones_f[64:65, 0:64], rhs=oc[ob][64:65, 0:ncol], start=True, stop=True)],
                     r=(B_oc[ob], B_const), w=(B_psf[bb],))
                P.op("act", lambda e: e.copy(out=rinv[:, 0:ncol], in_=psf[bb][0:64, 0:ncol]), r=(B_psf[bb],), w=(B_rinv,))
                ps_free(bb)
                dst_fn(oc[ob], B_oc[ob], bb)

            fin_state = [0]
            pt_state = [0]
            LOOK = 3

            def attention(nq, chunks, diag_kc, fillers, tail):
                allc = chunks + [diag_kc]
                qaug(0, nq, diag_kc)
                nfill = (len(fillers) + NH - 3) // (NH - 2)
                fpos = 0
                pending_fin = [None]
                tail_gen = [None]

                def flush_fin():
                    if pending_fin[0] is not None:
                        ob_, h_ = pending_fin[0]
                        pending_fin[0] = None

                        def dst(ocb, Bocb, bb, h_=h_):
                            P.op("dve", lambda e: e.tensor_tensor(out=aoT[:, h_, 0:nq], in0=ocb[0:64, 0:nq], in1=rinv[:, 0:nq], op=ALU.mult),
                                 r=(Bocb, B_rinv), w=(B_aoT,))
                        finalize_b(ob_, nq, dst)
                for h in range(NH):
                    bo = ps_get()
                    blocks = [(j, kb) for j in range(len(allc)) for kb in range(4)]
                    nb = len(blocks)
                    cur = {}
                    pend = {}
                    for idx in range(nb + LOOK):
                        if idx == 8:
                            flush_fin()
                        if idx < nb:
                            j, kb = blocks[idx]
                            kc = allc[j]
                            isdiag = (j == len(allc) - 1)
                            if kb == 0:
                                def loader(t, b, kc=kc, h=h):
                                    P.dma("sp", t[:, 0:512], KT[kc, :, h, :], r=(B_KT[kc], B_KTa[kc], B_KTn[kc]), w=(b,))

                                def vloader(t, b, kc=kc, h=h):
                                    P.dma("sp", t[:, :, :], VA[kc, :, h, :, :], r=(B_VA[kc],), w=(b,))
                                cur["k"] = KS.get(loader)
                                cur["v"] = VS.get(vloader)
                            kt, Bkt = cur["k"]
                            vt, Bvt = cur["v"]
                            c0 = kb * 128 if isdiag else 0
                            n = nq - c0
                            bs = ps_get()
                            P.mm([lambda e, kb=kb, c0=c0, n=n, bs=bs, kt=kt: e.matmul(psf[bs][:, 0:n], lhsT=kt[:, kb * 128:(kb + 1) * 128],
                                                                                  rhs=qT[:, h, c0:c0 + n], start=True, stop=True)],
                                 r=(Bkt, B_qT), w=(B_psf[bs],))
                            p = pt_state[0]; pt_state[0] = (p + 1) % NPT
                            P.op("act", lambda e, p=p, n=n, bs=bs: e.activation(out=pT[p][:, 0:n], in_=psf[bs][:, 0:n], func=AF.Exp),
                                 r=(B_psf[bs],), w=(B_pT[p],))
                            ps_free(bs)
                            if isdiag:
                                P.op("pool", lambda e, p=p: e.tensor_tensor(out=pT[p][:, 0:128], in0=pT[p][:, 0:128], in1=tri_b[:, :], op=ALU.mult),
                                     r=(B_pT[p], B_const), w=(B_pT[p],))
                            pend[idx] = (p, kb, c0, n, vt, Bvt)
                            if tail_gen[0] is not None:
                                next(tail_gen[0], None)
                        if idx >= LOOK:
                            i2 = idx - LOOK
                            p, kb, c0, n, vt, Bvt = pend.pop(i2)
                            P.mm([lambda e, p=p, kb=kb, c0=c0, n=n, vt=vt, i2=i2: e.matmul(
                                psf[bo][0:65, c0:c0 + n], lhsT=vt[:, kb, :], rhs=pT[p][:, 0:n], start=(i2 == 0), stop=(i2 == nb - 1))],
                                 r=(B_pT[p], Bvt), w=(B_psf[bo],))

                    flush_fin()
                    pending_fin[0] = (finalize_a(bo, nq), h)
                    for f in fillers[fpos:fpos + nfill]:
                        f()
                    fpos += nfill
                    if h == NH - 2:
                        for f in fillers[fpos:]:
                            f()
                        fpos = len(fillers)
                        tail_gen[0] = tail()
                flush_fin()
                if tail_gen[0] is not None:
                    for _ in tail_gen[0]:
                        pass

            def attention_sample(s, fillers=None):
                q0 = 32 * s
                kcs = [8 + 9 * s + j for j in range(9)]
                qaug(q0, 32, kcs[8])
                bo = ps_get()
                P.mm([lambda e: e.matmul(psf[bo][0:65, 0:256], lhsT=zrow[0:1, 0:65], rhs=zrow[0:1, 0:256], start=True, stop=False)],
                     r=(B_const,), w=(B_psf[bo],))
                blocks = [(j, kb) for j in range(8) for kb in range(4)] + [(8, 0)]
                nb = len(blocks)
                cur = {}
                pend = {}
                for idx in range(nb + LOOK):
                    if idx < nb:
                        j, kb = blocks[idx]
                        kc = kcs[j]
                        isdiag = (j == 8)
                        kw = 32 if isdiag else 128
                        if kb == 0:
                            def loader(t, b, kc=kc, isdiag=isdiag):
                                if isdiag:
                                    P.dma("sp", t[0:70, :, 0:32], KT[kc, :, :, 0:32], r=(B_KT[kc], B_KTa[kc], B_KTn[kc]), w=(b,))
                                else:
                                    P.dma("sp", t[0:70, :, :], KT[kc], r=(B_KT[kc], B_KTa[kc], B_KTn[kc]), w=(b,))

                            def vloader(t, b, kc=kc, isdiag=isdiag):
                                if isdiag:
                                    P.dma("sp", t[0:32, :, 0:65], VA[kc, 0:32, :, 0, :], r=(B_VA[kc],), w=(b,))
                                else:
                                    P.dma("sp", t[:, :, 0:260], VA[kc].rearrange("p h b d -> p h (b d)"), r=(B_VA[kc],), w=(b,))
                            cur["k"] = WS.get(loader)
                            cur["v"] = WS.get(vloader)
                        kt, Bkt = cur["k"]
                        vt, Bvt = cur["v"]
                        bs = ps_get()
                        P.mm([lambda e, h=h, kb=kb, kw=kw, bs=bs, kt=kt: e.matmul(psf[bs][0:kw, h * 32:(h + 1) * 32], lhsT=kt[0:70, h, kb * 128:kb * 128 + kw],
                                                                             rhs=qT[:, h, q0:q0 + 32], start=True, stop=True) for h in range(NH)],
                             r=(Bkt, B_qT), w=(B_psf[bs],))
                        p = pt_state[0]; pt_state[0] = (p + 1) % NPT
                        P.op("act", lambda e, p=p, kw=kw, bs=bs: e.activation(out=pT[p][0:kw, 0:256], in_=psf[bs][0:kw, 0:256], func=AF.Exp),
                             r=(B_psf[bs],), w=(B_pT[p],))
                        ps_free(bs)
                        if isdiag:
                            P.op("dve", lambda e, p=p: e.tensor_tensor(out=pT[p][0:32, 0:256].rearrange("p (h q) -> p h q", h=NH),
                                                                        in0=pT[p][0:32, 0:256].rearrange("p (h q) -> p h q", h=NH),
                                                                        in1=tri_b[0:32, 0:32].unsqueeze(1).to_broadcast([32, NH, 32]), op=ALU.mult),
                                 r=(B_pT[p], B_const), w=(B_pT[p],))
                        pend[idx] = (p, kb, kw, vt, Bvt)
                        if fillers:
                            for _ in range(4):
                                if fillers:
                                    fillers.pop(0)()
                    if idx >= LOOK:
                        i2 = idx - LOOK
                        p, kb, kw, vt, Bvt = pend.pop(i2)
                        P.mm([lambda e, h=h, p=p, kb=kb, kw=kw, vt=vt, i2=i2: e.matmul(
                            psf[bo][0:65, h * 32:(h + 1) * 32], lhsT=vt[0:kw, h, kb * 65:(kb + 1) * 65], rhs=pT[p][0:kw, h * 32:(h + 1) * 32],
                            start=False, stop=(i2 == nb - 1 and h == NH - 1)) for h in range(NH)],
                             r=(B_pT[p], Bvt), w=(B_psf[bo],))

                def dst(ocb, Bocb, bb):
                    P.op("dve", lambda e: e.tensor_tensor(out=aoT[:, :, q0:q0 + 32], in0=ocb[0:64, 0:256].rearrange("p (h q) -> p h q", h=NH),
                                                          in1=rinv[:, 0:256].rearrange("p (h q) -> p h q", h=NH), op=ALU.mult),
                         r=(Bocb, B_rinv), w=(B_aoT,))
                finalize_b(finalize_a(bo, 256), 256, dst)

            def tile2_load(ti, is_sample, l):
                xtile = xt[ti % 2]; Bx = B_xt[ti % 2]
                if not is_sample:
                    for bi in range(4):
                        P.dma("sp", xtile[:, bi, :], xloc[l, bi * 128:(bi + 1) * 128, :], w=(Bx,))
                    P.dma("sp", xhalo[:, :], xloc[l - 1, 480:512, :], w=(B_xhalo,))
                else:
                    P.dma("sp", xtile[0:64, 0, :], xs[:, :], w=(Bx,))

            pre_stats = [False]
            pre_apply = [False]

            def norm1_blocks(ti, is_sample):
                xtile = xt[ti % 2]
                if not is_sample:
                    return [(xhalo[:, :], 32, 0)] + [(xtile[:, bi, :], 128, 32 + bi * 128) for bi in range(4)], (B_xhalo, B_xt[ti % 2])
                return [(xtile[0:64, 0, :], 64, 0)], (B_xt[ti % 2],)

            def tile2(ti, is_sample, l, own_i, nxt):
                NT = 64 if is_sample else 512
                xtile = xt[ti % 2]; Bx = B_xt[ti % 2]
                n1b, n1r = norm1_blocks(ti, is_sample)
                if not pre_stats[0]:
                    norm_stats(n1b, n1r, stat2, B_stat2)
                pre_stats[0] = False
                if not pre_apply[0]:
                    norm_apply(n1b, g1c, xnT, B_xnT, n1r, stat2, B_stat2)
                pre_apply[0] = False
                if not is_sample:
                    tblocks = [(128, bi * 128) for bi in range(4)]
                    xc0 = 32; glu_n = 544; glu_c0 = 0
                    segs = [(32, 512)]
                    cu_dst0 = 0
                else:
                    tblocks = [(64, 0)]
                    xc0 = 0; glu_n = 64; glu_c0 = 0
                    segs = [(32, 32), (96, 32)]
                    P.op("dve", lambda e: e.memset(cuT[:, :, 0:128], 0.0), w=(B_cuT,))
                    for s in range(2):
                        P.dma("sp", schist[0:30, :], sconv[s], w=(B_schist,))
                        for cc in range(4):
                            bk = ps_get()
                            P.mm([lambda e, cc=cc, bk=bk: e.matmul(psf[bk][:, 0:30], lhsT=schist[0:30, cc * 128:(cc + 1) * 128], rhs=ident_f[0:30, 0:30],
                                                                  start=True, stop=True)], r=(B_schist, B_const), w=(B_psf[bk],))
                            P.op("act", lambda e, cc=cc, bk=bk, s=s: e.copy(out=cuT[:, cc, 64 * s + 2:64 * s + 32], in_=psf[bk][:, 0:30]), r=(B_psf[bk],), w=(B_cuT,))
                            ps_free(bk)

                wa, Bwa = wload(w_in, 0, 8, 0, 512)
                wg, Bwg = wload(w_in, 0, 8, 512, 512)
                for cc in range(4):
                    col = 0
                    while col < glu_n:
                        n = min(512, glu_n - col)
                        ba = ps_get(); bg = ps_get()
                        mm_fm(wg, Bwg, 8, cc * 128, 128, xnT, B_xnT, glu_c0 + col, n, bg)
                        mm_fm(wa, Bwa, 8, cc * 128, 128, xnT, B_xnT, glu_c0 + col, n, ba)
                        P.op("act", lambda e, bg=bg, n=n: e.activation(out=tg[:, 0:n], in_=psf[bg][:, 0:n], func=AF.Tanh, scale=0.5), r=(B_psf[bg],), w=(B_tg,))
                        ps_free(bg)
                        if not is_sample:
                            dsts = [(cuT[:, cc, col:col + n], 0, n)]
                        else:
                            dsts = [(cuT[:, cc, 32:64], 0, 32), (cuT[:, cc, 96:128], 32, 32)]
                        for dst, s0, sn in dsts:
                            P.op("dve", lambda e, dst=dst, s0=s0, sn=sn, ba=ba: e.scalar_tensor_tensor(out=dst, in0=tg[:, s0:s0 + sn], scalar=1.0, in1=psf[ba][:, s0:s0 + sn],
                                                                                                  op0=ALU.add, op1=ALU.mult), r=(B_tg, B_psf[ba]), w=(B_cuT,))
                        ps_free(ba)
                        col += n
                if not is_sample:
                    P.op("act", lambda e: e.activation(out=cuT[:, :, 0:544], in_=cuT[:, :, 0:544], func=AF.Copy, scale=0.5), r=(B_cuT,), w=(B_cuT,))
                else:
                    for s in range(2):
                        P.op("act", lambda e, s=s: e.activation(out=cuT[:, :, 64 * s + 32:64 * s + 64], in_=cuT[:, :, 64 * s + 32:64 * s + 64], func=AF.Copy, scale=0.5),
                             r=(B_cuT,), w=(B_cuT,))
                def hist_out(colbase, dst):
                    for cc in range(4):
                        bk = ps_get()
                        P.mm([lambda e, cc=cc, bk=bk: e.matmul(psf[bk][0:30, 0:128], lhsT=cuT[:, cc, colbase:colbase + 30], rhs=ident_f[:, :], start=True, stop=True)],
                             r=(B_cuT, B_const), w=(B_psf[bk],))
                        P.op("act", lambda e, cc=cc, bk=bk: e.copy(out=chst[0:30, cc * 128:(cc + 1) * 128], in_=psf[bk][0:30, 0:128]), r=(B_psf[bk],), w=(B_chst,))
                        ps_free(bk)
                    P.dma("sp", dst, chst[0:30, :], r=(B_chst,), w=())
                if is_sample:
                    for s in range(2):
                        hist_out(64 * s + 34, convs_o[s])
                elif own_i == 3:
                    hist_out(514, conv_own[:, :])

                wq, Bwq = wload(w_in, 0, 8, 1024, 512)
                for g in range(4):
                    bk = ps_get()
                    mm_fm(wq, Bwq, 8, g * 128, 128, xnT, B_xnT, xc0, NT, bk)
                    P.op("act", lambda e, g=g, bk=bk: e.activation(out=qT[0:64, 2 * g, 0:NT], in_=psf[bk][0:64, 0:NT], func=AF.Copy, scale=0.125), r=(B_psf[bk],), w=(B_qT,))
                    P.op("dve", lambda e, g=g, bk=bk: e.tensor_scalar(out=qT[0:64, 2 * g + 1, 0:NT], in0=psf[bk][64:128, 0:NT], scalar1=0.125, scalar2=None, op0=ALU.mult),
                         r=(B_psf[bk],), w=(B_qT,))
                    ps_free(bk)
                P.op("dve", lambda e: e.memset(qT[64:67, :, 0:NT], 1.0), w=(B_qT,))

                fillers = []
                for cc in range(4):
                    for (o0, n) in segs:
                        d0 = o0 - 32 if not is_sample else (0 if o0 == 32 else 32)
                        for j in range(CW):
                            src = cuT[:, cc, o0 - 30 + j:o0 - 30 + j + n]
                            if j == 0:
                                fillers.append(lambda src=src, cc=cc, d0=d0, n=n: P.op(
                                    "dve", lambda e: e.tensor_scalar(out=cacc[:, cc, d0:d0 + n], in0=src, scalar1=wdw[:, cc, 0:1],
                                                                     scalar2=bdw[:, cc:cc + 1], op0=ALU.mult, op1=ALU.add),
                                    r=(B_cuT, B_const), w=(B_cacc,)))
                            else:
                                fillers.append(lambda src=src, cc=cc, d0=d0, n=n, j=j: P.op(
                                    "dve", lambda e: e.scalar_tensor_tensor(out=cacc[:, cc, d0:d0 + n], in0=src, scalar=wdw[:, cc, j:j + 1],
                                                                            in1=cacc[:, cc, d0:d0 + n], op0=ALU.mult, op1=ALU.add),
                                    r=(B_cuT, B_const, B_cacc), w=(B_cacc,)))
                def ln_block():
                    b1 = ps_get(); b2 = ps_get()
                    P.mm([lambda e, cc=cc: e.matmul(psf[b1][:, 0:NT], lhsT=ones_f[:, :], rhs=cacc[:, cc, 0:NT], start=(cc == 0), stop=(cc == 3)) for cc in range(4)],
                         r=(B_cacc, B_const), w=(B_psf[b1],))
                    yield
                    for cc in range(4):
                        P.op("act", lambda e, cc=cc: e.activation(out=csq[:, 0:NT], in_=cacc[:, cc, 0:NT], func=AF.Square), r=(B_cacc,), w=(B_csq,))
                        yield
                        P.mm([lambda e, cc=cc: e.matmul(psf[b2][:, 0:NT], lhsT=ones_f[:, :], rhs=csq[:, 0:NT], start=(cc == 0), stop=(cc == 3))],
                             r=(B_csq, B_const), w=(B_psf[b2],))
                        yield
                    P.op("act", lambda e: e.activation(out=lnm[:, 0:NT], in_=psf[b1][:, 0:NT], func=AF.Copy, scale=1.0 / 512), r=(B_psf[b1],), w=(B_lnm,))
                    yield
                    ps_free(b1)
                    P.op("dve", lambda e: e.tensor_tensor(out=lnr[:, 0:NT], in0=lnm[:, 0:NT], in1=lnm[:, 0:NT], op=ALU.mult), r=(B_lnm,), w=(B_lnr,))
                    yield
                    P.op("dve", lambda e: e.scalar_tensor_tensor(out=lnr[:, 0:NT], in0=psf[b2][:, 0:NT], scalar=1.0 / 512, in1=lnr[:, 0:NT], op0=ALU.mult, op1=ALU.subtract),
                         r=(B_psf[b2], B_lnr), w=(B_lnr,))
                    yield
                    ps_free(b2)
                    P.op("dve", lambda e: e.tensor_scalar(out=lnr[:, 0:NT], in0=lnr[:, 0:NT], scalar1=1e-5, scalar2=None, op0=ALU.add), r=(B_lnr,), w=(B_lnr,))
                    yield
                    P.op("act", lambda e: e.activation(out=lnr[:, 0:NT], in_=lnr[:, 0:NT], func=AF.Sqrt), r=(B_lnr,), w=(B_lnr,))
                    yield
                    P.op("dve", lambda e: e.reciprocal(out=lnr[:, 0:NT], in_=lnr[:, 0:NT]), r=(B_lnr,), w=(B_lnr,))
                    yield
                    for cc in range(4):
                        P.op("dve", lambda e, cc=cc: e.tensor_tensor(out=cacc[:, cc, 0:NT], in0=cacc[:, cc, 0:NT], in1=lnm[:, 0:NT], op=ALU.subtract),
                             r=(B_cacc, B_lnm), w=(B_cacc,))
                        yield
                        P.op("dve", lambda e, cc=cc: e.tensor_tensor(out=cacc[:, cc, 0:NT], in0=cacc[:, cc, 0:NT], in1=lnr[:, 0:NT], op=ALU.mult),
                             r=(B_cacc, B_lnr), w=(B_cacc,))
                        yield
                        P.op("act", lambda e, cc=cc: e.activation(out=cacc[:, cc, 0:NT], in_=cacc[:, cc, 0:NT], func=AF.Identity, scale=lngh[:, cc:cc + 1], bias=lnbh[:, cc:cc + 1]),
                             r=(B_cacc, B_const), w=(B_cacc,))
                        yield
                        P.op("act", lambda e, cc=cc: e.activation(out=csq[:, 0:NT], in_=cacc[:, cc, 0:NT], func=AF.Tanh), r=(B_cacc,), w=(B_csq,))
                        yield
                        P.op("dve", lambda e, cc=cc: e.scalar_tensor_tensor(out=sT[:, cc, 0:NT], in0=csq[:, 0:NT], scalar=1.0, in1=cacc[:, cc, 0:NT], op0=ALU.add, op1=ALU.mult),
                             r=(B_csq, B_cacc), w=(B_sT,))
                        yield


                yield fillers
                if not is_sample:
                    attention(512, list(range(l)), l, fillers, ln_block)
                else:
                    for s in range(2):
                        attention_sample(s, fillers)
                    while fillers:
                        fillers.pop(0)()
                    for _ in ln_block():
                        pass

                if nxt is not None:
                    tile2_load(*nxt)

                def wao_load(c0):
                    src = w_ao[:, c0:c0 + 512].rearrange("(h p) n -> p h n", p=64)
                    return wload_generic(("w_ao", c0), [64, 8, 512], src, 64)
                wpw = {}; wgc = {}; wao = {}; wga = {}
                for half in range(2):
                    wgc[half] = wload(w_in, 0, 8, 2568 + half * 512, 512)
                    wpw[half] = wload(w_pw, 0, 4, half * 512, 512)
                    wga[half] = wload(w_in, 0, 8, 3592 + half * 512, 512)
                    wao[half] = wao_load(half * 512)
                    for f4 in range(4):
                        fc = half * 4 + f4
                        bg = ps_get()
                        mm_fm(wgc[half][0], wgc[half][1], 8, f4 * 128, 128, xnT, B_xnT, xc0, NT, bg)
                        P.op("act", lambda e, bg=bg: e.activation(out=thc[:, 0:NT], in_=psf[bg][:, 0:NT], func=AF.Tanh, scale=0.5), r=(B_psf[bg],), w=(B_thc,))
                        ps_free(bg)
                        by = ps_get()
                        mm_fm(wpw[half][0], wpw[half][1], 4, f4 * 128, 128, sT, B_sT, 0, NT, by)
                        P.op("dve", lambda e, by=by: e.scalar_tensor_tensor(out=m1[:, 0:NT], in0=thc[:, 0:NT], scalar=1.0, in1=psf[by][:, 0:NT], op0=ALU.add, op1=ALU.mult),
                             r=(B_thc, B_psf[by]), w=(B_m1,))
                        ps_free(by)
                        bg = ps_get()
                        mm_fm(wga[half][0], wga[half][1], 8, f4 * 128, 128, xnT, B_xnT, xc0, NT, bg)
                        P.op("act", lambda e, bg=bg: e.activation(out=thc[:, 0:NT], in_=psf[bg][:, 0:NT], func=AF.Tanh, scale=0.5), r=(B_psf[bg],), w=(B_thc,))
                        ps_free(bg)
                        by = ps_get()
                        wt, Bwt = wao[half]
                        P.mm([lambda e, h=h, by=by, wt=wt, f4=f4: e.matmul(psf[by][:, 0:NT], lhsT=wt[0:64, h, f4 * 128:(f4 + 1) * 128], rhs=aoT[:, h, 0:NT],
                                                                           start=(h == 0), stop=(h == NH - 1)) for h in range(NH)], r=(Bwt, B_aoT), w=(B_psf[by],))
                        P.op("dve", lambda e, by=by: e.scalar_tensor_tensor(out=thc[:, 0:NT], in0=thc[:, 0:NT], scalar=1.0, in1=psf[by][:, 0:NT], op0=ALU.add, op1=ALU.mult),
                             r=(B_thc, B_psf[by]), w=(B_thc,))
                        ps_free(by)
                        P.op("dve", lambda e, fc=fc: e.tensor_tensor(out=mixT[:, fc, 0:NT], in0=m1[:, 0:NT], in1=thc[:, 0:NT], op=ALU.add), r=(B_m1, B_thc), w=(B_mixT,))

                for half in range(2):
                    wo_t, wo_b = wload(w_out, 0, 8, half * 512, 512)
                    for bi, (npp, c0) in enumerate(tblocks):
                        bk = ps_get()
                        mm_tm(mixT, B_mixT, 8, c0, npp, wo_t, wo_b, 0, 512, bk)
                        P.op("dve", lambda e, bk=bk, bi=bi, npp=npp, half=half: e.scalar_tensor_tensor(out=xtile[0:npp, bi, half * 512:(half + 1) * 512], in0=psf[bk][0:npp, :], scalar=0.5,
                                                                                                  in1=xtile[0:npp, bi, half * 512:(half + 1) * 512], op0=ALU.mult, op1=ALU.add),
                             r=(B_psf[bk], Bx), w=(Bx,))
                        ps_free(bk)
                norm_T([(xtile[0:npp, bi, :], npp, c0) for bi, (npp, c0) in enumerate(tblocks)], g2c, zT, B_zT, extra_r=(Bx,))
                for c5 in range(6):
                    ncols = 512 if c5 < 5 else 256
                    wgt, Bwgt = wload(w_gate, 0, 8, c5 * 512, ncols)
                    wut, Bwut = wload(w_up, 0, 8, c5 * 512, ncols)
                    for f4 in range(ncols // 128):
                        fc = c5 * 4 + f4
                        bg = ps_get(); bu = ps_get()
                        mm_fm(wgt, Bwgt, 8, f4 * 128, 128, zT, B_zT, 0, NT, bg)
                        mm_fm(wut, Bwut, 8, f4 * 128, 128, zT, B_zT, 0, NT, bu)
                        P.op("act", lambda e, bg=bg: e.activation(out=thc[:, 0:NT], in_=psf[bg][:, 0:NT], func=AF.Tanh, scale=0.5), r=(B_psf[bg],), w=(B_thc,))
                        P.op("dve", lambda e, bg=bg: e.scalar_tensor_tensor(out=m1[:, 0:NT], in0=thc[:, 0:NT], scalar=1.0, in1=psf[bg][:, 0:NT], op0=ALU.add, op1=ALU.mult),
                             r=(B_thc, B_psf[bg]), w=(B_m1,))
                        ps_free(bg)
                        P.op("dve", lambda e, bu=bu, fc=fc: e.scalar_tensor_tensor(out=actT[:, fc, 0:NT], in0=m1[:, 0:NT], scalar=0.5, in1=psf[bu][:, 0:NT], op0=ALU.mult, op1=ALU.mult),
                             r=(B_m1, B_psf[bu]), w=(B_actT,))
                        ps_free(bu)
                        bg_step(3)
                if nxt is not None:
                    nb_, nr_ = norm1_blocks(nxt[0], nxt[1])
                    norm_stats(nb_, nr_, stat2, B_stat2)
                    pre_stats[0] = True
                kgroups = [(0, 8), (8, 8), (16, 6)]
                for half in range(2):
                    banks = [ps_get() for _ in tblocks]
                    for gi, (k0, nk) in enumerate(kgroups):
                        wd, Bwd = wload(w_down, k0, nk, half * 512, 512)
                        for bi, (npp, c0) in enumerate(tblocks):
                            P.mm([lambda e, kc=kc, bi=bi, npp=npp, c0=c0, k0=k0, nk=nk, gi=gi, wd=wd: e.matmul(
                                psf[banks[bi]][0:npp, :], lhsT=actT[:, k0 + kc, c0:c0 + npp], rhs=wd[:, kc, 0:512],
                                start=(gi == 0 and kc == 0), stop=(gi == 2 and kc == nk - 1)) for kc in range(nk)],
                                 r=(Bwd, B_actT), w=(B_psf[banks[bi]],))
                    for bi, (npp, c0) in enumerate(tblocks):
                        bk = banks[bi]
                        P.op("dve", lambda e, bk=bk, bi=bi, npp=npp, half=half: e.tensor_tensor(out=xtile[0:npp, bi, half * 512:(half + 1) * 512], in0=psf[bk][0:npp, :],
                                                                                               in1=xtile[0:npp, bi, half * 512:(half + 1) * 512], op=ALU.add),
                             r=(B_psf[bk], Bx), w=(Bx,))
                        ps_free(bk)
                if nxt is not None:
                    nb_, nr_ = norm1_blocks(nxt[0], nxt[1])
                    norm_apply(nb_, g1c, xnT, B_xnT, nr_, stat2, B_stat2)
                    pre_apply[0] = True
                P.op("dve", lambda e: e.memset(stat[:, 0:8], 0.0), w=(B_stat,))
                for bi, (npp, c0) in enumerate(tblocks):
                    P.op("act", lambda e, bi=bi, npp=npp: e.activation(out=junk[0:npp, :], in_=xtile[0:npp, bi, :], func=AF.Square, accum_out=stat[0:npp, bi:bi + 1]),
                         r=(Bx,), w=(B_junk, B_stat))
                rstd_from_ss(len(tblocks), D, 1e-6)
                for bi, (npp, c0) in enumerate(tblocks):
                    for hf, (yb, B_yb) in enumerate(((lnr, B_lnr), (kvout, B_kvout))):
                        P.op("dve", lambda e, bi=bi, npp=npp, hf=hf, yb=yb: e.scalar_tensor_tensor(
                            out=yb[0:npp, :], in0=xtile[0:npp, bi, hf * 512:(hf + 1) * 512], scalar=stat[0:npp, 16 + bi:17 + bi],
                            in1=gfin[0:npp, hf * 512:(hf + 1) * 512], op0=ALU.mult, op1=ALU.mult), r=(Bx, B_stat, B_const), w=(B_yb,))
                        if is_sample:
                            P.dma("sp", ys_o[:, hf * 512:(hf + 1) * 512], yb[0:64, :], r=(B_yb,), w=())
                        else:
                            P.dma("sp", y_own[own_i, bi * 128:(bi + 1) * 128, hf * 512:(hf + 1) * 512], yb[:, :], r=(B_yb,), w=())

            tile2_load(0, False, 1)
            g0 = tile2(0, False, 1, 0, (1, False, 3))
            fl0 = next(g0)

            def tick():
                for _ in range(2):
                    if fl0:
                        fl0.pop(0)()
            logf_section(tick)
            for _ in g0:
                pass
            for i in range(1, 4):
                nxt = (i + 1, False, 2 * i + 3) if i < 3 else (4, True, None)
                for _ in tile2(i, False, 2 * i + 1, i, nxt):
                    pass
            bg_step(100000)
            for _ in tile2(4, True, None, None, None):
                pass
            if not P.dry:
                P.finish()

        P.dry = True
        program()
        P.reset()
        psstate.update({"free": [True] * 6, "nxt": 0, "tn": 0})
        WS.start_real(); KS.start_real(); VS.start_real(); wl["n"] = 0
        program()
    return nc


_NC_CACHE = {}


def kernel(x_prompt, x_sample, cache_k, cache_v, cache_logf, state_conv, norm_mix_g, w_in, b_f, w_dw, b_dw, ln_g, ln_b,
           w_conv_pw, w_attn_o, w_out, norm_ffn_g, w_gate, w_up, w_down, final_norm_g):
    f = lambda a: np.ascontiguousarray(np.asarray(a, dtype=np.float32))
    x_prompt = f(x_prompt); x_sample = f(x_sample)
    cache_k = f(cache_k)[0].reshape(16, 4096, 512); cache_v = f(cache_v)[0].reshape(16, 4096, 512)
    cache_logf = f(cache_logf)[0]; state_conv = f(state_conv)[0]
    col = lambda v, n: np.ascontiguousarray(f(v).reshape(n, 128).T)
    shared = {
        "w_in": f(w_in)[0], "w_pw": f(w_conv_pw)[0], "w_ao": f(w_attn_o)[0], "w_out": f(w_out)[0],
        "w_gate": f(w_gate)[0], "w_up": f(w_up)[0], "w_down": f(w_down)[0],
        "g1c": col(norm_mix_g, 8), "g2c": col(norm_ffn_g, 8),
        "gfin": np.ascontiguousarray(np.broadcast_to(f(final_norm_g).reshape(1, D), (128, D))),
        "bfb": np.ascontiguousarray(np.broadcast_to(f(b_f).reshape(1, NH), (128, NH))),
        "wdw": np.ascontiguousarray(f(w_dw)[0].reshape(CW, 4, 128).transpose(2, 1, 0)),
        "bdw": col(b_dw, 4), "lng": col(ln_g, 4), "lnb": col(ln_b, 4),
        "ident": np.eye(128, dtype=np.float32), "tri": np.triu(np.ones((128, 128), np.float32)),
    }
    in_maps = []
    for c in range(8):
        b, p = c // 2, c % 2
        xc = x_prompt[b].reshape(8, 512, D)
        if p == 1:
            xl = xc
            km = np.ones((128, NLC), np.float32)
        else:
            xl = np.concatenate([np.zeros((1, 512, D), np.float32), xc[:7]], axis=0)
            km = np.ones((128, NLC), np.float32); km[:, 0] = 0.0
        m = dict(shared)
        m.update({"xloc": np.ascontiguousarray(xl), "xs": np.ascontiguousarray(x_sample[2 * c:2 * c + 2].reshape(64, D)),
                  "ck": np.ascontiguousarray(cache_k[2 * c:2 * c + 2]), "cv": np.ascontiguousarray(cache_v[2 * c:2 * c + 2]),
                  "clf": np.ascontiguousarray(cache_logf[2 * c:2 * c + 2]), "sconv": np.ascontiguousarray(state_conv[2 * c:2 * c + 2]),
                  "kmask": km})
        in_maps.append(m)
    if "nc" not in _NC_CACHE:
        _NC_CACHE["nc"] = build_nc()
    res = run_bass_kernel_spmd(_NC_CACHE["nc"], in_maps, core_ids=list(range(8)))
    R = res.results
    y_p = np.zeros((4, 4096, D), np.float32); k_p = np.zeros((1, 4, 4096, NH, HD), np.float32); v_p = np.zeros_like(k_p)
    lf_p = np.zeros((1, 4, 4096, NH), np.float32); cv_p = np.zeros((1, 4, 30, 512), np.float32)
    y_s = np.zeros((16, 32, D), np.float32); k_s = np.zeros((1, 16, 32, NH, HD), np.float32); v_s = np.zeros_like(k_s)
    lf_s = np.zeros((1, 16, 32, NH), np.float32); cv_s = np.zeros((1, 16, 30, 512), np.float32)
    for c in range(8):
        b, p = c // 2, c % 2
        r = R[c]
        for i in range(4):
            gch = 2 * i + 1 - (1 - p)
            sl = slice(gch * 512, (gch + 1) * 512)
            y_p[b, sl] = r["y_own"][i]
            k_p[0, b, sl] = r["k_own"][i].reshape(512, NH, HD)
            v_p[0, b, sl] = r["v_own"][i].reshape(512, NH, HD)
            lf_p[0, b, sl] = r["lf_own"][i]
        if p == 1:
            cv_p[0, b] = r["conv_own"]
        y_s[2 * c:2 * c + 2] = r["ys"].reshape(2, 32, D)
        k_s[0, 2 * c:2 * c + 2] = r["ks"].reshape(2, 32, NH, HD)
        v_s[0, 2 * c:2 * c + 2] = r["vs"].reshape(2, 32, NH, HD)
        lf_s[0, 2 * c:2 * c + 2] = r["lfs"].reshape(2, 32, NH)
        cv_s[0, 2 * c:2 * c + 2] = r["convs"]
    return (y_p, y_s, k_p, v_p, lf_p, cv_p, k_s, v_s, lf_s, cv_s)
```

```python
import numpy as np
from contextlib import ExitStack
import concourse.bass as bass
import concourse.mybir as mybir
from concourse.bass_utils import run_bass_kernel_spmd

F32 = mybir.dt.float32
BF16 = mybir.dt.bfloat16
AF = mybir.ActivationFunctionType
ALU = mybir.AluOpType

D = 1024; DIN = 4616; DFF = 2816; NH = 8; HD = 64; CW = 31
NLC = 8
NKC = 8 + 2 * 9
NDS = 64
NEGBIG = -30000.0


class Buf:
    __slots__ = ("w", "r")

    def __init__(self):
        self.w = {}
        self.r = {}


class Prog:
    def __init__(self, nc, es):
        self.nc = nc
        self.eng = {"pe": nc.tensor, "act": nc.scalar, "dve": nc.vector, "pool": nc.gpsimd, "sp": nc.sync}
        self.sem = {k: es.enter_context(nc.semaphore("s_" + k)) for k in self.eng}
        self.dsem = [es.enter_context(nc.semaphore("d%d" % i)) for i in range(NDS)]
        self.qrange = {"sp": (0, NDS - 16), "pool": (NDS - 16, NDS)}
        self.reset()

    def reset(self):
        self.cnt = {k: 0 for k in self.eng}
        self.seen = {k: {} for k in self.eng}
        self.dval = [0] * NDS
        self.dlast = [None] * NDS
        self.dnext = {"sp": 0, "pool": NDS - 16}
        self.dry = False
        self.nwait = 0

    def _wait(self, e, key, val):
        if key == ("e", "pe") and e == "pe":
            return
        s = self.seen[e]
        if s.get(key, 0) >= val:
            return
        s[key] = val
        sem = self.sem[key[1]] if key[0] == "e" else self.dsem[key[1]]
        self.eng[e].wait_ge(sem, val)
        self.nwait += 1

    def _deps(self, e, r, w):
        for b in r:
            for k, v in b.w.items():
                self._wait(e, k, v)
        for b in w:
            for k, v in b.w.items():
                self._wait(e, k, v)
            for k, v in b.r.items():
                self._wait(e, k, v)

    def _upd(self, key, val, r, w):
        for b in r:
            if b.r.get(key, 0) < val:
                b.r[key] = val
        for b in w:
            b.w = {key: val}
            b.r = {}

    def op(self, e, fn, r=(), w=()):
        if self.dry:
            return
        self._deps(e, r, w)
        ins = fn(self.eng[e])
        self.cnt[e] += 1
        ins.then_inc(self.sem[e], 1)
        self._upd(("e", e), self.cnt[e], r, w)

    def mm(self, fns, r=(), w=()):
        if self.dry:
            return
        self._deps("pe", r, w)
        ins = None
        for fn in fns:
            ins = fn(self.eng["pe"])
        self.cnt["pe"] += 1
        ins.then_inc(self.sem["pe"], 1)
        self._upd(("e", "pe"), self.cnt["pe"], r, w)

    def dma(self, q, out, in_, r=(), w=()):
        if self.dry:
            return
        for b in r:
            for k, v in b.w.items():
                self._wait(q, k, v)
        for b in w:
            for k, v in b.w.items():
                if k[0] != "d":
                    self._wait(q, k, v)
            for k, v in b.r.items():
                self._wait(q, k, v)
        i = self.dnext[q]
        lo, hi = self.qrange[q]
        self.dnext[q] = lo + (i + 1 - lo) % (hi - lo)
        if self.dlast[i] is not None:
            self._wait(q, ("d", i), self.dlast[i])
        ins = self.eng[q].dma_start(out=out, in_=in_)
        self.dval[i] += 16
        ins.then_inc(self.dsem[i], 16)
        self.dlast[i] = self.dval[i]
        key = ("d", i)
        for b in r:
            if b.r.get(key, 0) < self.dval[i]:
                b.r[key] = self.dval[i]
        for b in w:
            keep = {k: v for k, v in b.w.items() if k[0] == "d"}
            keep[key] = self.dval[i]
            b.w = keep
            b.r = {}

    def finish(self):
        for i in range(NDS):
            if self.dlast[i] is not None:
                self._wait("sp", ("d", i), self.dlast[i])


class Stream:
    def __init__(self, P, slots, depth):
        self.P = P
        self.slots = slots
        self.depth = depth
        self.rec = []
        self.issued = 0
        self.req = 0

    def start_real(self):
        self.issued = 0
        self.req = 0

    def get(self, loader, depth=None):
        depth = self.depth if depth is None else depth
        P = self.P
        n = len(self.slots)
        if P.dry:
            self.rec.append(loader)
            i = len(self.rec) - 1
            return self.slots[i % n]
        i = self.req
        self.req += 1
        lim = min(len(self.rec), i + 1 + depth)
        while self.issued < lim:
            j = self.issued
            t, b = self.slots[j % n]
            self.rec[j](t, b)
            self.issued += 1
        return self.slots[i % n]


def build_nc():
    nc = bass.Bass("TRN2", target_bir_lowering=False)

    def din(name, shape, dt=F32):
        return nc.dram_tensor(name, list(shape), dt, kind="ExternalInput").ap()

    def dout(name, shape, dt=F32):
        return nc.dram_tensor(name, list(shape), dt, kind="ExternalOutput").ap()

    xloc = din("xloc", [NLC, 512, D])
    xs = din("xs", [64, D])
    ck = din("ck", [2, 4096, 512]); cv = din("cv", [2, 4096, 512]); clf = din("clf", [2, 4096, NH])
    sconv = din("sconv", [2, 30, 512])
    kmask_d = din("kmask", [128, NLC])
    w_in = din("w_in", [D, DIN]); w_pw = din("w_pw", [512, D]); w_ao = din("w_ao", [512, D]); w_out = din("w_out", [D, D])
    w_gate = din("w_gate", [D, DFF]); w_up = din("w_up", [D, DFF]); w_down = din("w_down", [DFF, D])
    g1c_d = din("g1c", [128, 8]); g2c_d = din("g2c", [128, 8]); gfin_d = din("gfin", [128, D]); bf_d = din("bfb", [128, NH])
    wdw_d = din("wdw", [128, 4, CW]); bdw_d = din("bdw", [128, 4]); lng_d = din("lng", [128, 4]); lnb_d = din("lnb", [128, 4])
    ident_d = din("ident", [128, 128]); tri_d = din("tri", [128, 128])

    y_own = dout("y_own", [4, 512, D]); k_own = dout("k_own", [4, 512, 512]); v_own = dout("v_own", [4, 512, 512])
    lf_own = dout("lf_own", [4, 512, NH]); conv_own = dout("conv_own", [30, 512])
    ys_o = dout("ys", [64, D]); ks_o = dout("ks", [64, 512]); vs_o = dout("vs", [64, 512]); lfs_o = dout("lfs", [64, NH])
    convs_o = dout("convs", [2, 30, 512])

    KT = nc.dram_tensor("KT", [NKC, 70, NH, 512], BF16, kind="Internal").ap()
    VA = nc.dram_tensor("VA", [NKC, 128, NH, 4, 65], BF16, kind="Internal").ap()

    with ExitStack() as es:
        def sb(name, shape, dt=F32):
            return es.enter_context(nc.sbuf_tensor("sb_" + name, list(shape), dt))

        P = Prog(nc, es)
        ident_f = sb("ident_f", [128, 128]); ident_b = sb("ident_b", [128, 128], BF16)
        tri_f = sb("tri_f", [128, 128]); tri_b = sb("tri_b", [128, 128], BF16)
        ones_f = sb("ones_f", [128, 128])
        negones = sb("negones", [3, 512], BF16)
        g1c = sb("g1c", [128, 8]); g2c = sb("g2c", [128, 8]); gfin = sb("gfin", [128, D]); bfb = sb("bfb", [128, NH])
        wdw = sb("wdw", [128, 4, CW]); bdw = sb("bdw", [128, 4]); lng = sb("lng", [128, 4]); lnb = sb("lnb", [128, 4])
        lngh = sb("lngh", [128, 4]); lnbh = sb("lnbh", [128, 4])
        kmask = sb("kmask", [128, NLC]); kmb = sb("kmb", [128, NLC])
        B_const = Buf()

        xt = [sb("xt%d" % i, [128, 4, D]) for i in range(2)]
        B_xt = [Buf(), Buf()]
        xnb_ring = [(sb("xnb%d" % i, [128, D], BF16), Buf()) for i in range(2)]
        xnb_state = [0]
        junk = sb("junk", [128, D], BF16); B_junk = Buf()
        stat = sb("stat", [128, 32]); B_stat = Buf()
        stat2 = sb("stat2", [128, 32]); B_stat2 = Buf()
        st_sel = [(stat, B_stat), (stat2, B_stat2)]
        xnT = sb("xnT", [128, 8, 544], BF16); B_xnT = Buf()

        wslots = [(sb("wsl%d" % i, [128, 8, 512], BF16), Buf()) for i in range(6)]
        WS = Stream(P, wslots, 2)
        actT = sb("actT", [128, 22, 512], BF16); B_actT = Buf()
        wk_t = actT[:, 0:8, :]; wv_t = actT[:, 8:16, :]; wf_t = sb("wf_t", [128, 8, NH], BF16)
        B_wkv = B_actT

        aoT = sb("aoT", [64, NH, 512], BF16); B_aoT = Buf()
        ktst = aoT; B_ktst = B_aoT
        vaug = sb("vaug", [128, NH, 4, 65], BF16); B_vaug = Buf()
        kvout = sb("kvout", [128, 512]); B_kvout = Buf()
        sT = sb("sT", [128, 4, 512], BF16); B_sT = Buf()
        ckb = sT; B_ckb = B_sT

        zp = sb("zp", [128, 32, NH]); B_zp = Buf()
        lfp = sb("lfp", [128, 32, NH]); B_lfp = Buf()
        zs = sb("zs", [128, NH]); B_zs = Buf()
        lfs = sb("lfs", [128, NH]); B_lfs = Buf()
        lfc = sb("lfc", [128, 33, NH]); B_lfc = Buf()
        ctmp = sb("ctmp", [128, 33, NH]); B_ctmp = Buf()
        coff = sb("coff", [128, 33, NH]); B_coff = Buf()
        ctot = sb("ctot", [128, 33, NH]); B_ctot = Buf()
        cres = sb("cres", [128, 33, NH]); B_cres = Buf()
        spl = sb("spl", [128, 33, 3, NH], BF16); B_spl = Buf()
        rowst = sb("rowst", [24, 512], BF16); B_rowst = Buf()

        cuT = sb("cuT", [128, 4, 544]); B_cuT = Buf()
        cacc = sb("cacc", [128, 4, 512]); B_cacc = Buf()
        xhalo = cacc[:, :, :].rearrange("p a b -> p (a b)")[0:32, 0:D]; B_xhalo = B_cacc
        thc = sb("thc", [128, 512]); B_thc = Buf()
        tg = thc; B_tg = B_thc
        csq = thc; B_csq = B_thc
        m1 = sb("m1", [128, 512]); B_m1 = Buf()
        lnm = m1; B_lnm = B_m1
        lnr = sb("lnr", [128, 512]); B_lnr = Buf()
        mixT = sb("mixT", [128, 8, 512], BF16); B_mixT = Buf()
        qT = mixT[0:70, :, :]; B_qT = B_mixT; B_qTaug = B_mixT
        vsl = [(sb("vsl%d" % i, [128, 4, 65], BF16), Buf()) for i in range(6)]
        VS = Stream(P, vsl, 3)
        ktsl = [(sb("ktsl%d" % i, [70, 512], BF16), Buf()) for i in range(6)]
        KS = Stream(P, ktsl, 3)
        NPT = 5
        pT = [sb("pT%d" % i, [128, 512], BF16) for i in range(NPT)]; B_pT = [Buf() for _ in range(NPT)]
        oc = [sb("oc%d" % i, [65, 512]) for i in range(2)]; B_oc = [Buf(), Buf()]
        rinv = sb("rinv", [64, 512]); B_rinv = Buf()
        zrow = sb("zrow", [1, 512], BF16)
        zT = xnT[:, :, 0:512]; B_zT = B_xnT
        yout = cuT[:, :, :].rearrange("p a b -> p (a b)")[:, 0:D]; B_yout = B_cuT
        chst = sb("chst", [32, 512]); B_chst = Buf()
        schist = chst; B_schist = B_chst

        pst = [es.enter_context(nc.psum_tensor("pst%d" % i, [128, 1024], BF16)) for i in range(2)]
        B_pst = [Buf(), Buf()]
        psf = [es.enter_context(nc.psum_tensor("psf%d" % i, [128, 512], F32)) for i in range(6)]
        B_psf = [Buf() for _ in range(6)]
        psstate = {"free": [True] * 6, "nxt": 0, "tn": 0}

        def ps_get():
            for _ in range(6):
                i = psstate["nxt"]
                psstate["nxt"] = (i + 1) % 6
                if psstate["free"][i]:
                    psstate["free"][i] = False
                    return i
            raise RuntimeError("psum exhausted")

        def ps_free(i):
            psstate["free"][i] = True

        def pst_get():
            i = psstate["tn"]
            psstate["tn"] = 1 - i
            return i

        wl = {"n": 0, "flags": [], "scr": {}}

        def wload_generic(key, shape, src, part):
            if P.dry:
                first = key not in wl["scr"]
                if first:
                    wl["scr"][key] = (nc.dram_tensor("wsc%d" % len(wl["scr"]), list(shape), BF16, kind="Internal").ap(), Buf())
                wl["flags"].append(first)
            else:
                first = wl["flags"][wl["n"]]
                wl["n"] += 1
            scr, Bscr = wl["scr"][key]

            def view(t):
                return t[0:part, 0:shape[1], 0:shape[2]]
            if first:
                def loader(t, b):
                    P.dma("pool", view(t), src, r=(), w=(b,))
            else:
                def loader(t, b):
                    P.dma("sp", view(t), scr, r=(Bscr,), w=(b,))
            t, b = WS.get(loader)
            if first:
                P.dma("sp", scr, view(t), r=(b,), w=(Bscr,))
            return t, b

        def wload(W, k0, nk, c0, ncols, wn=None):
            src = W[k0 * 128:(k0 + nk) * 128, c0:c0 + ncols].rearrange("(k p) n -> p k n", p=128)
            return wload_generic((W.tensor.name, k0, nk, c0, ncols), [128, nk, ncols], src, 128)

        def rstd_from_ss(ncol, nfeat, eps, st=None, B_st=None):
            st = stat if st is None else st
            B_st = B_stat if B_st is None else B_st
            P.op("dve", lambda e: e.tensor_scalar(out=st[:, 8:8 + ncol], in0=st[:, 0:ncol], scalar1=1.0 / nfeat, scalar2=eps,
                                                 op0=ALU.mult, op1=ALU.add), r=(B_st,), w=(B_st,))
            P.op("act", lambda e: e.activation(out=st[:, 24:24 + ncol], in_=st[:, 8:8 + ncol], func=AF.Sqrt), r=(B_st,), w=(B_st,))
            P.op("dve", lambda e: e.reciprocal(out=st[:, 16:16 + ncol], in_=st[:, 24:24 + ncol]), r=(B_st,), w=(B_st,))

        def norm_stats(blocks, extra_r=(), st=None, B_st=None):
            st = stat if st is None else st
            B_st = B_stat if B_st is None else B_st
            nb = len(blocks)
            P.op("dve", lambda e: e.memset(st[:, 0:8], 0.0), w=(B_st,))
            for bi, (src, npp, c0) in enumerate(blocks):
                P.op("act", lambda e, src=src, npp=npp, bi=bi: e.activation(out=junk[0:npp, :], in_=src, func=AF.Square,
                                                                            accum_out=st[0:npp, bi:bi + 1]),
                     r=extra_r, w=(B_junk, B_st))
            rstd_from_ss(nb, D, 1e-6, st, B_st)

        def norm_apply(blocks, gcol, dstT, B_dstT, extra_r=(), st=None, B_st=None):
            st = stat if st is None else st
            B_st = B_stat if B_st is None else B_st
            for bi, (src, npp, c0) in enumerate(blocks):
                xnb, B_xnb = xnb_ring[xnb_state[0]]; xnb_state[0] = 1 - xnb_state[0]
                P.op("act", lambda e, src=src, npp=npp, bi=bi, xnb=xnb: e.activation(out=xnb[0:npp, :], in_=src, func=AF.Copy, scale=st[0:npp, 16 + bi:17 + bi]),
                     r=extra_r + (B_st,), w=(B_xnb,))
                ti = pst_get()
                pv = pst[ti][:, :].rearrange("p (k t) -> p k t", k=8)
                P.mm([lambda e, kc=kc, npp=npp, pv=pv, xnb=xnb: e.transpose(out=pv[:, kc, 0:npp], in_=xnb[0:npp, kc * 128:(kc + 1) * 128],
                                                                   identity=ident_b[0:npp, 0:npp]) for kc in range(8)],
                     r=(B_xnb, B_const), w=(B_pst[ti],))
                P.op("dve", lambda e, pv=pv, npp=npp, c0=c0: e.tensor_tensor(out=dstT[:, :, c0:c0 + npp], in0=pv[:, :, 0:npp],
                                                                            in1=gcol[:, 0:8].unsqueeze(2).to_broadcast([128, 8, npp]), op=ALU.mult),
                     r=(B_pst[ti], B_const), w=(B_dstT,))

        def norm_T(blocks, gcol, dstT, B_dstT, extra_r=(), st=None, B_st=None):
            norm_stats(blocks, extra_r, st, B_st)
            norm_apply(blocks, gcol, dstT, B_dstT, extra_r, st, B_st)

        def mm_fm(wt, B_wt, nk, wc0, M, actTile, B_act, ac0, N, bank):
            P.mm([lambda e, kc=kc: e.matmul(psf[bank][0:M, 0:N], lhsT=wt[:, kc, wc0:wc0 + M], rhs=actTile[:, kc, ac0:ac0 + N],
                                             start=(kc == 0), stop=(kc == nk - 1)) for kc in range(nk)],
                 r=(B_wt, B_act), w=(B_psf[bank],))

        def mm_tm(actTile, B_act, nk, ac0, M, wt, B_wt, wc0, N, bank, first=True, last=True, kp=128):
            P.mm([lambda e, kc=kc: e.matmul(psf[bank][0:M, 0:N], lhsT=actTile[0:kp, kc, ac0:ac0 + M], rhs=wt[0:kp, kc, wc0:wc0 + N],
                                             start=(first and kc == 0), stop=(last and kc == nk - 1)) for kc in range(nk)],
                 r=(B_wt, B_act), w=(B_psf[bank],))

        def program():
            for bi in range(4):
                P.dma("sp", xt[0][:, bi, :], xloc[0, bi * 128:(bi + 1) * 128, :], w=(B_xt[0],))
            P.dma("pool", wk_t, w_in[:, 1536:2048].rearrange("(k p) n -> p k n", p=128), w=(B_wkv,))
            for t, d in ((ident_f, ident_d), (tri_f, tri_d), (g1c, g1c_d), (g2c, g2c_d), (gfin, gfin_d), (bfb, bf_d),
                         (bdw, bdw_d), (lng, lng_d), (lnb, lnb_d), (kmask, kmask_d)):
                P.dma("sp", t[:], d, w=(B_const,))
            P.dma("sp", wdw[:], wdw_d, w=(B_const,))
            P.op("dve", lambda e: e.tensor_copy(out=ident_b[:], in_=ident_f[:]), r=(B_const,), w=(B_const,))
            P.op("dve", lambda e: e.tensor_copy(out=tri_b[:], in_=tri_f[:]), r=(B_const,), w=(B_const,))
            P.op("dve", lambda e: e.memset(ones_f[:], 1.0), w=(B_const,))
            P.op("dve", lambda e: e.memset(negones[:], -1.0), w=(B_const,))
            P.op("dve", lambda e: e.tensor_scalar(out=lngh[:], in0=lng[:], scalar1=0.5, scalar2=None, op0=ALU.mult), r=(B_const,), w=(B_const,))
            P.op("dve", lambda e: e.tensor_scalar(out=lnbh[:], in0=lnb[:], scalar1=0.5, scalar2=None, op0=ALU.mult), r=(B_const,), w=(B_const,))
            P.op("dve", lambda e: e.tensor_scalar(out=kmb[:], in0=kmask[:], scalar1=-1.0, scalar2=-NEGBIG, op0=ALU.add, op1=ALU.mult),
                 r=(B_const,), w=(B_const,))
            P.op("dve", lambda e: e.memset(vaug[:], 1.0), w=(B_vaug,))
            P.op("dve", lambda e: e.memset(zrow[:], 0.0), w=(B_const,))
            P.op("dve", lambda e: e.memset(zs[:], 0.0), w=(B_zs,))
            B_KT = [Buf() for _ in range(NKC)]
            B_KTa = [Buf() for _ in range(NKC)]
            B_KTn = [Buf() for _ in range(NKC)]
            B_VA = [Buf() for _ in range(NKC)]
            P.dma("pool", wv_t, w_in[:, 2048:2560].rearrange("(k p) n -> p k n", p=128), w=(B_wkv,))
            P.dma("pool", wf_t[:], w_in[:, 2560:2568].rearrange("(k p) n -> p k n", p=128), w=(B_wkv,))

            def phase1_load(ti, src_blocks):
                xtile = xt[ti % 2]; Bx = B_xt[ti % 2]
                for bi, (src, npp) in enumerate(src_blocks):
                    P.dma("sp", xtile[0:npp, bi, :], src, w=(Bx,))

            xnT2 = cuT[:, :, :].rearrange("p a b -> p (a b)").bitcast(BF16).rearrange("p (k t) -> p k t", k=8)
            xn_sel = [(xnT, B_xnT), (xnT2, B_cuT)]

            def phase1_stats(ti, src_blocks, kc_idx, own_i, is_sample):
                xtile = xt[ti % 2]; Bx = B_xt[ti % 2]
                st, B_st = st_sel[ti % 2]
                norm_stats([(xtile[0:npp, bi, :], npp, bi * 128) for bi, (src, npp) in enumerate(src_blocks)], (Bx,), st, B_st)

            def phase1_norm(ti, src_blocks, kc_idx, own_i, is_sample):
                xtile = xt[ti % 2]; Bx = B_xt[ti % 2]
                xnT, B_xnT = xn_sel[ti % 2]
                st, B_st = st_sel[ti % 2]
                norm_apply([(xtile[0:npp, bi, :], npp, bi * 128) for bi, (src, npp) in enumerate(src_blocks)], g1c, xnT, B_xnT, (Bx,), st, B_st)

            def phase1_tile(ti, src_blocks, kc_idx, own_i, is_sample, mid):
                xtile = xt[ti % 2]; Bx = B_xt[ti % 2]
                xnT, B_xnT = xn_sel[ti % 2]
                nb = len(src_blocks)
                NT = sum(b[1] for b in src_blocks)
                for g in range(4):
                    bk = ps_get()
                    mm_fm(wk_t, B_wkv, 8, g * 128, 128, xnT, B_xnT, 0, NT, bk)
                    P.op("act", lambda e, g=g, bk=bk: e.copy(out=ktst[0:64, 2 * g, 0:NT], in_=psf[bk][0:64, 0:NT]), r=(B_psf[bk],), w=(B_ktst,))
                    P.op("dve", lambda e, g=g, bk=bk: e.tensor_copy(out=ktst[0:64, 2 * g + 1, 0:NT], in_=psf[bk][64:128, 0:NT]), r=(B_psf[bk],), w=(B_ktst,))
                    ps_free(bk)
                    cstep()
                if not is_sample:
                    P.dma("sp", KT[kc_idx, 0:64, :, :], ktst[:, :, :], r=(B_ktst,), w=(B_KT[kc_idx],))
                else:
                    for s in range(2):
                        kcn = 8 + 9 * s + 8
                        P.dma("sp", KT[kcn, 0:64, :, 0:32], ktst[:, :, 32 * s:32 * s + 32], r=(B_ktst,), w=(B_KT[kcn],))
                mid()
                for bi, (src, npp) in enumerate(src_blocks):
                    if own_i is not None or is_sample:
                        bk = ps_get()
                        mm_tm(xnT, B_xnT, 8, bi * 128, npp, wk_t, B_wkv, 0, 512, bk)
                        P.op("act", lambda e, bk=bk, npp=npp, bi=bi: e.copy(out=kvout[0:npp, :], in_=psf[bk][0:npp, :]), r=(B_psf[bk],), w=(B_kvout,))
                        ps_free(bk)
                        cstep()
                        if own_i is not None:
                            P.dma("sp", k_own[own_i, bi * 128:(bi + 1) * 128, :], kvout[:, :], r=(B_kvout,), w=())
                        else:
                            P.dma("sp", ks_o[:, :], kvout[0:64, :], r=(B_kvout,), w=())
                for bi, (src, npp) in enumerate(src_blocks):
                    bk = ps_get()
                    mm_tm(xnT, B_xnT, 8, bi * 128, npp, wv_t, B_wkv, 0, 512, bk)
                    P.op("dve", lambda e, bk=bk, npp=npp, bi=bi: e.tensor_copy(out=vaug[0:npp, :, bi, 0:64],
                                                                               in_=psf[bk][0:npp, :].rearrange("p (h d) -> p h d", h=NH)),
                         r=(B_psf[bk],), w=(B_vaug,))
                    if own_i is not None or is_sample:
                        P.op("act", lambda e, bk=bk, npp=npp, bi=bi: e.copy(out=kvout[0:npp, :], in_=psf[bk][0:npp, :]), r=(B_psf[bk],), w=(B_kvout,))
                        if own_i is not None:
                            P.dma("sp", v_own[own_i, bi * 128:(bi + 1) * 128, :], kvout[:, :], r=(B_kvout,), w=())
                        else:
                            P.dma("sp", vs_o[:, :], kvout[0:64, :], r=(B_kvout,), w=())
                    ps_free(bk)
                    cstep()
                if not is_sample:
                    P.dma("sp", VA[kc_idx].rearrange("p h b d -> p (h b d)"), vaug[:, :, :, :].rearrange("p h b d -> p (h b d)"), r=(B_vaug,), w=(B_VA[kc_idx],))
                else:
                    for s in range(2):
                        kcn = 8 + 9 * s + 8
                        P.dma("sp", VA[kcn, 0:32, :, 0, :], vaug[32 * s:32 * s + 32, :, 0, :], r=(B_vaug,), w=(B_VA[kcn],))
                for bi, (src, npp) in enumerate(src_blocks):
                    bk = ps_get()
                    P.mm([lambda e, kc=kc, bi=bi, npp=npp, bk=bk: e.matmul(psf[bk][0:npp, 0:NH], lhsT=xnT[:, kc, bi * 128:bi * 128 + npp], rhs=wf_t[:, kc, :],
                                                                          start=(kc == 0), stop=(kc == 7)) for kc in range(8)],
                         r=(B_wkv, B_xnT), w=(B_psf[bk],))
                    if not is_sample:
                        P.op("dve", lambda e, bk=bk, bi=bi: e.tensor_tensor(out=zp[:, kc_idx * 4 + bi, :], in0=psf[bk][:, 0:NH], in1=bfb[:, :], op=ALU.add),
                             r=(B_psf[bk], B_const), w=(B_zp,))
                    else:
                        P.op("dve", lambda e, bk=bk: e.tensor_tensor(out=zs[0:64, :], in0=psf[bk][0:64, 0:NH], in1=bfb[0:64, :], op=ALU.add),
                             r=(B_psf[bk], B_const), w=(B_zs,))
                    ps_free(bk)
                    cstep()

            ktst2_ring = [(mixT[0:64, :, :], B_mixT),
                          (cacc[:, :, :].rearrange("p a b -> p (a b)").bitcast(BF16)[0:64, :].rearrange("p (h t) -> p h t", h=NH), B_cacc)]
            kt2_state = [0]
            vaug2 = actT[:, 16:21, :].rearrange("p a b -> p (a b)")[:, 0:NH * 4 * 65].rearrange("p (h b d) -> p h b d", h=NH, b=4)
            B_vaug2 = Buf()
            P.op("dve", lambda e: e.memset(vaug2, 1.0), w=(B_vaug2,))

            def cache_conv(s, cc):
                kcn = 8 + 9 * s + cc

                def kl(t, b):
                    P.dma("pool", t[:, 0:4, :], ck[s, cc * 512:(cc + 1) * 512, :].rearrange("(b p) n -> p b n", p=128), w=(b,))

                def vl(t, b):
                    P.dma("pool", t[:, 0:4, :], cv[s, cc * 512:(cc + 1) * 512, :].rearrange("(b p) n -> p b n", p=128), w=(b,))
                tk, Bk = WS.get(kl, depth=4)
                ktst2, B_kt2 = ktst2_ring[kt2_state[0]]; kt2_state[0] = 1 - kt2_state[0]
                for bi in range(4):
                    ti = pst_get()
                    pv = pst[ti][:, 0:512].rearrange("p (g t) -> p g t", g=4)
                    P.mm([lambda e, g=g, bi=bi, pv=pv: e.transpose(out=pv[:, g, :], in_=tk[:, bi, g * 128:(g + 1) * 128], identity=ident_b[:, :])
                          for g in range(4)], r=(Bk, B_const), w=(B_pst[ti],))
                    kv = ktst2[:, :, bi * 128:(bi + 1) * 128].rearrange("p (g e) t -> p g e t", e=2)
                    P.op("act", lambda e, pv=pv, kv=kv: e.copy(out=kv[:, :, 0, :], in_=pv[0:64, :, :]), r=(B_pst[ti],), w=(B_kt2,))
                    P.op("dve", lambda e, pv=pv, kv=kv: e.tensor_copy(out=kv[:, :, 1, :], in_=pv[64:128, :, :]), r=(B_pst[ti],), w=(B_kt2,))
                    yield
                P.dma("sp", KT[kcn, 0:64, :, :], ktst2, r=(B_kt2,), w=(B_KT[kcn],))
                tv, Bv = WS.get(vl, depth=4)
                P.op("act", lambda e: e.copy(out=vaug2[:, :, :, 0:64].rearrange("p h b d -> p b h d"), in_=tv[:, 0:4, :].rearrange("p b (h d) -> p b h d", h=NH)),
                     r=(Bv,), w=(B_vaug2,))
                P.dma("sp", VA[kcn].rearrange("p h b d -> p (h b d)"), vaug2.rearrange("p h b d -> p (h b d)"), r=(B_vaug2,), w=(B_VA[kcn],))
                yield

            cgen = [None]

            def cstep():
                while True:
                    if cgen[0] is None:
                        if not cconv:
                            return
                        cgen[0] = cache_conv(*cconv.pop(0))
                    try:
                        next(cgen[0])
                        return
                    except StopIteration:
                        cgen[0] = None

            p1tiles = [(l, [(xloc[l, bi * 128:(bi + 1) * 128, :], 128) for bi in range(4)], l, (l // 2) if (l % 2 == 1) else None, False) for l in range(NLC)]
            p1tiles.append((NLC, [(xs[:, :], 64)], None, None, True))
            phase1_load(1, p1tiles[1][1])
            for kc in range(NKC):
                P.dma("sp", KT[kc, 67:70, :, :], negones[:, :].unsqueeze(1).to_broadcast([3, NH, 512]), r=(B_const,), w=(B_KTn[kc],))
            phase1_stats(*p1tiles[0])
            phase1_norm(*p1tiles[0])
            cconv = [(s_, c_) for s_ in range(2) for c_ in range(8)]
            for ti_, tl in enumerate(p1tiles):
                def mid(ti_=ti_):
                    if ti_ + 1 < len(p1tiles):
                        phase1_norm(*p1tiles[ti_ + 1])
                    if ti_ + 2 < len(p1tiles):
                        phase1_load(ti_ + 2, p1tiles[ti_ + 2][1])
                if ti_ + 1 < len(p1tiles):
                    phase1_stats(*p1tiles[ti_ + 1])
                phase1_tile(*tl, mid)
            while cconv or cgen[0] is not None:
                cstep()
            B_actT.r.update(B_vaug2.r); B_actT.r.update(B_vaug2.w)

            tick_on = [True]
            bg_gen = [None]

            def bg_step(n):
                for _ in range(n):
                    if bg_gen[0] is not None:
                        try:
                            next(bg_gen[0])
                        except StopIteration:
                            bg_gen[0] = None

            def logf_section(tick):
                def dv(fn, **kw):
                    P.op("dve", fn, **kw)
                    if tick_on[0]:
                        tick()
                P.op("act", lambda e: e.activation(out=lfp[:], in_=zp[:], func=AF.Exp, scale=-1.0), r=(B_zp,), w=(B_lfp,))
                P.op("act", lambda e: e.activation(out=lfs[:], in_=zs[:], func=AF.Exp, scale=-1.0), r=(B_zs,), w=(B_lfs,))
                P.op("act", lambda e: e.activation(out=lfp[:], in_=lfp[:], func=AF.Ln, bias=1.0), r=(B_lfp,), w=(B_lfp,))
                P.op("act", lambda e: e.activation(out=lfs[:], in_=lfs[:], func=AF.Ln, bias=1.0), r=(B_lfs,), w=(B_lfs,))
                for l in range(NLC):
                    dv(lambda e, l=l: e.tensor_scalar(out=lfp[:, 4 * l:4 * l + 4, :], in0=lfp[:, 4 * l:4 * l + 4, :], scalar1=kmask[:, l:l + 1],
                                                              scalar2=None, op0=ALU.mult), r=(B_lfp, B_const), w=(B_lfp,))
                dv(lambda e: e.tensor_scalar(out=lfp[:], in0=lfp[:], scalar1=-1.0, scalar2=None, op0=ALU.mult), r=(B_lfp,), w=(B_lfp,))
                dv(lambda e: e.tensor_scalar(out=lfs[:], in0=lfs[:], scalar1=-1.0, scalar2=None, op0=ALU.mult), r=(B_lfs,), w=(B_lfs,))
                for i in range(4):
                    l = 2 * i + 1
                    P.dma("sp", lf_own[i].rearrange("(b p) h -> p b h", p=128), lfp[:, 4 * l:4 * l + 4, :], r=(B_lfp,), w=())
                P.dma("sp", lfs_o[:, :], lfs[0:64, :], r=(B_lfs,), w=())

                def cumsum_rows(LF, B_LF, NB, kc_list, maskcol):
                    b1 = ps_get(); b2 = ps_get()
                    lf2 = LF[:, 0:NB, :].rearrange("p b h -> p (b h)")
                    P.mm([lambda e: e.matmul(psf[b1][:, 0:NB * NH], lhsT=tri_f[:, :], rhs=lf2, start=True, stop=True)], r=(B_LF, B_const), w=(B_psf[b1],))
                    yield
                    P.mm([lambda e: e.matmul(psf[b2][:, 0:NB * NH], lhsT=ones_f[:, :], rhs=lf2, start=True, stop=True)], r=(B_LF, B_const), w=(B_psf[b2],))
                    yield
                    P.op("act", lambda e: e.copy(out=ctot[:, 0:NB, :].rearrange("p b h -> p (b h)"), in_=psf[b2][:, 0:NB * NH]), r=(B_psf[b2],), w=(B_ctot,))
                    yield
                    ps_free(b2)
                    dv(lambda e: e.memset(coff[:, 0, :], 0.0), w=(B_coff,))
                    yield
                    dv(lambda e: e.tensor_copy(out=coff[:, 1:NB, :], in_=ctot[:, 0:NB - 1, :]), r=(B_ctot,), w=(B_coff,))
                    yield
                    bufs = [(coff, B_coff), (ctot, B_ctot)]
                    for si, sh in enumerate([1, 2, 4, 8, 16, 32]):
                        (src, Bs), (dst, Bd) = bufs[si % 2], bufs[(si + 1) % 2]
                        m = min(sh, NB)
                        dv(lambda e, src=src, dst=dst, m=m: e.tensor_copy(out=dst[:, 0:m, :], in_=src[:, 0:m, :]), r=(Bs,), w=(Bd,))
                        yield
                        if sh < NB:
                            dv(lambda e, src=src, dst=dst, sh=sh: e.tensor_tensor(out=dst[:, sh:NB, :], in0=src[:, sh:NB, :], in1=src[:, 0:NB - sh, :], op=ALU.add),
                                 r=(Bs,), w=(Bd,))
                            yield
                    dv(lambda e: e.tensor_tensor(out=ctmp[:, 0:NB, :].rearrange("p b h -> p (b h)"), in0=psf[b1][:, 0:NB * NH],
                                                          in1=coff[:, 0:NB, :].rearrange("p b h -> p (b h)"), op=ALU.add), r=(B_psf[b1], B_coff), w=(B_ctmp,))
                    yield
                    ps_free(b1)
                    dv(lambda e: e.tensor_scalar(out=cres[:, 0:NB, :], in0=ctmp[:, 0:NB, :], scalar1=-1.0, scalar2=None, op0=ALU.mult),
                         r=(B_ctmp,), w=(B_cres,))
                    yield
                    if maskcol:
                        for l in range(NLC):
                            dv(lambda e, l=l: e.tensor_scalar(out=cres[:, 4 * l:4 * l + 4, :], in0=cres[:, 4 * l:4 * l + 4, :], scalar1=kmb[:, l:l + 1],
                                                                      scalar2=None, op0=ALU.add), r=(B_cres, B_const), w=(B_cres,))
                            yield
                    dv(lambda e: e.tensor_copy(out=spl[:, 0:NB, 0, :], in_=cres[:, 0:NB, :]), r=(B_cres,), w=(B_spl,))
                    yield
                    dv(lambda e: e.tensor_tensor(out=ctmp[:, 0:NB, :], in0=cres[:, 0:NB, :], in1=spl[:, 0:NB, 0, :], op=ALU.subtract),
                         r=(B_cres, B_spl), w=(B_ctmp,))
                    yield
                    dv(lambda e: e.tensor_copy(out=spl[:, 0:NB, 1, :], in_=ctmp[:, 0:NB, :]), r=(B_ctmp,), w=(B_spl,))
                    yield
                    dv(lambda e: e.tensor_tensor(out=cres[:, 0:NB, :], in0=ctmp[:, 0:NB, :], in1=spl[:, 0:NB, 1, :], op=ALU.subtract),
                         r=(B_ctmp, B_spl), w=(B_cres,))
                    yield
                    dv(lambda e: e.tensor_copy(out=spl[:, 0:NB, 2, :], in_=cres[:, 0:NB, :]), r=(B_cres,), w=(B_spl,))
                    yield
                    for ci, kc in enumerate(kc_list):
                        nbk = min(4, NB - 4 * ci)
                        ti = pst_get()
                        P.mm([lambda e, bi=bi, ci=ci, ti=ti: e.transpose(out=pst[ti][0:24, bi * 128:(bi + 1) * 128],
                                                                          in_=spl[:, 4 * ci + bi, :, :].rearrange("p j h -> p (j h)"), identity=ident_b[:, :])
                              for bi in range(nbk)], r=(B_spl, B_const), w=(B_pst[ti],))
                        yield
                        P.op("act", lambda e, ti=ti, nbk=nbk: e.copy(out=rowst[:, 0:nbk * 128], in_=pst[ti][0:24, 0:nbk * 128]), r=(B_pst[ti],), w=(B_rowst,))
                        yield
                        ncol = 512 if nbk == 4 else 32
                        P.dma("sp", KT[kc, 64:67, :, 0:ncol].rearrange("j h k -> (j h) k"), rowst[:, 0:ncol], r=(B_rowst,), w=(B_KTa[kc],))
                        yield

                for _ in cumsum_rows(lfp, B_lfp, 32, list(range(8)), True):
                    pass

                def sample_cs():
                    for s in range(2):
                        P.dma("sp", lfc[:, 0:32, :], clf[s].rearrange("(b p) h -> p b h", p=128), w=(B_lfc,))
                        dv(lambda e: e.memset(lfc[:, 32, :], 0.0), w=(B_lfc,))
                        dv(lambda e, s=s: e.tensor_copy(out=lfc[0:32, 32, :], in_=lfs[32 * s:32 * s + 32, :]), r=(B_lfs,), w=(B_lfc,))
                        yield
                        yield from cumsum_rows(lfc, B_lfc, 33, [8 + 9 * s + j for j in range(9)], False)
                tick_on[0] = False
                bg_gen[0] = sample_cs()

            def qaug(q0, nq, diag_kc):
                P.dma("sp", qT[67:70, :, q0:q0 + nq], KT[diag_kc, 64:67, :, 0:nq], r=(B_KTa[diag_kc],), w=(B_qTaug,))

            def finalize_a(bo, ncol):
                ob = fin_state[0]; fin_state[0] = 1 - ob
                P.op("act", lambda e: e.copy(out=oc[ob][:, 0:ncol], in_=psf[bo][0:65, 0:ncol]), r=(B_psf[bo],), w=(B_oc[ob],))
                ps_free(bo)
                P.op("dve", lambda e: e.reciprocal(out=oc[ob][64:65, 0:ncol], in_=oc[ob][64:65, 0:ncol]), r=(B_oc[ob],), w=(B_oc[ob],))
                return ob

            def finalize_b(ob, ncol, dst_fn):
                bb = ps_get()
                P.mm([lambda e: e.matmul(psf[bb][0:64, 0:ncol], lhsT=ones_f[64:65, 0:64], rhs=oc[ob][64:65, 0:ncol], start=True, stop=True)],
                     r=(B_oc[ob], B_const), w=(B_psf[bb],))
                P.op("act", lambda e: e.copy(out=rinv[:, 0:ncol], in_=psf[bb][0:64, 0:ncol]), r=(B_psf[bb],), w=(B_rinv,))
                ps_free(bb)
                dst_fn(oc[ob], B_oc[ob], bb)

            fin_state = [0]
            pt_state = [0]
            LOOK = 3

            def attention(nq, chunks, diag_kc, fillers, tail):
                allc = chunks + [diag_kc]
                qaug(0, nq, diag_kc)
                nfill = (len(fillers) + NH - 3) // (NH - 2)
                fpos = 0
                pending_fin = [None]
                tail_gen = [None]

                def flush_fin():
                    if pending_fin[0] is not None:
                        ob_, h_ = pending_fin[0]
                        pending_fin[0] = None

                        def dst(ocb, Bocb, bb, h_=h_):
                            P.op("dve", lambda e: e.tensor_tensor(out=aoT[:, h_, 0:nq], in0=ocb[0:64, 0:nq], in1=rinv[:, 0:nq], op=ALU.mult),
                                 r=(Bocb, B_rinv), w=(B_aoT,))
                        finalize_b(ob_, nq, dst)
                for h in range(NH):
                    bo = ps_get()
                    blocks = [(j, kb) for j in range(len(allc)) for kb in range(4)]
                    nb = len(blocks)
                    cur = {}
                    pend = {}
                    for idx in range(nb + LOOK):
                        if idx == 8:
                            flush_fin()
                        if idx < nb:
                            j, kb = blocks[idx]
                            kc = allc[j]
                            isdiag = (j == len(allc) - 1)
                            if kb == 0:
                                def loader(t, b, kc=kc, h=h):
                                    P.dma("sp", t[:, 0:512], KT[kc, :, h, :], r=(B_KT[kc], B_KTa[kc], B_KTn[kc]), w=(b,))

                                def vloader(t, b, kc=kc, h=h):
                                    P.dma("sp", t[:, :, :], VA[kc, :, h, :, :], r=(B_VA[kc],), w=(b,))
                                cur["k"] = KS.get(loader)
                                cur["v"] = VS.get(vloader)
                            kt, Bkt = cur["k"]
                            vt, Bvt = cur["v"]
                            c0 = kb * 128 if isdiag else 0
                            n = nq - c0
                            bs = ps_get()
                            P.mm([lambda e, kb=kb, c0=c0, n=n, bs=bs, kt=kt: e.matmul(psf[bs][:, 0:n], lhsT=kt[:, kb * 128:(kb + 1) * 128],
                                                                                  rhs=qT[:, h, c0:c0 + n], start=True, stop=True)],
                                 r=(Bkt, B_qT), w=(B_psf[bs],))
                            p = pt_state[0]; pt_state[0] = (p + 1) % NPT
                            P.op("act", lambda e, p=p, n=n, bs=bs: e.activation(out=pT[p][:, 0:n], in_=psf[bs][:, 0:n], func=AF.Exp),
                                 r=(B_psf[bs],), w=(B_pT[p],))
                            ps_free(bs)
                            if isdiag:
                                P.op("pool", lambda e, p=p: e.tensor_tensor(out=pT[p][:, 0:128], in0=pT[p][:, 0:128], in1=tri_b[:, :], op=ALU.mult),
                                     r=(B_pT[p], B_const), w=(B_pT[p],))
                            pend[idx] = (p, kb, c0, n, vt, Bvt)
                            if tail_gen[0] is not None:
                                next(tail_gen[0], None)
                        if idx >= LOOK:
                            i2 = idx - LOOK
                            p, kb, c0, n, vt, Bvt = pend.pop(i2)
                            P.mm([lambda e, p=p, kb=kb, c0=c0, n=n, vt=vt, i2=i2: e.matmul(
                                psf[bo][0:65, c0:c0 + n], lhsT=vt[:, kb, :], rhs=pT[p][:, 0:n], start=(i2 == 0), stop=(i2 == nb - 1))],
                                 r=(B_pT[p], Bvt), w=(B_psf[bo],))

                    flush_fin()
                    pending_fin[0] = (finalize_a(bo, nq), h)
                    for f in fillers[fpos:fpos + nfill]:
                        f()
                    fpos += nfill
                    if h == NH - 2:
                        for f in fillers[fpos:]:
                            f()
                        fpos = len(fillers)
                        tail_gen[0] = tail()
                flush_fin()
                if tail_gen[0] is not None:
                    for _ in tail_gen[0]:
                        pass

            def attention_sample(s, fillers=None):
                q0 = 32 * s
                kcs = [8 + 9 * s + j for j in range(9)]
                qaug(q0, 32, kcs[8])
                bo = ps_get()
                P.mm([lambda e: e.matmul(psf[bo][0:65, 0:256], lhsT=zrow[0:1, 0:65], rhs=zrow[0:1, 0:256], start=True, stop=False)],
                     r=(B_const,), w=(B_psf[bo],))
                blocks = [(j, kb) for j in range(8) for kb in range(4)] + [(8, 0)]
                nb = len(blocks)
                cur = {}
                pend = {}
                for idx in range(nb + LOOK):
                    if idx < nb:
                        j, kb = blocks[idx]
                        kc = kcs[j]
                        isdiag = (j == 8)
                        kw = 32 if isdiag else 128
                        if kb == 0:
                            def loader(t, b, kc=kc, isdiag=isdiag):
                                if isdiag:
                                    P.dma("sp", t[0:70, :, 0:32], KT[kc, :, :, 0:32], r=(B_KT[kc], B_KTa[kc], B_KTn[kc]), w=(b,))
                                else:
                                    P.dma("sp", t[0:70, :, :], KT[kc], r=(B_KT[kc], B_KTa[kc], B_KTn[kc]), w=(b,))

                            def vloader(t, b, kc=kc, isdiag=isdiag):
                                if isdiag:
                                    P.dma("sp", t[0:32, :, 0:65], VA[kc, 0:32, :, 0, :], r=(B_VA[kc],), w=(b,))
                                else:
                                    P.dma("sp", t[:, :, 0:260], VA[kc].rearrange("p h b d -> p h (b d)"), r=(B_VA[kc],), w=(b,))
                            cur["k"] = WS.get(loader)
                            cur["v"] = WS.get(vloader)
                        kt, Bkt = cur["k"]
                        vt, Bvt = cur["v"]
                        bs = ps_get()
                        P.mm([lambda e, h=h, kb=kb, kw=kw, bs=bs, kt=kt: e.matmul(psf[bs][0:kw, h * 32:(h + 1) * 32], lhsT=kt[0:70, h, kb * 128:kb * 128 + kw],
                                                                             rhs=qT[:, h, q0:q0 + 32], start=True, stop=True) for h in range(NH)],
                             r=(Bkt, B_qT), w=(B_psf[bs],))
                        p = pt_state[0]; pt_state[0] = (p + 1) % NPT
                        P.op("act", lambda e, p=p, kw=kw, bs=bs: e.activation(out=pT[p][0:kw, 0:256], in_=psf[bs][0:kw, 0:256], func=AF.Exp),
                             r=(B_psf[bs],), w=(B_pT[p],))
                        ps_free(bs)
                        if isdiag:
                            P.op("dve", lambda e, p=p: e.tensor_tensor(out=pT[p][0:32, 0:256].rearrange("p (h q) -> p h q", h=NH),
                                                                        in0=pT[p][0:32, 0:256].rearrange("p (h q) -> p h q", h=NH),
                                                                        in1=tri_b[0:32, 0:32].unsqueeze(1).to_broadcast([32, NH, 32]), op=ALU.mult),
                                 r=(B_pT[p], B_const), w=(B_pT[p],))
                        pend[idx] = (p, kb, kw, vt, Bvt)
                        if fillers:
                            for _ in range(4):
                                if fillers:
                                    fillers.pop(0)()
                    if idx >= LOOK:
                        i2 = idx - LOOK
                        p, kb, kw, vt, Bvt = pend.pop(i2)
                        P.mm([lambda e, h=h, p=p, kb=kb, kw=kw, vt=vt, i2=i2: e.matmul(
                            psf[bo][0:65, h * 32:(h + 1) * 32], lhsT=vt[0:kw, h, kb * 65:(kb + 1) * 65], rhs=pT[p][0:kw, h * 32:(h + 1) * 32],
                            start=False, stop=(i2 == nb - 1 and h == NH - 1)) for h in range(NH)],
                             r=(B_pT[p], Bvt), w=(B_psf[bo],))

                def dst(ocb, Bocb, bb):
                    P.op("dve", lambda e: e.tensor_tensor(out=aoT[:, :, q0:q0 + 32], in0=ocb[0:64, 0:256].rearrange("p (h q) -> p h q", h=NH),
                                                          in1=rinv[:, 0:256].rearrange("p (h q) -> p h q", h=NH), op=ALU.mult),
                         r=(Bocb, B_rinv), w=(B_aoT,))
                finalize_b(finalize_a(bo, 256), 256, dst)

            def tile2_load(ti, is_sample, l):
                xtile = xt[ti % 2]; Bx = B_xt[ti % 2]
                if not is_sample:
                    for bi in range(4):
                        P.dma("sp", xtile[:, bi, :], xloc[l, bi * 128:(bi + 1) * 128, :], w=(Bx,))
                    P.dma("sp", xhalo[:, :], xloc[l - 1, 480:512, :], w=(B_xhalo,))
                else:
                    P.dma("sp", xtile[0:64, 0, :], xs[:, :], w=(Bx,))

            pre_stats = [False]
            pre_apply = [False]

            def norm1_blocks(ti, is_sample):
                xtile = xt[ti % 2]
                if not is_sample:
                    return [(xhalo[:, :], 32, 0)] + [(xtile[:, bi, :], 128, 32 + bi * 128) for bi in range(4)], (B_xhalo, B_xt[ti % 2])
                return [(xtile[0:64, 0, :], 64, 0)], (B_xt[ti % 2],)

            def tile2(ti, is_sample, l, own_i, nxt):
                NT = 64 if is_sample else 512
                xtile = xt[ti % 2]; Bx = B_xt[ti % 2]
                n1b, n1r = norm1_blocks(ti, is_sample)
                if not pre_stats[0]:
                    norm_stats(n1b, n1r, stat2, B_stat2)
                pre_stats[0] = False
                if not pre_apply[0]:
                    norm_apply(n1b, g1c, xnT, B_xnT, n1r, stat2, B_stat2)
                pre_apply[0] = False
                if not is_sample:
                    tblocks = [(128, bi * 128) for bi in range(4)]
                    xc0 = 32; glu_n = 544; glu_c0 = 0
                    segs = [(32, 512)]
                    cu_dst0 = 0
                else:
                    tblocks = [(64, 0)]
                    xc0 = 0; glu_n = 64; glu_c0 = 0
                    segs = [(32, 32), (96, 32)]
                    P.op("dve", lambda e: e.memset(cuT[:, :, 0:128], 0.0), w=(B_cuT,))
                    for s in range(2):
                        P.dma("sp", schist[0:30, :], sconv[s], w=(B_schist,))
                        for cc in range(4):
                            bk = ps_get()
                            P.mm([lambda e, cc=cc, bk=bk: e.matmul(psf[bk][:, 0:30], lhsT=schist[0:30, cc * 128:(cc + 1) * 128], rhs=ident_f[0:30, 0:30],
                                                                  start=True, stop=True)], r=(B_schist, B_const), w=(B_psf[bk],))
                            P.op("act", lambda e, cc=cc, bk=bk, s=s: e.copy(out=cuT[:, cc, 64 * s + 2:64 * s + 32], in_=psf[bk][:, 0:30]), r=(B_psf[bk],), w=(B_cuT,))
                            ps_free(bk)

                wa, Bwa = wload(w_in, 0, 8, 0, 512)
                wg, Bwg = wload(w_in, 0, 8, 512, 512)
                for cc in range(4):
                    col = 0
                    while col < glu_n:
                        n = min(512, glu_n - col)
                        ba = ps_get(); bg = ps_get()
                        mm_fm(wg, Bwg, 8, cc * 128, 128, xnT, B_xnT, glu_c0 + col, n, bg)
                        mm_fm(wa, Bwa, 8, cc * 128, 128, xnT, B_xnT, glu_c0 + col, n, ba)
                        P.op("act", lambda e, bg=bg, n=n: e.activation(out=tg[:, 0:n], in_=psf[bg][:, 0:n], func=AF.Tanh, scale=0.5), r=(B_psf[bg],), w=(B_tg,))
                        ps_free(bg)
                        if not is_sample:
                            dsts = [(cuT[:, cc, col:col + n], 0, n)]
                        else:
                            dsts = [(cuT[:, cc, 32:64], 0, 32), (cuT[:, cc, 96:128], 32, 32)]
                        for dst, s0, sn in dsts:
                            P.op("dve", lambda e, dst=dst, s0=s0, sn=sn, ba=ba: e.scalar_tensor_tensor(out=dst, in0=tg[:, s0:s0 + sn], scalar=1.0, in1=psf[ba][:, s0:s0 + sn],
                                                                                                  op0=ALU.add, op1=ALU.mult), r=(B_tg, B_psf[ba]), w=(B_cuT,))
                        ps_free(ba)
                        col += n
                if not is_sample:
                    P.op("act", lambda e: e.activation(out=cuT[:, :, 0:544], in_=cuT[:, :, 0:544], func=AF.Copy, scale=0.5), r=(B_cuT,), w=(B_cuT,))
                else:
                    for s in range(2):
                        P.op("act", lambda e, s=s: e.activation(out=cuT[:, :, 64 * s + 32:64 * s + 64], in_=cuT[:, :, 64 * s + 32:64 * s + 64], func=AF.Copy, scale=0.5),
                             r=(B_cuT,), w=(B_cuT,))
                def hist_out(colbase, dst):
                    for cc in range(4):
                        bk = ps_get()
                        P.mm([lambda e, cc=cc, bk=bk: e.matmul(psf[bk][0:30, 0:128], lhsT=cuT[:, cc, colbase:colbase + 30], rhs=ident_f[:, :], start=True, stop=True)],
                             r=(B_cuT, B_const), w=(B_psf[bk],))
                        P.op("act", lambda e, cc=cc, bk=bk: e.copy(out=chst[0:30, cc * 128:(cc + 1) * 128], in_=psf[bk][0:30, 0:128]), r=(B_psf[bk],), w=(B_chst,))
                        ps_free(bk)
                    P.dma("sp", dst, chst[0:30, :], r=(B_chst,), w=())
                if is_sample:
                    for s in range(2):
                        hist_out(64 * s + 34, convs_o[s])
                elif own_i == 3:
                    hist_out(514, conv_own[:, :])

                wq, Bwq = wload(w_in, 0, 8, 1024, 512)
                for g in range(4):
                    bk = ps_get()
                    mm_fm(wq, Bwq, 8, g * 128, 128, xnT, B_xnT, xc0, NT, bk)
                    P.op("act", lambda e, g=g, bk=bk: e.activation(out=qT[0:64, 2 * g, 0:NT], in_=psf[bk][0:64, 0:NT], func=AF.Copy, scale=0.125), r=(B_psf[bk],), w=(B_qT,))
                    P.op("dve", lambda e, g=g, bk=bk: e.tensor_scalar(out=qT[0:64, 2 * g + 1, 0:NT], in0=psf[bk][64:128, 0:NT], scalar1=0.125, scalar2=None, op0=ALU.mult),
                         r=(B_psf[bk],), w=(B_qT,))
                    ps_free(bk)
                P.op("dve", lambda e: e.memset(qT[64:67, :, 0:NT], 1.0), w=(B_qT,))

                fillers = []
                for cc in range(4):
                    for (o0, n) in segs:
                        d0 = o0 - 32 if not is_sample else (0 if o0 == 32 else 32)
                        for j in range(CW):
                            src = cuT[:, cc, o0 - 30 + j:o0 - 30 + j + n]
                            if j == 0:
                                fillers.append(lambda src=src, cc=cc, d0=d0, n=n: P.op(
                                    "dve", lambda e: e.tensor_scalar(out=cacc[:, cc, d0:d0 + n], in0=src, scalar1=wdw[:, cc, 0:1],
                                                                     scalar2=bdw[:, cc:cc + 1], op0=ALU.mult, op1=ALU.add),
                                    r=(B_cuT, B_const), w=(B_cacc,)))
                            else:
                                fillers.append(lambda src=src, cc=cc, d0=d0, n=n, j=j: P.op(
                                    "dve", lambda e: e.scalar_tensor_tensor(out=cacc[:, cc, d0:d0 + n], in0=src, scalar=wdw[:, cc, j:j + 1],
                                                                            in1=cacc[:, cc, d0:d0 + n], op0=ALU.mult, op1=ALU.add),
                                    r=(B_cuT, B_const, B_cacc), w=(B_cacc,)))
                def ln_block():
                    b1 = ps_get(); b2 = ps_get()
                    P.mm([lambda e, cc=cc: e.matmul(psf[b1][:, 0:NT], lhsT=ones_f[:, :], rhs=cacc[:, cc, 0:NT], start=(cc == 0), stop=(cc == 3)) for cc in range(4)],
                         r=(B_cacc, B_const), w=(B_psf[b1],))
                    yield
                    for cc in range(4):
                        P.op("act", lambda e, cc=cc: e.activation(out=csq[:, 0:NT], in_=cacc[:, cc, 0:NT], func=AF.Square), r=(B_cacc,), w=(B_csq,))
                        yield
                        P.mm([lambda e, cc=cc: e.matmul(psf[b2][:, 0:NT], lhsT=ones_f[:, :], rhs=csq[:, 0:NT], start=(cc == 0), stop=(cc == 3))],
                             r=(B_csq, B_const), w=(B_psf[b2],))
                        yield
                    P.op("act", lambda e: e.activation(out=lnm[:, 0:NT], in_=psf[b1][:, 0:NT], func=AF.Copy, scale=1.0 / 512), r=(B_psf[b1],), w=(B_lnm,))
                    yield
                    ps_free(b1)
                    P.op("dve", lambda e: e.tensor_tensor(out=lnr[:, 0:NT], in0=lnm[:, 0:NT], in1=lnm[:, 0:NT], op=ALU.mult), r=(B_lnm,), w=(B_lnr,))
                    yield
                    P.op("dve", lambda e: e.scalar_tensor_tensor(out=lnr[:, 0:NT], in0=psf[b2][:, 0:NT], scalar=1.0 / 512, in1=lnr[:, 0:NT], op0=ALU.mult, op1=ALU.subtract),
                         r=(B_psf[b2], B_lnr), w=(B_lnr,))
                    yield
                    ps_free(b2)
                    P.op("dve", lambda e: e.tensor_scalar(out=lnr[:, 0:NT], in0=lnr[:, 0:NT], scalar1=1e-5, scalar2=None, op0=ALU.add), r=(B_lnr,), w=(B_lnr,))
                    yield
                    P.op("act", lambda e: e.activation(out=lnr[:, 0:NT], in_=lnr[:, 0:NT], func=AF.Sqrt), r=(B_lnr,), w=(B_lnr,))
                    yield
                    P.op("dve", lambda e: e.reciprocal(out=lnr[:, 0:NT], in_=lnr[:, 0:NT]), r=(B_lnr,), w=(B_lnr,))
                    yield
                    for cc in range(4):
                        P.op("dve", lambda e, cc=cc: e.tensor_tensor(out=cacc[:, cc, 0:NT], in0=cacc[:, cc, 0:NT], in1=lnm[:, 0:NT], op=ALU.subtract),
                             r=(B_cacc, B_lnm), w=(B_cacc,))
                        yield
                        P.op("dve", lambda e, cc=cc: e.tensor_tensor(out=cacc[:, cc, 0:NT], in0=cacc[:, cc, 0:NT], in1=lnr[:, 0:NT], op=ALU.mult),
                             r=(B_cacc, B_lnr), w=(B_cacc,))
                        yield
                        P.op("act", lambda e, cc=cc: e.activation(out=cacc[:, cc, 0:NT], in_=cacc[:, cc, 0:NT], func=AF.Identity, scale=lngh[:, cc:cc + 1], bias=lnbh[:, cc:cc + 1]),
                             r=(B_cacc, B_const), w=(B_cacc,))
                        yield
                        P.op("act", lambda e, cc=cc: e.activation(out=csq[:, 0:NT], in_=cacc[:, cc, 0:NT], func=AF.Tanh), r=(B_cacc,), w=(B_csq,))
                        yield
                        P.op("dve", lambda e, cc=cc: e.scalar_tensor_tensor(out=sT[:, cc, 0:NT], in0=csq[:, 0:NT], scalar=1.0, in1=cacc[:, cc, 0:NT], op0=ALU.add, op1=ALU.mult),
                             r=(B_csq, B_cacc), w=(B_sT,))
                        yield


                yield fillers
                if not is_sample:
                    attention(512, list(range(l)), l, fillers, ln_block)
                else:
                    for s in range(2):
                        attention_sample(s, fillers)
                    while fillers:
                        fillers.pop(0)()
                    for _ in ln_block():
                        pass


                def wao_load(c0):
                    src = w_ao[:, c0:c0 + 512].rearrange("(h p) n -> p h n", p=64)
                    return wload_generic(("w_ao", c0), [64, 8, 512], src, 64)
                wpw = {}; wgc = {}; wao = {}; wga = {}
                for half in range(2):
                    wgc[half] = wload(w_in, 0, 8, 2568 + half * 512, 512)
                    wpw[half] = wload(w_pw, 0, 4, half * 512, 512)
                    wga[half] = wload(w_in, 0, 8, 3592 + half * 512, 512)
                    wao[half] = wao_load(half * 512)
                    for f4 in range(4):
                        fc = half * 4 + f4
                        bg = ps_get()
                        mm_fm(wgc[half][0], wgc[half][1], 8, f4 * 128, 128, xnT, B_xnT, xc0, NT, bg)
                        P.op("act", lambda e, bg=bg: e.activation(out=thc[:, 0:NT], in_=psf[bg][:, 0:NT], func=AF.Tanh, scale=0.5), r=(B_psf[bg],), w=(B_thc,))
                        ps_free(bg)
                        by = ps_get()
                        mm_fm(wpw[half][0], wpw[half][1], 4, f4 * 128, 128, sT, B_sT, 0, NT, by)
                        P.op("dve", lambda e, by=by: e.scalar_tensor_tensor(out=m1[:, 0:NT], in0=thc[:, 0:NT], scalar=1.0, in1=psf[by][:, 0:NT], op0=ALU.add, op1=ALU.mult),
                             r=(B_thc, B_psf[by]), w=(B_m1,))
                        ps_free(by)
                        bg = ps_get()
                        mm_fm(wga[half][0], wga[half][1], 8, f4 * 128, 128, xnT, B_xnT, xc0, NT, bg)
                        P.op("act", lambda e, bg=bg: e.activation(out=thc[:, 0:NT], in_=psf[bg][:, 0:NT], func=AF.Tanh, scale=0.5), r=(B_psf[bg],), w=(B_thc,))
                        ps_free(bg)
                        by = ps_get()
                        wt, Bwt = wao[half]
                        P.mm([lambda e, h=h, by=by, wt=wt, f4=f4: e.matmul(psf[by][:, 0:NT], lhsT=wt[0:64, h, f4 * 128:(f4 + 1) * 128], rhs=aoT[:, h, 0:NT],
                                                                           start=(h == 0), stop=(h == NH - 1)) for h in range(NH)], r=(Bwt, B_aoT), w=(B_psf[by],))
                        P.op("dve", lambda e, by=by: e.scalar_tensor_tensor(out=thc[:, 0:NT], in0=thc[:, 0:NT], scalar=1.0, in1=psf[by][:, 0:NT], op0=ALU.add, op1=ALU.mult),
                             r=(B_thc, B_psf[by]), w=(B_thc,))
                        ps_free(by)
                        P.op("dve", lambda e, fc=fc: e.tensor_tensor(out=mixT[:, fc, 0:NT], in0=m1[:, 0:NT], in1=thc[:, 0:NT], op=ALU.add), r=(B_m1, B_thc), w=(B_mixT,))

                if nxt is not None:
                    tile2_load(*nxt)
                for half in range(2):
                    wo_t, wo_b = wload(w_out, 0, 8, half * 512, 512)
                    for bi, (npp, c0) in enumerate(tblocks):
                        bk = ps_get()
                        mm_tm(mixT, B_mixT, 8, c0, npp, wo_t, wo_b, 0, 512, bk)
                        P.op("dve", lambda e, bk=bk, bi=bi, npp=npp, half=half: e.scalar_tensor_tensor(out=xtile[0:npp, bi, half * 512:(half + 1) * 512], in0=psf[bk][0:npp, :], scalar=0.5,
                                                                                                  in1=xtile[0:npp, bi, half * 512:(half + 1) * 512], op0=ALU.mult, op1=ALU.add),
                             r=(B_psf[bk], Bx), w=(Bx,))
                        ps_free(bk)
                norm_T([(xtile[0:npp, bi, :], npp, c0) for bi, (npp, c0) in enumerate(tblocks)], g2c, zT, B_zT, extra_r=(Bx,))
                for c5 in range(6):
                    ncols = 512 if c5 < 5 else 256
                    wgt, Bwgt = wload(w_gate, 0, 8, c5 * 512, ncols)
                    wut, Bwut = wload(w_up, 0, 8, c5 * 512, ncols)
                    for f4 in range(ncols // 128):
                        fc = c5 * 4 + f4
                        bg = ps_get(); bu = ps_get()
                        mm_fm(wgt, Bwgt, 8, f4 * 128, 128, zT, B_zT, 0, NT, bg)
                        mm_fm(wut, Bwut, 8, f4 * 128, 128, zT, B_zT, 0, NT, bu)
                        P.op("act", lambda e, bg=bg: e.activation(out=thc[:, 0:NT], in_=psf[bg][:, 0:NT], func=AF.Tanh, scale=0.5), r=(B_psf[bg],), w=(B_thc,))
                        P.op("dve", lambda e, bg=bg: e.scalar_tensor_tensor(out=m1[:, 0:NT], in0=thc[:, 0:NT], scalar=1.0, in1=psf[bg][:, 0:NT], op0=ALU.add, op1=ALU.mult),
                             r=(B_thc, B_psf[bg]), w=(B_m1,))
                        ps_free(bg)
                        P.op("dve", lambda e, bu=bu, fc=fc: e.scalar_tensor_tensor(out=actT[:, fc, 0:NT], in0=m1[:, 0:NT], scalar=0.5, in1=psf[bu][:, 0:NT], op0=ALU.mult, op1=ALU.mult),
                             r=(B_m1, B_psf[bu]), w=(B_actT,))
                        ps_free(bu)
                        bg_step(3)
                if nxt is not None:
                    nb_, nr_ = norm1_blocks(nxt[0], nxt[1])
                    norm_stats(nb_, nr_, stat2, B_stat2)
                    pre_stats[0] = True
                kgroups = [(0, 8), (8, 8), (16, 6)]
                for half in range(2):
                    banks = [ps_get() for _ in tblocks]
                    for gi, (k0, nk) in enumerate(kgroups):
                        wd, Bwd = wload(w_down, k0, nk, half * 512, 512)
                        for bi, (npp, c0) in enumerate(tblocks):
                            P.mm([lambda e, kc=kc, bi=bi, npp=npp, c0=c0, k0=k0, nk=nk, gi=gi, wd=wd: e.matmul(
                                psf[banks[bi]][0:npp, :], lhsT=actT[:, k0 + kc, c0:c0 + npp], rhs=wd[:, kc, 0:512],
                                start=(gi == 0 and kc == 0), stop=(gi == 2 and kc == nk - 1)) for kc in range(nk)],
                                 r=(Bwd, B_actT), w=(B_psf[banks[bi]],))
                    for bi, (npp, c0) in enumerate(tblocks):
                        bk = banks[bi]
                        P.op("dve", lambda e, bk=bk, bi=bi, npp=npp, half=half: e.tensor_tensor(out=xtile[0:npp, bi, half * 512:(half + 1) * 512], in0=psf[bk][0:npp, :],
                                                                                               in1=xtile[0:npp, bi, half * 512:(half + 1) * 512], op=ALU.add),
                             r=(B_psf[bk], Bx), w=(Bx,))
                        ps_free(bk)
                if nxt is not None:
                    nb_, nr_ = norm1_blocks(nxt[0], nxt[1])
                    norm_apply(nb_, g1c, xnT, B_xnT, nr_, stat2, B_stat2)
                    pre_apply[0] = True
                P.op("dve", lambda e: e.memset(stat[:, 0:8], 0.0), w=(B_stat,))
                for bi, (npp, c0) in enumerate(tblocks):
                    P.op("act", lambda e, bi=bi, npp=npp: e.activation(out=junk[0:npp, :], in_=xtile[0:npp, bi, :], func=AF.Square, accum_out=stat[0:npp, bi:bi + 1]),
                         r=(Bx,), w=(B_junk, B_stat))
                rstd_from_ss(len(tblocks), D, 1e-6)
                for bi, (npp, c0) in enumerate(tblocks):
                    for hf, (yb, B_yb) in enumerate(((lnr, B_lnr), (kvout, B_kvout))):
                        P.op("dve", lambda e, bi=bi, npp=npp, hf=hf, yb=yb: e.scalar_tensor_tensor(
                            out=yb[0:npp, :], in0=xtile[0:npp, bi, hf * 512:(hf + 1) * 512], scalar=stat[0:npp, 16 + bi:17 + bi],
                            in1=gfin[0:npp, hf * 512:(hf + 1) * 512], op0=ALU.mult, op1=ALU.mult), r=(Bx, B_stat, B_const), w=(B_yb,))
                        if is_sample:
                            P.dma("sp", ys_o[:, hf * 512:(hf + 1) * 512], yb[0:64, :], r=(B_yb,), w=())
                        else:
                            P.dma("sp", y_own[own_i, bi * 128:(bi + 1) * 128, hf * 512:(hf + 1) * 512], yb[:, :], r=(B_yb,), w=())

            tile2_load(0, False, 1)
            g0 = tile2(0, False, 1, 0, (1, False, 3))
            fl0 = next(g0)

            def tick():
                for _ in range(2):
                    if fl0:
                        fl0.pop(0)()
            logf_section(tick)
            for _ in g0:
                pass
            for i in range(1, 4):
                nxt = (i + 1, False, 2 * i + 3) if i < 3 else (4, True, None)
                for _ in tile2(i, False, 2 * i + 1, i, nxt):
                    pass
            bg_step(100000)
            for _ in tile2(4, True, None, None, None):
                pass
            if not P.dry:
                P.finish()

        P.dry = True
        program()
        P.reset()
        psstate.update({"free": [True] * 6, "nxt": 0, "tn": 0})
        WS.start_real(); KS.start_real(); VS.start_real(); wl["n"] = 0
        program()
    return nc


_NC_CACHE = {}


def kernel(x_prompt, x_sample, cache_k, cache_v, cache_logf, state_conv, norm_mix_g, w_in, b_f, w_dw, b_dw, ln_g, ln_b,
           w_conv_pw, w_attn_o, w_out, norm_ffn_g, w_gate, w_up, w_down, final_norm_g):
    f = lambda a: np.ascontiguousarray(np.asarray(a, dtype=np.float32))
    x_prompt = f(x_prompt); x_sample = f(x_sample)
    cache_k = f(cache_k)[0].reshape(16, 4096, 512); cache_v = f(cache_v)[0].reshape(16, 4096, 512)
    cache_logf = f(cache_logf)[0]; state_conv = f(state_conv)[0]
    col = lambda v, n: np.ascontiguousarray(f(v).reshape(n, 128).T)
    shared = {
        "w_in": f(w_in)[0], "w_pw": f(w_conv_pw)[0], "w_ao": f(w_attn_o)[0], "w_out": f(w_out)[0],
        "w_gate": f(w_gate)[0], "w_up": f(w_up)[0], "w_down": f(w_down)[0],
        "g1c": col(norm_mix_g, 8), "g2c": col(norm_ffn_g, 8),
        "gfin": np.ascontiguousarray(np.broadcast_to(f(final_norm_g).reshape(1, D), (128, D))),
        "bfb": np.ascontiguousarray(np.broadcast_to(f(b_f).reshape(1, NH), (128, NH))),
        "wdw": np.ascontiguousarray(f(w_dw)[0].reshape(CW, 4, 128).transpose(2, 1, 0)),
        "bdw": col(b_dw, 4), "lng": col(ln_g, 4), "lnb": col(ln_b, 4),
        "ident": np.eye(128, dtype=np.float32), "tri": np.triu(np.ones((128, 128), np.float32)),
    }
    in_maps = []
    for c in range(8):
        b, p = c // 2, c % 2
        xc = x_prompt[b].reshape(8, 512, D)
        if p == 1:
            xl = xc
            km = np.ones((128, NLC), np.float32)
        else:
            xl = np.concatenate([np.zeros((1, 512, D), np.float32), xc[:7]], axis=0)
            km = np.ones((128, NLC), np.float32); km[:, 0] = 0.0
        m = dict(shared)
        m.update({"xloc": np.ascontiguousarray(xl), "xs": np.ascontiguousarray(x_sample[2 * c:2 * c + 2].reshape(64, D)),
                  "ck": np.ascontiguousarray(cache_k[2 * c:2 * c + 2]), "cv": np.ascontiguousarray(cache_v[2 * c:2 * c + 2]),
                  "clf": np.ascontiguousarray(cache_logf[2 * c:2 * c + 2]), "sconv": np.ascontiguousarray(state_conv[2 * c:2 * c + 2]),
                  "kmask": km})
        in_maps.append(m)
    if "nc" not in _NC_CACHE:
        _NC_CACHE["nc"] = build_nc()
    res = run_bass_kernel_spmd(_NC_CACHE["nc"], in_maps, core_ids=list(range(8)))
    R = res.results
    y_p = np.zeros((4, 4096, D), np.float32); k_p = np.zeros((1, 4, 4096, NH, HD), np.float32); v_p = np.zeros_like(k_p)
    lf_p = np.zeros((1, 4, 4096, NH), np.float32); cv_p = np.zeros((1, 4, 30, 512), np.float32)
    y_s = np.zeros((16, 32, D), np.float32); k_s = np.zeros((1, 16, 32, NH, HD), np.float32); v_s = np.zeros_like(k_s)
    lf_s = np.zeros((1, 16, 32, NH), np.float32); cv_s = np.zeros((1, 16, 30, 512), np.float32)
    for c in range(8):
        b, p = c // 2, c % 2
        r = R[c]
        for i in range(4):
            gch = 2 * i + 1 - (1 - p)
            sl = slice(gch * 512, (gch + 1) * 512)
            y_p[b, sl] = r["y_own"][i]
            k_p[0, b, sl] = r["k_own"][i].reshape(512, NH, HD)
            v_p[0, b, sl] = r["v_own"][i].reshape(512, NH, HD)
            lf_p[0, b, sl] = r["lf_own"][i]
        if p == 1:
            cv_p[0, b] = r["conv_own"]
        y_s[2 * c:2 * c + 2] = r["ys"].reshape(2, 32, D)
        k_s[0, 2 * c:2 * c + 2] = r["ks"].reshape(2, 32, NH, HD)
        v_s[0, 2 * c:2 * c + 2] = r["vs"].reshape(2, 32, NH, HD)
        lf_s[0, 2 * c:2 * c + 2] = r["lfs"].reshape(2, 32, NH)
        cv_s[0, 2 * c:2 * c + 2] = r["convs"]
    return (y_p, y_s, k_p, v_p, lf_p, cv_p, k_s, v_s, lf_s, cv_s)
```
